# Optimizing a Trainium2 kernel written in Bass

```python
import math
import jax, jax.numpy as jnp
from jax import lax
import numpy as np

D_MODEL = 1024
BATCH = 2
SEQ = 8192
DEPTH = 1

GRID_W = 64
PLE_DIM = 256
ATT_HEADS = 8
ATT_KV_HEADS = 2
HEAD_DIM = 128
Q_BLOCK = 128
ROPE_THETA = 10000.0
DN_HEADS = 8
DN_DK = 128
DN_DV = 128
CONV_K = 5
CHUNK = 64
EPS = 1e-6

ATT_W = ATT_HEADS * HEAD_DIM
KV_W = ATT_KV_HEADS * HEAD_DIM
DN_KW = DN_HEADS * DN_DK
DN_VW = DN_HEADS * DN_DV
IN_SIZES = (ATT_W, KV_W, KV_W, ATT_W,
            DN_KW, DN_KW, DN_VW, 2 * DN_HEADS, 2 * DN_HEADS, DN_VW,
            D_MODEL, D_MODEL)
IN_W = ATT_W + 2 * KV_W + ATT_W + 2 * DN_KW + DN_VW + 4 * DN_HEADS + DN_VW + 2 * D_MODEL

kernel_name = 'hybrid_gqa_axialrope_gated_deltanet_bidir_block'


def rms_norm(x, w):
    xf = x.astype(jnp.float32)
    y = xf * lax.rsqrt(jnp.mean(xf * xf, axis=-1, keepdims=True) + EPS)
    return (y * w.astype(jnp.float32)).astype(x.dtype)


def l2_norm(x):
    xf = x.astype(jnp.float32)
    return xf * lax.rsqrt(jnp.sum(xf * xf, axis=-1, keepdims=True) + EPS)


def split_cols(t):
    outs, start = [], 0
    for size in IN_SIZES:
        outs.append(t[..., start:start + size])
        start += size
    return outs


def axial_rope_angles(seq_len):
    rows = seq_len // GRID_W
    row = jnp.broadcast_to(jnp.arange(rows)[:, None], (rows, GRID_W)).reshape(seq_len)
    col = jnp.broadcast_to(jnp.arange(GRID_W)[None, :], (rows, GRID_W)).reshape(seq_len)
    n_freq = HEAD_DIM // 4
    inv_freq = ROPE_THETA ** (-jnp.arange(n_freq, dtype=jnp.float32) / n_freq)
    ang = jnp.concatenate([row.astype(jnp.float32)[:, None] * inv_freq,
                           col.astype(jnp.float32)[:, None] * inv_freq], axis=-1)
    return jnp.cos(ang), jnp.sin(ang)


def apply_rope(x, cos, sin):
    xf = x.astype(jnp.float32).reshape(*x.shape[:-1], HEAD_DIM // 2, 2)
    x0, x1 = xf[..., 0], xf[..., 1]
    out = jnp.stack([x0 * cos - x1 * sin, x0 * sin + x1 * cos], axis=-1)
    return out.reshape(x.shape).astype(x.dtype)


def gqa_attention(q, k, v):
    B, Hq, S, D = q.shape
    G = Hq // ATT_KV_HEADS
    nb = S // Q_BLOCK
    qb = q.reshape(B, ATT_KV_HEADS, G, nb, Q_BLOCK, D).transpose(3, 0, 1, 2, 4, 5)
    scale = D ** -0.5

    def block(qi):
        s = jnp.einsum('bkgqd,bksd->bkgqs', qi, k).astype(jnp.float32) * scale
        pr = jax.nn.softmax(s, axis=-1).astype(v.dtype)
        return jnp.einsum('bkgqs,bksd->bkgqd', pr, v)

    o = lax.map(block, qb)
    return o.transpose(1, 2, 3, 0, 4, 5).reshape(B, Hq, S, D)


def short_conv(x, w):
    C = x.shape[-1]
    return lax.conv_general_dilated(
        x, w[:, None, :].astype(x.dtype), window_strides=(1,),
        padding=((CONV_K // 2, CONV_K // 2),),
        dimension_numbers=('NWC', 'WIO', 'NWC'), feature_group_count=C)


def chunk_gated_delta_rule(q, k, v, g, beta):
    B, H, S, DK = q.shape
    DV = v.shape[-1]
    n = S // CHUNK
    q = q.reshape(B, H, n, CHUNK, DK)
    k = k.reshape(B, H, n, CHUNK, DK)
    v = v.reshape(B, H, n, CHUNK, DV)
    beta = beta.reshape(B, H, n, CHUNK)
    g = jnp.cumsum(g.reshape(B, H, n, CHUNK), axis=-1)
    tril = jnp.tril(jnp.ones((CHUNK, CHUNK), dtype=bool))
    eye = jnp.eye(CHUNK, dtype=jnp.float32)
    decay = jnp.exp(jnp.where(tril, g[..., :, None] - g[..., None, :], -jnp.inf))
    k_beta = k * beta[..., None]
    v_beta = v * beta[..., None]
    L = jnp.tril(jnp.einsum('bhncd,bhnjd->bhncj', k_beta, k) * decay, -1)
    T = lax.linalg.triangular_solve(eye + L, jnp.broadcast_to(eye, L.shape),
                                    left_side=True, lower=True, unit_diagonal=True)
    u = jnp.einsum('bhncj,bhnjv->bhncv', T, v_beta)
    w = jnp.einsum('bhncj,bhnjd->bhncd', T, k_beta * jnp.exp(g)[..., None])
    a_intra = jnp.einsum('bhncd,bhnjd->bhncj', q, k) * decay

    def step(state, xs):
        q_c, k_c, u_c, w_c, g_c, a_c = xs
        g_last = g_c[..., -1]
        v_new = u_c - jnp.einsum('bhcd,bhdv->bhcv', w_c, state)
        o = (jnp.einsum('bhcd,bhdv->bhcv', q_c * jnp.exp(g_c)[..., None], state)
             + jnp.einsum('bhcj,bhjv->bhcv', a_c, v_new))
        k_dec = k_c * jnp.exp(g_last[..., None] - g_c)[..., None]
        state = state * jnp.exp(g_last)[..., None, None] + jnp.einsum('bhcd,bhcv->bhdv', k_dec, v_new)
        return state, o

    front = lambda t: jnp.moveaxis(t, 2, 0)
    xs = (front(q), front(k), front(u), front(w), front(g), front(a_intra))
    state0 = jnp.zeros((B, H, DK, DV), jnp.float32)
    _, o = lax.scan(step, state0, xs)
    return jnp.moveaxis(o, 0, 2).reshape(B, H, S, DV)


def bidir_gated_deltanet(q, k, v, g, beta):
    flip = lambda t: jnp.flip(t, axis=2)
    o_f = chunk_gated_delta_rule(q, k, v, g[0], beta[0])
    o_b = flip(chunk_gated_delta_rule(flip(q), flip(k), flip(v), flip(g[1]), flip(beta[1])))
    return o_f + o_b


def setup_inputs(seed: int = 0) -> dict:
    key = jax.random.key(seed)
    ks = jax.random.split(key, 20)
    nrm = lambda k, shape, scale: jax.random.normal(k, shape, jnp.float32) * scale
    gain = lambda k, shape: 1.0 + 0.02 * jax.random.normal(k, shape, jnp.float32)
    dt = jnp.exp(jax.random.uniform(ks[8], (DEPTH, 2, DN_HEADS), jnp.float32,
                                    math.log(1e-3), math.log(1e-1)))
    return {
        'x': nrm(ks[0], (BATCH, SEQ, D_MODEL), 1.0),
        'p': nrm(ks[1], (DEPTH, BATCH, SEQ, PLE_DIM), 1.0),
        'norm_pre': gain(ks[2], (DEPTH, D_MODEL)),
        'w_in': nrm(ks[3], (DEPTH, D_MODEL, IN_W), D_MODEL ** -0.5),
        'q_norm': gain(ks[4], (DEPTH, HEAD_DIM)),
        'k_norm': gain(ks[5], (DEPTH, HEAD_DIM)),
        'conv_w': nrm(ks[6], (DEPTH, CONV_K, 2 * DN_KW + DN_VW), CONV_K ** -0.5),
        'a_log': jnp.log(jax.random.uniform(ks[7], (DEPTH, 2, DN_HEADS), jnp.float32, 1.0, 16.0)),
        'dt_bias': dt + jnp.log(-jnp.expm1(-dt)),
        'dn_norm': gain(ks[9], (DEPTH, DN_DV)),
        'w_br_att': nrm(ks[10], (DEPTH, ATT_W, D_MODEL), ATT_W ** -0.5),
        'w_br_dn': nrm(ks[11], (DEPTH, DN_VW, D_MODEL), DN_VW ** -0.5),
        'w_out': nrm(ks[12], (DEPTH, D_MODEL, D_MODEL), D_MODEL ** -0.5),
        'norm_post': gain(ks[13], (DEPTH, D_MODEL)),
        'w_ple_proj': nrm(ks[14], (DEPTH, PLE_DIM, D_MODEL), PLE_DIM ** -0.5),
        'w_ple_gate': nrm(ks[15], (DEPTH, D_MODEL, D_MODEL), D_MODEL ** -0.5),
        'ple_norm': gain(ks[16], (DEPTH, D_MODEL)),
    }


def reference(x, p, norm_pre, w_in, q_norm, k_norm, conv_w, a_log, dt_bias, dn_norm,
              w_br_att, w_br_dn, w_out, norm_post, w_ple_proj, w_ple_gate, ple_norm):
    B, S, _ = x.shape
    cos, sin = axial_rope_angles(S)
    heads = lambda t, nh, hd: t.reshape(B, S, nh, hd).transpose(0, 2, 1, 3)
    for i in range(DEPTH):
        h = rms_norm(x, norm_pre[i])
        (aq, ak, av, az, dq, dk, dv, db, da, dz, gate_att, gate_dn) = split_cols(h @ w_in[i])

        q = apply_rope(rms_norm(heads(aq, ATT_HEADS, HEAD_DIM), q_norm[i]), cos, sin)
        k = apply_rope(rms_norm(heads(ak, ATT_KV_HEADS, HEAD_DIM), k_norm[i]), cos, sin)
        v = heads(av, ATT_KV_HEADS, HEAD_DIM)
        o_att = gqa_attention(q, k, v).transpose(0, 2, 1, 3).reshape(B, S, ATT_W)
        y_att = (o_att * jax.nn.silu(az)) @ w_br_att[i]

        qkv = jax.nn.silu(short_conv(jnp.concatenate([dq, dk, dv], axis=-1), conv_w[i]))
        cq, ck, cv = qkv[..., :DN_KW], qkv[..., DN_KW:2 * DN_KW], qkv[..., 2 * DN_KW:]
        qd = l2_norm(heads(cq, DN_HEADS, DN_DK)) * (DN_DK ** -0.5)
        kd = l2_norm(heads(ck, DN_HEADS, DN_DK))
        vd = heads(cv, DN_HEADS, DN_DV).astype(jnp.float32)
        da4 = da.astype(jnp.float32).reshape(B, S, 2, DN_HEADS)
        db4 = db.astype(jnp.float32).reshape(B, S, 2, DN_HEADS)
        g = -jnp.exp(a_log[i].astype(jnp.float32)) * jax.nn.softplus(da4 + dt_bias[i].astype(jnp.float32))
        beta = jax.nn.sigmoid(db4)
        g = g.transpose(2, 0, 3, 1)
        beta = beta.transpose(2, 0, 3, 1)
        o_dn = bidir_gated_deltanet(qd, kd, vd, g, beta).astype(x.dtype)
        o_dn = rms_norm(o_dn.transpose(0, 2, 1, 3), dn_norm[i]).reshape(B, S, DN_VW)
        y_dn = (o_dn * jax.nn.silu(dz)) @ w_br_dn[i]

        mix = (jax.nn.sigmoid(gate_att) * y_att + jax.nn.sigmoid(gate_dn) * y_dn) @ w_out[i]
        x = x + rms_norm(mix, norm_post[i])

        e = p[i] @ w_ple_proj[i]
        x = x + rms_norm(jax.nn.sigmoid(x @ w_ple_gate[i]) * e, ple_norm[i])
    return x
```

```python
import contextlib
import numpy as np
import ml_dtypes
import concourse.bass as bass
import concourse.mybir as mybir
from concourse.bass_utils import run_bass_kernel_spmd

F32 = mybir.dt.float32
BF16 = mybir.dt.bfloat16
I32 = mybir.dt.int32
AF = mybir.ActivationFunctionType
ALU = mybir.AluOpType

S = 8192
D = 1024
NT = S // 512
NPAIR = S // 128
EPS = 1e-6
NFM = 16
NTM = 136
WCOLS = NFM * 128 + NTM
TOK2 = 2048
NDMA_SEMS = 40


class Prog:
    NGEN = 6

    def __init__(self, nc, stack):
        self.nc = nc
        self.eng = {'pe': nc.tensor, 'act': nc.scalar, 'dve': nc.vector, 'pool': nc.gpsimd, 'sp': nc.sync}
        self.lists = {e: [] for e in self.eng}
        self.sems = {}
        for g in range(self.NGEN):
            for e in ('pe', 'act', 'dve', 'pool'):
                self.sems[('c', e, g)] = stack.enter_context(nc.semaphore("c_%s_%d" % (e, g)))
        self.dma = []
        for i in range(NDMA_SEMS):
            self.sems[('d', i)] = stack.enter_context(nc.semaphore("d_%d" % i))
            self.dma.append(0)
        self.rr = 0
        self.gen = 0
        self.cnt = {e: 0 for e in self.eng}
        self.seen = {e: {} for e in self.eng}
        self.lastw = {}
        self.readers = {}
        self.excl = set()

    def psum_keys(self, *keys):
        self.excl.update(keys)

    @staticmethod
    def _owner(tok):
        s = tok[0]
        return s[1] if s[0] == 'c' else None

    def _need(self, engine, tok, waits):
        s, v = tok
        if s[0] == 'c' and s[2] < self.gen:
            return
        if self.seen[engine].get(s, 0) < v:
            self.seen[engine][s] = v
            waits.append((s, v))

    def op(self, engine, fn, reads=(), writes=(), dma=False, inc=16):
        waits = []
        same_sync = engine in ('act', 'dve', 'pool')
        for k in list(reads) + list(writes):
            t = self.lastw.get(k)
            if t is not None and (self._owner(t) != engine or same_sync):
                self._need(engine, t, waits)
        for k in list(writes) + [k for k in reads if k in self.excl]:
            for t in self.readers.get(k, ()):
                if self._owner(t) != engine:
                    self._need(engine, t, waits)
        if dma:
            i = self.rr
            self.rr = (self.rr + 1) % NDMA_SEMS
            s = ('d', i)
            if self.dma[i] > 0:
                self._need(engine, (s, self.dma[i]), waits)
            self.dma[i] += inc
            tok = (s, self.dma[i])
            incv = inc
        else:
            self.cnt[engine] += 1
            s = ('c', engine, self.gen)
            tok = (s, self.cnt[engine])
            incv = 1
        self.lists[engine].append((waits, fn, s, incv))
        for k in writes:
            self.lastw[k] = tok
            self.readers[k] = []
        for k in reads:
            self.readers.setdefault(k, []).append(tok)
        return tok

    def barrier(self):
        toks = [(('c', e, self.gen), self.cnt[e]) for e in ('pe', 'act', 'dve', 'pool') if self.cnt[e] > 0]
        toks += [(('d', i), v) for i, v in enumerate(self.dma) if v > 0]
        for e in self.eng:
            waits = []
            for t in toks:
                if self._owner(t) != e:
                    self._need(e, t, waits)
            if waits:
                self.lists[e].append((waits, None, None, 0))
        self.gen += 1
        assert self.gen < self.NGEN
        for e in self.cnt:
            self.cnt[e] = 0

    def emit(self):
        nc = self.nc
        with nc.Block() as block:
            def run(ename):
                def body(eng):
                    for waits, fn, s, incv in self.lists[ename]:
                        for (ws, wv) in waits:
                            eng.wait_ge(self.sems[ws], wv)
                        if fn is not None:
                            fn(eng).then_inc(self.sems[s], incv)
                return body
            block.tensor(run('pe'))
            block.scalar(run('act'))
            block.vector(run('dve'))
            block.gpsimd(run('pool'))
            block.sync(run('sp'))


def emit_rsqrt(P, out_ap, in_ap, scale, reads, writes):
    P.op('act', lambda e: e.activation(out=out_ap, in_=in_ap, func=AF.Ln, bias=EPS, scale=scale), reads=reads, writes=writes)
    P.op('act', lambda e: e.activation(out=out_ap, in_=out_ap, func=AF.Exp, scale=-0.5), reads=writes, writes=writes)


def emit_rms_hT(P, x_src, wb, ident, xt, xn, hT, ps_tp, stat, tb, key):
    xs = xt[tb % 2]
    xk = ('xt', tb % 2)
    P.op('sp', lambda e: e.dma_start(out=xs[:, :], in_=x_src), writes=[xk], dma=True)
    sq = xn[tb % 2]
    nk = ('xn', tb % 2)
    st = stat[tb % 2]
    sk = ('stat', tb % 2)
    P.op('act', lambda e: e.activation(out=sq[:, :], in_=xs[:, :], func=AF.Square, accum_out=st[:, 0:1]),
         reads=[xk], writes=[nk, sk])
    emit_rsqrt(P, st[:, 2:3], st[:, 0:1], 1.0 / D, [sk], [sk])
    P.op('dve', lambda e: e.scalar_tensor_tensor(out=sq[:, :], in0=xs[:, :], scalar=st[:, 2:3], in1=wb[:, :],
                                                 op0=ALU.mult, op1=ALU.mult), reads=[xk, sk, 'wb'], writes=[nk])
    pk = ('ps_tp', tb % 2)
    pt = ps_tp[tb % 2]
    for k in range(8):
        P.op('pe', lambda e, k=k: e.transpose(out=pt[:, k * 128:(k + 1) * 128], in_=sq[:, k * 128:(k + 1) * 128],
                                              identity=ident[:, :]), reads=[nk, 'ident'], writes=[pk])
    P.op('act', lambda e: e.copy(out=hT[:, :, tb * 128:(tb + 1) * 128],
                                 in_=pt[:, :].rearrange("p (k t) -> p k t", k=8)), reads=[pk], writes=[key])


def load_cast(P, dst_ap_fn, src_ap_fn, stg, nchunks, dkey, width):
    for c in range(nchunks):
        sl = c % 2
        P.op('sp', lambda e, c=c, sl=sl: e.dma_start(out=stg[sl][:, 0:width], in_=src_ap_fn(c)),
             writes=[('stg', sl)], dma=True)
        eng = 'dve' if c % 2 == 0 else 'pool'
        P.op(eng, lambda e, c=c, sl=sl: e.tensor_copy(out=dst_ap_fn(c), in_=stg[sl][:, 0:width]),
             reads=[('stg', sl)], writes=[dkey])


def phase_proj(nc, P, io, scr):
    with contextlib.ExitStack() as st:
        sb = lambda name, shape, dt: st.enter_context(nc.sbuf_tensor(name, shape, dt))
        ps = lambda name, shape, dt: st.enter_context(nc.psum_tensor(name, shape, dt))
        W = sb("a_W", [128, 8, WCOLS], BF16)
        stg = [sb("a_stg%d" % i, [128, WCOLS], F32) for i in range(2)]
        wb = sb("a_wb", [128, D], F32)
        ident = sb("a_ident", [128, 128], BF16)
        ones = sb("a_ones", [128, 128], BF16)
        xt = [sb("a_xt%d" % i, [128, D], F32) for i in range(2)]
        xn = [sb("a_xn%d" % i, [128, D], BF16) for i in range(2)]
        stat = [sb("a_stat%d" % i, [128, 4], F32) for i in range(2)]
        hT = [sb("a_hT%d" % i, [128, 8, 512], BF16) for i in range(2)]
        cs = [sb("a_cs%d" % i, [128, 2, 512], F32) for i in range(2)]
        nw = sb("a_nw", [128, 4], F32)
        cw = sb("a_cw", [128, 6, 5], F32)
        cstg = [sb("a_cstg%d" % g, [128, 520], F32) for g in range(6)]
        cacc = [sb("a_cacc%d" % i, [128, 512], F32) for i in range(2)]
        csil = [sb("a_csil%d" % i, [128, 512], F32) for i in range(2)]
        sqb = [sb("a_sqb%d" % i, [128, 512], BF16) for i in range(2)]
        rstd = [sb("a_rstd%d" % i, [128, 512], F32) for i in range(2)]
        t1 = [sb("a_t1%d" % i, [128, 512], F32) for i in range(2)]
        t2 = [sb("a_t2%d" % i, [128, 512], F32) for i in range(2)]
        ob = [sb("a_ob%d" % i, [128, 512], BF16) for i in range(4)]
        vtm = [sb("a_vtm%d" % i, [128, 128], BF16) for i in range(2)]
        gsm = [sb("a_gsm%d" % i, [128, 32], F32) for i in range(2)]
        cst = sb("a_cst", [128, 8], F32)
        ps_tp = [ps("a_ptp%d" % i, [128, D], BF16) for i in range(2)]
        ps_fm = [ps("a_pfm%d" % i, [128, 512], F32) for i in range(4)]
        ps_ss = ps("a_pss", [128, 512], F32)
        ps_tm = ps("a_ptm", [128, 512], F32)
        P.psum_keys(('pfm', 0), ('pfm', 1), ('pfm', 2), ('pfm', 3), 'pss', 'ptm', ('ps_tp', 0), ('ps_tp', 1))

        P.op('sp', lambda e: e.dma_start(out=wb[:, :], in_=io['norm_pre'][0:1, :].partition_broadcast(128)),
             writes=['wb'], dma=True)
        P.op('sp', lambda e: e.dma_start(out=ident[:, :], in_=io['ident'][:, :]), writes=['ident'], dma=True)
        P.op('sp', lambda e: e.dma_start(out=ones[:, :], in_=io['ones'][:, :]), writes=['ones'], dma=True)
        P.op('sp', lambda e: e.dma_start(out=nw[:, :], in_=io['nw'][:, :]), writes=['nw'], dma=True)
        P.op('sp', lambda e: e.dma_start(out=cw[:, :, :], in_=io['cw'][:, :, :]), writes=['cw'], dma=True)
        P.op('sp', lambda e: e.dma_start(out=cst[:, :], in_=io['gcst'][0:1, :].partition_broadcast(128)),
             writes=['cst'], dma=True)
        P.op('act', lambda e: e.activation(out=cst[:, 4:8], in_=cst[:, 4:8], func=AF.Exp), reads=['cst'], writes=['cst'])
        P.op('dve', lambda e: e.tensor_scalar(out=cst[:, 4:8], in0=cst[:, 4:8], scalar1=-1.0, scalar2=None,
                                              op0=ALU.mult), reads=['cst'], writes=['cst'])
        for g in range(6):
            P.op('pool', lambda e, g=g: e.memset(cstg[g][:, :], 0.0), writes=[('cstg', g)])
        load_cast(P, lambda c: W[:, c, :], lambda c: io['w_in'][c * 128:(c + 1) * 128, :], stg, 8, 'W', WCOLS)

        rope_pairs = [(0, 2, 0, ('QT', 0)), (1, 3, 0, ('QT', 1)), (4, 5, 2, ('KT', 0))]
        silu_groups = [(6, ('SAZ', 0)), (7, ('SAZ', 1)), (14, ('SDZ', 0)), (15, ('SDZ', 1))]
        conv_groups = [(8, 'DQ', 0), (9, 'DQ', 1), (10, 'DK', 0), (11, 'DK', 1), (12, 'DV', 0), (13, 'DV', 1)]
        cnt = {'fm': 0, 'x': 0, 'ob': 0}

        def fm_matmul(g, h):
            slot = cnt['fm'] % 4
            cnt['fm'] += 1
            for k in range(8):
                P.op('pe', lambda e, k=k, slot=slot: e.matmul(ps_fm[slot][:, :], lhsT=W[:, k, g * 128:(g + 1) * 128],
                                                              rhs=h[0][:, k, :], start=(k == 0), stop=(k == 7)),
                     reads=['W', h[1]], writes=[('pfm', slot)])
            return slot

        def next_ob():
            i = cnt['ob'] % 4
            cnt['ob'] += 1
            return i

        def sumsq_rstd(src_ap, srckey, scale, i2):
            P.op('act', lambda e: e.activation(out=sqb[i2][:, :], in_=src_ap, func=AF.Square),
                 reads=[srckey], writes=[('sqb', i2)])
            P.op('pe', lambda e: e.matmul(ps_ss[:, :], lhsT=ones[:, :], rhs=sqb[i2][:, :], start=True, stop=True),
                 reads=['ones', ('sqb', i2)], writes=['pss'])
            emit_rsqrt(P, rstd[i2][:, :], ps_ss[:, :], scale, ['pss'], [('rstd', i2)])

        def conv_post(g, name, hh, ncols, ocol0, tok0):
            gi = g - 8
            i2 = gi % 2
            ck = ('cstg', gi)
            acc = cacc[i2]
            ak = ('cacc', i2)
            P.op('pool', lambda e: e.tensor_scalar(out=acc[:, 0:ncols], in0=cstg[gi][:, 0:ncols], scalar1=cw[:, gi, 0:1],
                                                   scalar2=None, op0=ALU.mult), reads=[ck, 'cw'], writes=[ak])
            for k in range(1, 5):
                P.op('dve', lambda e, k=k: e.scalar_tensor_tensor(out=acc[:, 0:ncols], in0=cstg[gi][:, k:k + ncols],
                                                                   scalar=cw[:, gi, k:k + 1], in1=acc[:, 0:ncols],
                                                                   op0=ALU.mult, op1=ALU.add),
                     reads=[ck, 'cw', ak], writes=[ak])
            sil = csil[i2]
            sk = ('csil', i2)
            P.op('act', lambda e: e.activation(out=sil[:, 0:ncols], in_=acc[:, 0:ncols], func=AF.Silu),
                 reads=[ak], writes=[sk])
            oi = next_ob()
            ok = ('ob', oi)
            n = ncols - ocol0
            if name == 'DV':
                P.op('dve', lambda e: e.tensor_copy(out=ob[oi][:, 0:n], in_=sil[:, ocol0:ncols]), reads=[sk], writes=[ok])
            else:
                P.op('act', lambda e: e.activation(out=sqb[i2][:, 0:ncols], in_=sil[:, 0:ncols], func=AF.Square),
                     reads=[sk], writes=[('sqb', i2)])
                P.op('pe', lambda e: e.matmul(ps_ss[:, 0:ncols], lhsT=ones[:, :], rhs=sqb[i2][:, 0:ncols],
                                              start=True, stop=True), reads=['ones', ('sqb', i2)], writes=['pss'])
                emit_rsqrt(P, rstd[i2][:, 0:ncols], ps_ss[:, 0:ncols], 1.0, ['pss'], [('rstd', i2)])
                sc = (128.0 ** -0.5) if name == 'DQ' else 1.0
                P.op('dve', lambda e: e.scalar_tensor_tensor(out=ob[oi][:, 0:n], in0=sil[:, ocol0:ncols], scalar=sc,
                                                             in1=rstd[i2][:, ocol0:ncols], op0=ALU.mult, op1=ALU.mult),
                     reads=[sk, ('rstd', i2)], writes=[ok])
            P.op('sp', lambda e: e.dma_start(out=scr[name][hh, :, tok0:tok0 + n], in_=ob[oi][:, 0:n]),
                 reads=[ok], writes=[(name, hh)], dma=True)

        for T in range(NT if NT_LIM is None else NT_LIM):
            t0 = T * 512
            h = (hT[T % 2], ('hT', T % 2))
            for tb in range(4):
                emit_rms_hT(P, io['x'][t0 + tb * 128:t0 + (tb + 1) * 128, :], wb, ident, xt, xn, h[0], ps_tp, stat,
                            tb, h[1])
            c2 = cs[T % 2]
            ck2 = ('cs', T % 2)
            P.op('sp', lambda e, c2=c2, t0=t0: e.dma_start(out=c2[:, 0, :], in_=io['cos'][:, t0:t0 + 512]),
                 writes=[ck2], dma=True)
            P.op('sp', lambda e, c2=c2, t0=t0: e.dma_start(out=c2[:, 1, :], in_=io['sin'][:, t0:t0 + 512]),
                 writes=[ck2], dma=True)
            for (g, gs, wc, (dn, hh)) in (rope_pairs if 'rope' in PARTS else []):
                sa = fm_matmul(g, h)
                sbk = fm_matmul(gs, h)
                i2 = cnt['x'] % 2
                cnt['x'] += 1
                sumsq_rstd(ps_fm[sa][:, :], ('pfm', sa), 1.0 / 128, i2)
                if ROPE_LVL < 2:
                    continue
                if ROPE_VAR != 3:
                    P.op('dve', lambda e, sa=sa, i2=i2, wc=wc: e.tensor_scalar(
                        out=t1[i2][:, :], in0=ps_fm[sa][:, :], scalar1=(1.0 if ROPE_VAR == 1 else nw[:, wc:wc + 1]), scalar2=None, op0=ALU.mult),
                        reads=[('pfm', sa), 'nw'] + ([('sqb', i2)] if ROPE_VAR == 4 else []), writes=[('t1', i2)])
                if ROPE_VAR != 2:
                    P.op('dve', lambda e, sbk=sbk, i2=i2, wc=wc: e.tensor_scalar(
                        out=t2[i2][:, :], in0=ps_fm[sbk][:, :], scalar1=(1.0 if ROPE_VAR == 1 else nw[:, wc + 1:wc + 2]), scalar2=None, op0=ALU.mult),
                        reads=[('pfm', sbk), 'nw'], writes=[('t2', i2)])
                P.op('pool', lambda e, i2=i2, c2=c2: e.tensor_tensor(out=t1[i2][:, :], in0=t1[i2][:, :], in1=c2[:, 0, :],
                                                                     op=ALU.mult), reads=[('t1', i2), ck2], writes=[('t1', i2)])
                P.op('pool', lambda e, i2=i2, c2=c2: e.tensor_tensor(out=t2[i2][:, :], in0=t2[i2][:, :], in1=c2[:, 1, :],
                                                                     op=ALU.mult), reads=[('t2', i2), ck2], writes=[('t2', i2)])
                if ROPE_LVL < 3:
                    continue
                P.op('pool', lambda e, i2=i2: e.tensor_tensor(out=t1[i2][:, :], in0=t1[i2][:, :], in1=t2[i2][:, :],
                                                              op=ALU.add), reads=[('t1', i2), ('t2', i2)],
                     writes=[('t1', i2)])
                oi = next_ob()
                P.op('pool', lambda e, i2=i2, oi=oi: e.tensor_tensor(out=ob[oi][:, :], in0=t1[i2][:, :],
                                                                     in1=rstd[i2][:, :], op=ALU.mult),
                     reads=[('t1', i2), ('rstd', i2)], writes=[('ob', oi)])
                P.op('sp', lambda e, oi=oi, dn=dn, hh=hh, t0=t0: e.dma_start(out=scr[dn][hh, :, t0:t0 + 512],
                                                                             in_=ob[oi][:, :]),
                     reads=[('ob', oi)], writes=[(dn, hh)], dma=True)
            for (g, (dn, hh)) in (silu_groups if 'silu' in PARTS else []):
                sa = fm_matmul(g, h)
                oi = next_ob()
                P.op('act', lambda e, sa=sa, oi=oi: e.activation(out=ob[oi][:, :], in_=ps_fm[sa][:, :], func=AF.Silu),
                     reads=[('pfm', sa)], writes=[('ob', oi)])
                P.op('sp', lambda e, oi=oi, dn=dn, hh=hh, t0=t0: e.dma_start(out=scr[dn][hh, :, t0:t0 + 512],
                                                                             in_=ob[oi][:, :]),
                     reads=[('ob', oi)], writes=[(dn, hh)], dma=True)
            for (g, name, hh) in (conv_groups if 'conv' in PARTS else []):
                sa = fm_matmul(g, h)
                gi = g - 8
                P.op('act', lambda e, sa=sa, gi=gi: e.copy(out=cstg[gi][:, 4:516], in_=ps_fm[sa][:, :]),
                     reads=[('pfm', sa)], writes=[('cstg', gi)])
                if T == 0:
                    conv_post(g, name, hh, 512, 2, 0)
                else:
                    conv_post(g, name, hh, 512, 0, t0 - 2)
                P.op('pool', lambda e, gi=gi: e.tensor_copy(out=cstg[gi][:, 0:4], in_=cstg[gi][:, 512:516]),
                     reads=[('cstg', gi)], writes=[('cstg', gi)])
                if T == NT - 1:
                    P.op('pool', lambda e, gi=gi: e.tensor_copy(out=cstg[gi][:, 0:66], in_=cstg[gi][:, 450:516]),
                         reads=[('cstg', gi)], writes=[('cstg', gi)])
                    P.op('pool', lambda e, gi=gi: e.memset(cstg[gi][:, 66:72], 0.0), writes=[('cstg', gi)])
                    conv_post(g, name, hh, 64, 0, S - 64)
            for tb in (range(4) if 'tm' in PARTS else []):
                for k in range(8):
                    P.op('pe', lambda e, k=k, tb=tb, h=h: e.matmul(ps_tm[:, 0:NTM], lhsT=h[0][:, k, tb * 128:(tb + 1) * 128],
                                                                   rhs=W[:, k, NFM * 128:WCOLS], start=(k == 0), stop=(k == 7)),
                         reads=['W', h[1]], writes=['ptm'])
                i2 = tb % 2
                vk = ('vtm', i2)
                P.op('act', lambda e, i2=i2: e.copy(out=vtm[i2][:, :], in_=ps_tm[:, 0:128]), reads=['ptm'], writes=[vk])
                r0 = t0 + tb * 128
                P.op('sp', lambda e, i2=i2, r0=r0: e.dma_start(out=scr['V'][r0:r0 + 128, :], in_=vtm[i2][:, :]),
                     reads=[vk], writes=['V'], dma=True)
                g2 = gsm[i2]
                gk = ('gsm', i2)
                P.op('act', lambda e, g2=g2: e.activation(out=g2[:, 4:8], in_=ps_tm[:, 128:132], func=AF.Sigmoid),
                     reads=['ptm'], writes=[gk])
                P.op('dve', lambda e, g2=g2: e.tensor_tensor(out=g2[:, 8:12], in0=ps_tm[:, 132:136], in1=cst[:, 0:4],
                                                             op=ALU.add), reads=['ptm', 'cst'], writes=[gk])
                P.op('act', lambda e, g2=g2: e.activation(out=g2[:, 12:16], in_=g2[:, 8:12], func=AF.Abs), reads=[gk], writes=[gk])
                P.op('act', lambda e, g2=g2: e.activation(out=g2[:, 16:20], in_=g2[:, 12:16], func=AF.Exp, scale=-1.0),
                     reads=[gk], writes=[gk])
                P.op('act', lambda e, g2=g2: e.activation(out=g2[:, 20:24], in_=g2[:, 16:20], func=AF.Ln, bias=1.0),
                     reads=[gk], writes=[gk])
                P.op('act', lambda e, g2=g2: e.activation(out=g2[:, 24:28], in_=g2[:, 8:12], func=AF.Relu), reads=[gk], writes=[gk])
                P.op('dve', lambda e, g2=g2: e.tensor_tensor(out=g2[:, 24:28], in0=g2[:, 24:28], in1=g2[:, 20:24],
                                                             op=ALU.add), reads=[gk], writes=[gk])
                P.op('dve', lambda e, g2=g2: e.tensor_tensor(out=g2[:, 0:4], in0=g2[:, 24:28], in1=cst[:, 4:8],
                                                             op=ALU.mult), reads=[gk, 'cst'], writes=[gk])
                P.op('sp', lambda e, g2=g2, r0=r0: e.dma_start(out=scr['GB'][r0:r0 + 128, :], in_=g2[:, 0:8]),
                     reads=[gk], writes=['GB'], dma=True)
        P.barrier()


def phase_attn(nc, P, io, scr):
    NKB = S // 128
    NQC = S // 512
    with contextlib.ExitStack() as st:
        sb = lambda name, shape, dt: st.enter_context(nc.sbuf_tensor(name, shape, dt))
        ps = lambda name, shape, dt: st.enter_context(nc.psum_tensor(name, shape, dt))
        QT = [sb("b_QT%d" % i, [128, S], BF16) for i in range(2)]
        KT = sb("b_KT", [128, S], BF16)
        V = sb("b_V", [128, NKB, 128], BF16)
        ones = sb("b_ones", [128, 128], BF16)
        pT = [sb("b_pT%d" % i, [128, 512], BF16) for i in range(3)]
        rinv = [sb("b_rinv%d" % i, [128, 512], F32) for i in range(2)]
        of = [sb("b_of%d" % i, [128, 512], F32) for i in range(2)]
        saz = [sb("b_saz%d" % i, [128, 512], BF16) for i in range(2)]
        gb = [sb("b_g%d" % i, [128, 512], BF16) for i in range(2)]
        ps_s = [ps("b_ps%d" % i, [128, 512], F32) for i in range(3)]
        ps_o = [ps("b_po%d" % i, [128, 512], F32) for i in range(2)]
        ps_r = [ps("b_pr%d" % i, [128, 512], F32) for i in range(2)]
        P.psum_keys(('b_ps', 0), ('b_ps', 1), ('b_ps', 2), ('b_po', 0), ('b_po', 1), ('b_pr', 0), ('b_pr', 1))

        P.op('sp', lambda e: e.dma_start(out=ones[:, :], in_=io['ones'][:, :]), writes=['b_ones'], dma=True)
        for c in range(4):
            sl = slice(c * 2048, (c + 1) * 2048)
            for hh in range(2):
                P.op('sp', lambda e, hh=hh, sl=sl: e.dma_start(out=QT[hh][:, sl], in_=scr['QT'][hh, :, sl]),
                     reads=[('QT', hh)], writes=[('b_QT', hh)], dma=True)
            P.op('sp', lambda e, sl=sl: e.dma_start(out=KT[:, sl], in_=scr['KT'][0, :, sl]),
                 reads=[('KT', 0)], writes=['b_KT'], dma=True)
        vsrc = scr['V'].ap().rearrange("(b p) d -> p b d", p=128)
        for c in range(4):
            P.op('sp', lambda e, c=c: e.dma_start(out=V[:, c * 16:(c + 1) * 16, :], in_=vsrc[:, c * 16:(c + 1) * 16, :]),
                 reads=['V'], writes=['b_V'], dma=True)

        scale = 128.0 ** -0.5
        it = 0
        for hh in range(2):
            for qc in range(NQC):
                q0 = qc * 512
                po = ps_o[it % 2]
                pr = ps_r[it % 2]
                pok = ('b_po', it % 2)
                prk = ('b_pr', it % 2)

                def smm(kb, hh=hh, q0=q0):
                    s = kb % 3
                    P.op('pe', lambda e: e.matmul(ps_s[s][:, :], lhsT=KT[:, kb * 128:(kb + 1) * 128],
                                                  rhs=QT[hh][:, q0:q0 + 512], start=True, stop=True),
                         reads=['b_KT', ('b_QT', hh)], writes=[('b_ps', s)])
                    P.op('act', lambda e: e.activation(out=pT[s][:, :], in_=ps_s[s][:, :], func=AF.Exp, scale=scale),
                         reads=[('b_ps', s)], writes=[('b_pT', s)])

                smm(0)
                smm(1)
                for kb in range(NKB):
                    if kb + 2 < NKB:
                        smm(kb + 2)
                    s = kb % 3
                    P.op('pe', lambda e, kb=kb, s=s, po=po: e.matmul(po[:, :], lhsT=V[:, kb, :], rhs=pT[s][:, :],
                                                              start=(kb == 0), stop=(kb == NKB - 1)),
                         reads=['b_V', ('b_pT', s)], writes=[pok])
                    P.op('pe', lambda e, kb=kb, s=s, pr=pr: e.matmul(pr[:, :], lhsT=ones[:, :], rhs=pT[s][:, :],
                                                              start=(kb == 0), stop=(kb == NKB - 1)),
                         reads=['b_ones', ('b_pT', s)], writes=[prk])
                i2 = it % 2
                P.op('sp', lambda e, i2=i2, hh=hh, q0=q0: e.dma_start(out=saz[i2][:, :], in_=scr['SAZ'][hh, :, q0:q0 + 512]),
                     reads=[('SAZ', hh)], writes=[('b_saz', i2)], dma=True)
                P.op('dve', lambda e, i2=i2, pr=pr: e.reciprocal(out=rinv[i2][:, :], in_=pr[:, :]),
                     reads=[prk], writes=[('b_rinv', i2)])
                P.op('dve', lambda e, i2=i2, po=po: e.tensor_tensor(out=of[i2][:, :], in0=po[:, :], in1=rinv[i2][:, :],
                                                                    op=ALU.mult),
                     reads=[pok, ('b_rinv', i2)], writes=[('b_of', i2)])
                P.op('pool', lambda e, i2=i2: e.tensor_tensor(out=gb[i2][:, :], in0=of[i2][:, :], in1=saz[i2][:, :],
                                                              op=ALU.mult),
                     reads=[('b_of', i2), ('b_saz', i2)], writes=[('b_g', i2)])
                P.op('sp', lambda e, i2=i2, hh=hh, q0=q0: e.dma_start(out=scr['GT'][hh * 128:(hh + 1) * 128, q0:q0 + 512],
                                                                      in_=gb[i2][:, :]),
                     reads=[('b_g', i2)], writes=['GT'], dma=True)
                it += 1
        P.barrier()


def phase_dn(nc, P, io, scr):
    with contextlib.ExitStack() as st:
        sb = lambda name, shape, dt: st.enter_context(nc.sbuf_tensor(name, shape, dt))
        ps = lambda name, shape, dt: st.enter_context(nc.psum_tensor(name, shape, dt))
        TRI4 = sb("c_TRI4", [128, 4, 128], F32)
        SM4 = sb("c_SM4", [128, 4, 128], F32)
        IM4 = sb("c_IM4", [128, 4, 128], F32)
        I4 = sb("c_I4", [128, 4, 128], F32)
        ONESF = sb("c_ONESF", [128, 128], F32)
        BLK = sb("c_BLK", [128, 128], F32)
        ident = sb("c_ident", [128, 128], BF16)
        dnw = sb("c_dnw", [128, 128], F32)
        Oacc = sb("c_Oacc", [128, NPAIR, 2, 128], F32)
        Sf = sb("c_Sf", [128, 4, 128], F32)
        Sbf = sb("c_Sbf", [128, 4, 128], BF16)
        qT4 = sb("c_qT4", [128, 4, 128], BF16)
        kT4 = sb("c_kT4", [128, 4, 128], BF16)
        vT4 = sb("c_vT4", [128, 4, 128], BF16)
        gbt = sb("c_gb", [128, 8], F32)
        sm = sb("c_sm", [128, 24], F32)
        gtri = sb("c_gtri", [128, 4, 128], F32)
        absz = sb("c_absz", [128, 4, 128], F32)
        W4 = sb("c_W4", [128, 4, 128], F32)
        EROW = sb("c_EROW", [128, 4, 128], F32)
        Wm = sb("c_Wm", [128, 4, 128], F32)
        Wi = sb("c_Wi", [128, 4, 128], F32)
        Pb = [sb("c_P%d" % i, [128, 4, 128], BF16) for i in range(2)]
        Ptb = [sb("c_Pt%d" % i, [128, 4, 128], BF16) for i in range(2)]
        Xb = [sb("c_X%d" % i, [128, 4, 128], BF16) for i in range(2)]
        aT4 = sb("c_aT4", [128, 4, 128], BF16)
        qg4 = sb("c_qg4", [128, 4, 128], BF16)
        kg4 = sb("c_kg4", [128, 4, 128], BF16)
        kdec4 = sb("c_kdec4", [128, 4, 128], BF16)
        vtok4 = sb("c_vtok4", [128, 4, 128], BF16)
        ub4 = sb("c_ub4", [128, 4, 128], F32)
        wT4 = sb("c_wT4", [128, 4, 128], BF16)
        vnew = sb("c_vnew", [128, 4, 128], BF16)
        fstat = sb("c_fstat", [128, 4], F32)
        fsq = sb("c_fsq", [128, 128], F32)
        fon = sb("c_fon", [128, 128], BF16)
        sdz = sb("c_sdz", [128, 512], BF16)
        gout = sb("c_gout", [128, 512], BF16)
        B0 = ps("c_B0", [128, 8, 128], BF16)
        B1 = ps("c_B1", [128, 4, 128], F32)
        B2 = ps("c_B2", [128, 4, 128], F32)
        B3 = ps("c_B3", [128, 4, 128], F32)
        U = [ps("c_U%d" % i, [128, 4, 128], F32) for i in range(4)]
        P.psum_keys('B0', 'B1', 'B2', 'B3', ('U', 0), ('U', 1), ('U', 2), ('U', 3))

        def ld(t, src, key):
            P.op('sp', lambda e: e.dma_start(out=t, in_=src), writes=[key], dma=True)
        ld(TRI4[:, :, :], io['TRI4'][:, :, :], 'TRI4')
        ld(SM4[:, :, :], io['SM4'][:, :, :], 'SM4')
        ld(IM4[:, :, :], io['IM4'][:, :, :], 'IM4')
        ld(I4[:, :, :], io['I4'][:, :, :], 'I4')
        ld(ONESF[:, :], io['ONESF'][:, :], 'ONESF')
        ld(BLK[:, :], io['BLK'][:, :], 'BLK')
        ld(ident[:, :], io['ident'][:, :], 'c_ident')
        ld(dnw[:, :], io['dn_norm'][0:1, :].partition_broadcast(128), 'dnw')
        P.op('pool', lambda e: e.memset(Sf[:, :, :], 0.0), writes=['Sf'])
        P.op('pool', lambda e: e.memset(Sbf[:, :, :], 0.0), writes=['Sbf'])

        flat = lambda t: t[:, :, :].rearrange("p u t -> p (u t)")

        for p in range(NPAIR if DN_STEPS is None else DN_STEPS):
            pair = [p, p, NPAIR - 1 - p, NPAIR - 1 - p]
            for u in range(4):
                c0 = pair[u] * 128
                hh = u % 2
                P.op('sp', lambda e, u=u, hh=hh, c0=c0: e.dma_start(out=qT4[:, u, :], in_=scr['DQ'][hh, :, c0:c0 + 128]),
                     reads=[('DQ', hh)], writes=['qT4'], dma=True)
                P.op('sp', lambda e, u=u, hh=hh, c0=c0: e.dma_start(out=kT4[:, u, :], in_=scr['DK'][hh, :, c0:c0 + 128]),
                     reads=[('DK', hh)], writes=['kT4'], dma=True)
                P.op('sp', lambda e, u=u, hh=hh, c0=c0: e.dma_start(out=vT4[:, u, :], in_=scr['DV'][hh, :, c0:c0 + 128]),
                     reads=[('DV', hh)], writes=['vT4'], dma=True)
            for d in range(2):
                r0 = pair[2 * d] * 128
                for off in (0, 4):
                    a = off + 2 * d
                    P.op('sp', lambda e, r0=r0, a=a: e.dma_start(out=gbt[:, a:a + 2], in_=scr['GB'][r0:r0 + 128, a:a + 2]),
                         reads=['GB'], writes=['gbt'], dma=True)
            for u in range(4):
                P.op('dve', lambda e, u=u: e.tensor_scalar(out=gtri[:, u, :], in0=TRI4[:, u, :], scalar1=gbt[:, u:u + 1],
                                                           scalar2=None, op0=ALU.mult), reads=['TRI4', 'gbt'], writes=['gtri'])
            P.op('dve', lambda e: e.tensor_scalar(out=sm[:, 20:24], in0=gbt[:, 4:8], scalar1=-1.0, scalar2=None,
                                                  op0=ALU.mult), reads=['gbt'], writes=['negb'])
            P.op('pe', lambda e: e.matmul(B1[:, 0, 0:2], lhsT=TRI4[:, 0, :], rhs=gbt[:, 0:2], start=True, stop=True),
                 reads=['TRI4', 'gbt'], writes=['B1'])
            P.op('pe', lambda e: e.matmul(B1[:, 0, 2:4], lhsT=TRI4[:, 2, :], rhs=gbt[:, 2:4], start=True, stop=True),
                 reads=['TRI4', 'gbt'], writes=['B1'])
            P.op('pe', lambda e: e.matmul(B1[:, 0, 4:8], lhsT=BLK[:, :], rhs=gbt[:, 0:4], start=True, stop=True),
                 reads=['BLK', 'gbt'], writes=['B1'])
            P.op('act', lambda e: e.copy(out=sm[:, 0:8], in_=B1[:, 0, 0:8]), reads=['B1'], writes=['sm'])
            for u in range(4):
                P.op('pe', lambda e, u=u: e.matmul(B1[:, u, :], lhsT=ONESF[:, :], rhs=gtri[:, u, :], start=True, stop=True),
                     reads=['ONESF', 'gtri'], writes=['B1'])
            P.op('dve', lambda e: e.tensor_tensor(out=sm[:, 8:12], in0=sm[:, 4:8], in1=sm[:, 0:4], op=ALU.subtract),
                 reads=['sm'], writes=['sm2'])
            P.op('act', lambda e: e.activation(out=sm[:, 12:16], in_=sm[:, 8:12], func=AF.Exp), reads=['sm2'], writes=['sm3'])
            P.op('act', lambda e: e.activation(out=sm[:, 16:20], in_=sm[:, 0:4], func=AF.Exp), reads=['sm'], writes=['sm4'])
            for u in range(4):
                P.op('dve', lambda e, u=u: e.tensor_scalar(out=absz[:, u, :], in0=B1[:, u, :], scalar1=sm[:, u:u + 1],
                                                           scalar2=None, op0=ALU.subtract),
                     reads=['B1', 'sm'], writes=['absz'])
            P.op('act', lambda e: e.activation(out=flat(absz), in_=flat(absz), func=AF.Abs), reads=['absz'], writes=['absz'])
            P.op('act', lambda e: e.activation(out=flat(W4), in_=flat(absz), func=AF.Exp, scale=-1.0),
                 reads=['absz'], writes=['W4'])
            P.op('act', lambda e: e.activation(out=flat(EROW), in_=B1[:, :, :].rearrange("p u t -> p (u t)"), func=AF.Exp),
                 reads=['B1'], writes=['EROW'])
            if DN_LVL < 2:
                continue
            for u in range(4):
                P.op('pe', lambda e, u=u: e.matmul(B2[:, u, :], lhsT=kT4[:, u, :], rhs=kT4[:, u, :], start=True, stop=True),
                     reads=['kT4'], writes=['B2'])
            for u in range(4):
                P.op('pe', lambda e, u=u: e.matmul(B3[:, u, :], lhsT=kT4[:, u, :], rhs=qT4[:, u, :], start=True, stop=True),
                     reads=['kT4', 'qT4'], writes=['B3'])
            for u in range(4):
                P.op('pe', lambda e, u=u: e.transpose(out=B0[:, u, :], in_=kT4[:, u, :], identity=ident[:, :]),
                     reads=['kT4', 'c_ident'], writes=['B0'])
            for u in range(4):
                P.op('pe', lambda e, u=u: e.transpose(out=B0[:, 4 + u, :], in_=vT4[:, u, :], identity=ident[:, :]),
                     reads=['vT4', 'c_ident'], writes=['B0'])
            P.op('dve', lambda e: e.tensor_tensor(out=flat(Wm), in0=flat(W4), in1=flat(SM4), op=ALU.mult),
                 reads=['W4', 'SM4'], writes=['Wm'])
            P.op('pool', lambda e: e.tensor_tensor(out=flat(Wi), in0=flat(W4), in1=flat(IM4), op=ALU.mult),
                 reads=['W4', 'IM4'], writes=['Wi'])
            for u in range(4):
                P.op('dve', lambda e, u=u: e.scalar_tensor_tensor(out=Pb[0][:, u, :], in0=B2[:, u, :], scalar=sm[:, 20 + u:21 + u],
                                                                  in1=Wm[:, u, :], op0=ALU.mult, op1=ALU.mult),
                     reads=['B2', 'negb', 'Wm'], writes=[('P', 0)])
            P.op('dve', lambda e: e.tensor_tensor(out=flat(aT4), in0=B3[:, :, :].rearrange("p u t -> p (u t)"), in1=flat(Wi),
                                                  op=ALU.mult), reads=['B3', 'Wi'], writes=['aT4'])
            P.op('dve', lambda e: e.tensor_tensor(out=flat(qg4), in0=flat(qT4), in1=flat(EROW), op=ALU.mult),
                 reads=['qT4', 'EROW'], writes=['qg4'])
            for u in range(4):
                P.op('act', lambda e, u=u: e.activation(out=kg4[:, u, :], in_=B0[:, u, :], func=AF.Copy, scale=sm[:, 16 + u:17 + u]),
                     reads=['B0', 'sm4'], writes=['kg4'])
                P.op('act', lambda e, u=u: e.activation(out=kdec4[:, u, :], in_=B0[:, u, :], func=AF.Copy, scale=sm[:, 12 + u:13 + u]),
                     reads=['B0', 'sm3'], writes=['kdec4'])
            P.op('act', lambda e: e.copy(out=flat(vtok4), in_=B0[:, 4:8, :].rearrange("p u t -> p (u t)")),
                 reads=['B0'], writes=['vtok4'])
            if DN_LVL < 3:
                continue
            for u in range(4):
                P.op('pe', lambda e, u=u: e.transpose(out=B0[:, u, :], in_=Pb[0][:, u, :], identity=ident[:, :]),
                     reads=[('P', 0), 'c_ident'], writes=['B0'])
            P.op('act', lambda e: e.copy(out=flat(Ptb[0]), in_=B0[:, 0:4, :].rearrange("p u t -> p (u t)")),
                 reads=['B0'], writes=[('Pt', 0)])
            P.op('pool', lambda e: e.tensor_tensor(out=flat(Xb[0]), in0=flat(Pb[0]), in1=flat(I4), op=ALU.add),
                 reads=[('P', 0), 'I4'], writes=[('X', 0)])
            cur = 0
            for lvl in range(1, 6):
                nxt = 1 - cur
                if lvl < 5:
                    for u in range(4):
                        P.op('pe', lambda e, u=u, cur=cur: e.matmul(B2[:, u, :], lhsT=Ptb[cur][:, u, :], rhs=Pb[cur][:, u, :],
                                                                    start=True, stop=True),
                             reads=[('P', cur), ('Pt', cur)], writes=['B2'])
                for u in range(4):
                    P.op('pe', lambda e, u=u, cur=cur: e.matmul(B3[:, u, :], lhsT=Pb[cur][:, u, :], rhs=Ptb[cur][:, u, :],
                                                                start=True, stop=True),
                         reads=[('P', cur), ('Pt', cur)], writes=['B3'])
                if lvl < 5:
                    P.op('act', lambda e, nxt=nxt: e.copy(out=flat(Pb[nxt]), in_=B2[:, :, :].rearrange("p u t -> p (u t)")),
                         reads=['B2'], writes=[('P', nxt)])
                P.op('dve', lambda e, nxt=nxt: e.tensor_copy(out=flat(Ptb[nxt]), in_=B3[:, :, :].rearrange("p u t -> p (u t)")),
                     reads=['B3'], writes=[('Pt', nxt)])
                for u in range(4):
                    P.op('pe', lambda e, u=u, cur=cur, nxt=nxt: e.matmul(B1[:, u, :], lhsT=Ptb[nxt][:, u, :], rhs=Xb[cur][:, u, :],
                                                                         start=True, stop=True),
                         reads=[('Pt', nxt), ('X', cur)], writes=['B1'])
                P.op('dve', lambda e, cur=cur, nxt=nxt: e.tensor_tensor(out=flat(Xb[nxt]), in0=B1[:, :, :].rearrange("p u t -> p (u t)"),
                                                                        in1=flat(Xb[cur]), op=ALU.add),
                     reads=['B1', ('X', cur)], writes=[('X', nxt)])
                cur = nxt
            if DN_LVL < 4:
                continue
            X = Xb[cur]
            Xk = ('X', cur)
            for u in range(4):
                P.op('pe', lambda e, u=u, X=X: e.matmul(B2[:, u, :], lhsT=X[:, u, :], rhs=vtok4[:, u, :], start=True, stop=True),
                     reads=[Xk, 'vtok4'], writes=['B2'])
            for u in range(4):
                P.op('pe', lambda e, u=u, X=X: e.matmul(B3[:, u, :], lhsT=kg4[:, u, :], rhs=X[:, u, :], start=True, stop=True),
                     reads=[Xk, 'kg4'], writes=['B3'])
            for u in range(4):
                P.op('act', lambda e, u=u: e.activation(out=ub4[:, u, :], in_=B2[:, u, :], func=AF.Copy, scale=gbt[:, 4 + u:5 + u]),
                     reads=['B2', 'gbt'], writes=['ub4'])
            P.op('dve', lambda e: e.tensor_copy(out=flat(wT4), in_=B3[:, :, :].rearrange("p u t -> p (u t)")),
                 reads=['B3'], writes=['wT4'])
            if DN_LVL < 5:
                continue
            for ci in range(2):
                for u in range(4):
                    fwd = u < 2
                    c = ci if fwd else 1 - ci
                    r = slice(64 * c, 64 * c + 64)
                    col = 64 * c + 63 if fwd else 64 * c
                    sk = ('Sbf', u)
                    uk = ('U', u)
                    P.op('pe', lambda e, u=u: e.matmul(U[u][:, 0, :], lhsT=wT4[:, u, :], rhs=Sbf[:, u, :], start=True, stop=True),
                         reads=['wT4', sk, 'Sbf'], writes=[uk])
                    P.op('dve', lambda e, u=u, r=r: e.scalar_tensor_tensor(out=vnew[r, u, :], in0=U[u][r, 0, :],
                                                                           scalar=sm[r, 20 + u:21 + u], in1=ub4[r, u, :],
                                                                           op0=ALU.mult, op1=ALU.add),
                         reads=[uk, 'negb', 'ub4'], writes=[('vnew', u)])
                    P.op('pe', lambda e, u=u: e.matmul(U[u][:, 1, :], lhsT=qg4[:, u, :], rhs=Sbf[:, u, :], start=True, stop=False),
                         reads=['qg4', sk, 'Sbf'], writes=[uk])
                    P.op('pe', lambda e, u=u, r=r: e.matmul(U[u][:, 1, :], lhsT=aT4[r, u, :], rhs=vnew[r, u, :], start=False, stop=True),
                         reads=['aT4', ('vnew', u)], writes=[uk])
                    P.op('pe', lambda e, u=u, r=r: e.matmul(U[u][:, 2, :], lhsT=kdec4[r, u, :], rhs=vnew[r, u, :], start=True, stop=True),
                         reads=['kdec4', ('vnew', u)], writes=[uk])
                    hh = u % 2
                    ok = ('Oacc', pair[u], hh)
                    if p < NPAIR // 2:
                        P.op('act', lambda e, u=u, r=r, hh=hh, pu=pair[u]: e.copy(out=Oacc[r, pu, hh, :], in_=U[u][r, 1, :]),
                             reads=[uk], writes=[ok])
                    else:
                        P.op('dve', lambda e, u=u, r=r, hh=hh, pu=pair[u]: e.tensor_tensor(out=Oacc[r, pu, hh, :], in0=U[u][r, 1, :],
                                                                                           in1=Oacc[r, pu, hh, :], op=ALU.add),
                             reads=[uk, ok], writes=[ok])
                    P.op('dve', lambda e, u=u, col=col: e.scalar_tensor_tensor(out=Sf[:, u, :], in0=Sf[:, u, :],
                                                                               scalar=EROW[:, u, col:col + 1], in1=U[u][:, 2, :],
                                                                               op0=ALU.mult, op1=ALU.add),
                         reads=[uk, 'EROW', ('Sf', u), 'Sf'], writes=[('Sf', u)])
                    P.op('act', lambda e, u=u: e.copy(out=Sbf[:, u, :], in_=Sf[:, u, :]), reads=[('Sf', u), 'Sf'], writes=[sk])
        for hh in (range(2) if DN_LVL >= 6 else []):
            for T in range(NT):
                P.op('sp', lambda e, hh=hh, T=T: e.dma_start(out=sdz[:, :], in_=scr['SDZ'][hh, :, T * 512:(T + 1) * 512]),
                     reads=[('SDZ', hh)], writes=['sdz'], dma=True)
                for j in range(4):
                    pp = T * 4 + j
                    ok = ('Oacc', pp, hh)
                    P.op('act', lambda e, pp=pp, hh=hh: e.activation(out=fsq[:, :], in_=Oacc[:, pp, hh, :], func=AF.Square,
                                                                     accum_out=fstat[:, 0:1]), reads=[ok], writes=['fsq', 'fstat'])
                    emit_rsqrt(P, fstat[:, 2:3], fstat[:, 0:1], 1.0 / 128, ['fstat'], ['fstat'])
                    P.op('dve', lambda e, pp=pp, hh=hh: e.scalar_tensor_tensor(out=fon[:, :], in0=Oacc[:, pp, hh, :],
                                                                               scalar=fstat[:, 2:3], in1=dnw[:, :],
                                                                               op0=ALU.mult, op1=ALU.mult),
                         reads=[ok, 'fstat', 'dnw'], writes=['fon'])
                    P.op('pe', lambda e: e.transpose(out=B0[:, 0, :], in_=fon[:, :], identity=ident[:, :]),
                         reads=['fon', 'c_ident'], writes=['B0'])
                    P.op('dve', lambda e, j=j: e.tensor_tensor(out=gout[:, j * 128:(j + 1) * 128], in0=B0[:, 0, :],
                                                               in1=sdz[:, j * 128:(j + 1) * 128], op=ALU.mult),
                         reads=['B0', 'sdz'], writes=['gout'])
                P.op('sp', lambda e, hh=hh, T=T: e.dma_start(out=scr['GT'][256 + hh * 128:256 + (hh + 1) * 128, T * 512:(T + 1) * 512],
                                                             in_=gout[:, :]), reads=['gout'], writes=['GT'], dma=True)
        P.barrier()


def phase_tail(nc, P, io, gsrc, g_reads, gall=None):
    with contextlib.ExitStack() as st:
        sb = lambda name, shape, dt: st.enter_context(nc.sbuf_tensor(name, shape, dt))
        ps = lambda name, shape, dt: st.enter_context(nc.psum_tensor(name, shape, dt))
        Wn = {}
        for nm in ('w_ba', 'w_bd', 'w_ga', 'w_gd', 'w_out', 'w_pg'):
            Wn[nm] = sb("e_" + nm, [128, 8, D], BF16)
        Wpp = sb("e_wpp", [128, 2, D], BF16)
        stg = [sb("e_stg%d" % i, [128, D], F32) for i in range(2)]
        wb = sb("e_wb", [128, D], F32)
        wpost = sb("e_wpost", [128, D], F32)
        wple = sb("e_wple", [128, D], F32)
        ident = sb("e_ident", [128, 128], BF16)
        xt = [sb("e_xt%d" % i, [128, D], F32) for i in range(2)]
        xn = [sb("e_xn%d" % i, [128, D], BF16) for i in range(2)]
        stat = [sb("e_stat%d" % i, [128, 4], F32) for i in range(2)]
        st2 = sb("e_st2", [128, 8], F32)
        hT = sb("e_hT", [128, 8, 512], BF16)
        GaT = sb("e_GaT", [128, 8, 512], BF16)
        GdT = sb("e_GdT", [128, 8, 512], BF16)
        mixT = sb("e_mixT", [128, 8, 512], BF16)
        sa = sb("e_sa", [128, 512], F32)
        sd = sb("e_sd", [128, 512], F32)
        m1 = sb("e_m1", [128, 512], F32)
        m2 = sb("e_m2", [128, 512], F32)
        tmp = sb("e_tmp", [128, D], F32)
        x1 = sb("e_x1", [128, D], F32)
        x1b = sb("e_x1b", [128, D], BF16)
        x1T = sb("e_x1T", [128, 8, 128], BF16)
        pt = sb("e_pt", [128, 256], F32)
        ptb = sb("e_ptb", [128, 256], BF16)
        pT = sb("e_pT", [128, 2, 128], BF16)
        s2 = sb("e_s2", [128, D], F32)
        sq = sb("e_sq", [128, D], BF16)
        E = [ps("e_E%d" % i, [128, 512], F32) for i in range(4)]
        E45 = ps("e_E45", [128, 2, 512], F32)
        ps_tp = [ps("e_ptp%d" % i, [128, D], BF16) for i in range(2)]
        P.psum_keys(('E', 0), ('E', 1), ('E', 2), ('E', 3), 'E45', ('ps_tp', 0), ('ps_tp', 1))
        if gall is not None:
            selm = sb("e_selm", [128, 8, 128], BF16)
            cbuf = [sb("e_cb%d" % i, [128, 512], BF16) for i in range(6)]
            P.op('sp', lambda e: e.dma_start(out=selm[:, :, :], in_=io['selm'][:, :, :]), writes=['selm'], dma=True)
        ncb = [0]

        def ldb(t, src, key):
            P.op('sp', lambda e: e.dma_start(out=t, in_=src), writes=[key], dma=True)
        ldb(wb[:, :], io['norm_pre'][0:1, :].partition_broadcast(128), 'wb')
        ldb(wpost[:, :], io['norm_post'][0:1, :].partition_broadcast(128), 'wpost')
        ldb(wple[:, :], io['ple_norm'][0:1, :].partition_broadcast(128), 'wple')
        ldb(ident[:, :], io['ident'][:, :], 'ident')
        for nm in Wn:
            load_cast(P, lambda c, nm=nm: Wn[nm][:, c, :], lambda c, nm=nm: io[nm][c * 128:(c + 1) * 128, :], stg, 8, nm, D)
        load_cast(P, lambda c: Wpp[:, c, :], lambda c: io['w_pp'][c * 128:(c + 1) * 128, :], stg, 2, 'w_pp', D)

        for T in range(TOK2 // 512):
            t0 = T * 512
            for tb in range(4):
                emit_rms_hT(P, io['x2'][t0 + tb * 128:t0 + (tb + 1) * 128, :], wb, ident, xt, xn, hT, ps_tp, stat, tb, 'hT')
            for kt in range(8):
                if gall is None:
                    P.op('sp', lambda e, kt=kt, t0=t0: e.dma_start(out=GaT[:, kt, :], in_=gsrc(0, kt, t0)),
                         reads=g_reads, writes=['GaT'], dma=True)
                    P.op('sp', lambda e, kt=kt, t0=t0: e.dma_start(out=GdT[:, kt, :], in_=gsrc(1, kt, t0)),
                         reads=g_reads, writes=['GdT'], dma=True)
                    continue
                for kind, dst, dkey in ((0, GaT, 'GaT'), (1, GdT, 'GdT')):
                    bank = (kt * 2 + kind) % 4
                    for cand in range(8):
                        bb, qq = cand // 4, cand % 4
                        row0 = (bb * 4 + kt // 2) * 512 + kind * 256 + (kt % 2) * 128
                        col0 = qq * TOK2 + t0
                        ci = ncb[0] % 6
                        ncb[0] += 1
                        P.op('sp', lambda e, ci=ci, row0=row0, col0=col0: e.dma_start(
                            out=cbuf[ci][:, :], in_=gall[row0:row0 + 128, col0:col0 + 512]),
                            reads=['GALL'], writes=[('cb', ci)], dma=True)
                        P.op('pe', lambda e, ci=ci, cand=cand, bank=bank: e.matmul(
                            E[bank][:, :], lhsT=selm[:, cand, :], rhs=cbuf[ci][:, :], start=(cand == 0), stop=(cand == 7)),
                            reads=['selm', ('cb', ci)], writes=[('E', bank)])
                    if kind == 0:
                        P.op('act', lambda e, kt=kt, bank=bank, dst=dst: e.copy(out=dst[:, kt, :], in_=E[bank][:, :]),
                             reads=[('E', bank)], writes=[dkey])
                    else:
                        P.op('dve', lambda e, kt=kt, bank=bank, dst=dst: e.tensor_copy(out=dst[:, kt, :], in_=E[bank][:, :]),
                             reads=[('E', bank)], writes=[dkey])
            for fo in range(8):
                fs = slice(fo * 128, (fo + 1) * 128)
                for (bank, wn, src, skey) in ((0, 'w_ba', GaT, 'GaT'), (1, 'w_ga', hT, 'hT'), (2, 'w_bd', GdT, 'GdT'), (3, 'w_gd', hT, 'hT')):
                    for k in range(8):
                        P.op('pe', lambda e, bank=bank, wn=wn, src=src, k=k, fs=fs: e.matmul(
                            E[bank][:, :], lhsT=Wn[wn][:, k, fs], rhs=src[:, k, :], start=(k == 0), stop=(k == 7)),
                            reads=[wn, skey], writes=[('E', bank)])
                P.op('act', lambda e: e.activation(out=sa[:, :], in_=E[1][:, :], func=AF.Sigmoid), reads=[('E', 1)], writes=['sa'])
                P.op('act', lambda e: e.activation(out=sd[:, :], in_=E[3][:, :], func=AF.Sigmoid), reads=[('E', 3)], writes=['sd'])
                P.op('dve', lambda e: e.tensor_tensor(out=m1[:, :], in0=E[0][:, :], in1=sa[:, :], op=ALU.mult),
                     reads=[('E', 0), 'sa'], writes=['m1'])
                P.op('dve', lambda e: e.tensor_tensor(out=m2[:, :], in0=E[2][:, :], in1=sd[:, :], op=ALU.mult),
                     reads=[('E', 2), 'sd'], writes=['m2'])
                P.op('pool', lambda e, fo=fo: e.tensor_tensor(out=mixT[:, fo, :], in0=m1[:, :], in1=m2[:, :], op=ALU.add),
                     reads=['m1', 'm2'], writes=['mixT'])
            for tb in range(4):
                ts_ = slice(tb * 128, (tb + 1) * 128)
                r0 = t0 + tb * 128
                for half in range(2):
                    for k in range(8):
                        P.op('pe', lambda e, half=half, k=k, ts_=ts_: e.matmul(
                            E45[:, half, :], lhsT=mixT[:, k, ts_], rhs=Wn['w_out'][:, k, half * 512:(half + 1) * 512],
                            start=(k == 0), stop=(k == 7)), reads=['mixT', 'w_out'], writes=['E45'])
                for half in range(2):
                    P.op('act', lambda e, half=half: e.activation(out=sq[:, half * 512:(half + 1) * 512], in_=E45[:, half, :],
                                                                  func=AF.Square, accum_out=st2[:, half:half + 1]),
                         reads=['E45'], writes=['sq', 'st2'])
                P.op('dve', lambda e: e.tensor_tensor(out=st2[:, 2:3], in0=st2[:, 0:1], in1=st2[:, 1:2], op=ALU.add),
                     reads=['st2'], writes=['st2'])
                emit_rsqrt(P, st2[:, 4:5], st2[:, 2:3], 1.0 / D, ['st2'], ['st2'])
                P.op('sp', lambda e, r0=r0: e.dma_start(out=x1[:, :], in_=io['x2'][r0:r0 + 128, :]), writes=['x1'], dma=True)
                for half in range(2):
                    hs = slice(half * 512, (half + 1) * 512)
                    P.op('dve', lambda e, half=half, hs=hs: e.scalar_tensor_tensor(out=tmp[:, hs], in0=E45[:, half, :],
                                                                                   scalar=st2[:, 4:5], in1=wpost[:, hs],
                                                                                   op0=ALU.mult, op1=ALU.mult),
                         reads=['E45', 'st2', 'wpost'], writes=['tmp'])
                P.op('pool', lambda e: e.tensor_tensor(out=x1[:, :], in0=x1[:, :], in1=tmp[:, :], op=ALU.add),
                     reads=['x1', 'tmp'], writes=['x1'])
                P.op('pool', lambda e: e.tensor_copy(out=x1b[:, :], in_=x1[:, :]), reads=['x1'], writes=['x1b'])
                ptp = ps_tp[tb % 2]
                pk = ('ps_tp', tb % 2)
                for k in range(8):
                    P.op('pe', lambda e, k=k, ptp=ptp: e.transpose(out=ptp[:, k * 128:(k + 1) * 128],
                                                                   in_=x1b[:, k * 128:(k + 1) * 128], identity=ident[:, :]),
                         reads=['x1b', 'ident'], writes=[pk])
                P.op('act', lambda e, ptp=ptp: e.copy(out=x1T[:, :, :], in_=ptp[:, :].rearrange("p (k t) -> p k t", k=8)),
                     reads=[pk], writes=['x1T'])
                P.op('sp', lambda e, r0=r0: e.dma_start(out=pt[:, :], in_=io['p2'][r0:r0 + 128, :]), writes=['pt'], dma=True)
                P.op('dve', lambda e: e.tensor_copy(out=ptb[:, :], in_=pt[:, :]), reads=['pt'], writes=['ptb'])
                ptp2 = ps_tp[(tb + 1) % 2]
                pk2 = ('ps_tp', (tb + 1) % 2)
                for k in range(2):
                    P.op('pe', lambda e, k=k, ptp2=ptp2: e.transpose(out=ptp2[:, k * 128:(k + 1) * 128],
                                                                     in_=ptb[:, k * 128:(k + 1) * 128], identity=ident[:, :]),
                         reads=['ptb', 'ident'], writes=[pk2])
                P.op('act', lambda e, ptp2=ptp2: e.copy(out=pT[:, :, :], in_=ptp2[:, 0:256].rearrange("p (k t) -> p k t", k=2)),
                     reads=[pk2], writes=['pT'])
                for half in range(2):
                    hs = slice(half * 512, (half + 1) * 512)
                    for k in range(8):
                        P.op('pe', lambda e, half=half, k=k, hs=hs: e.matmul(E45[:, half, :], lhsT=x1T[:, k, :], rhs=Wn['w_pg'][:, k, hs],
                                                                             start=(k == 0), stop=(k == 7)),
                             reads=['x1T', 'w_pg'], writes=['E45'])
                    for k in range(2):
                        P.op('pe', lambda e, half=half, k=k, hs=hs: e.matmul(E[half][:, :], lhsT=pT[:, k, :], rhs=Wpp[:, k, hs],
                                                                             start=(k == 0), stop=(k == 1)),
                             reads=['pT', 'w_pp'], writes=[('E', half)])
                for half in range(2):
                    hs = slice(half * 512, (half + 1) * 512)
                    P.op('act', lambda e, half=half, hs=hs: e.activation(out=s2[:, hs], in_=E45[:, half, :], func=AF.Sigmoid),
                         reads=['E45'], writes=['s2'])
                    P.op('dve', lambda e, half=half, hs=hs: e.tensor_tensor(out=s2[:, hs], in0=E[half][:, :], in1=s2[:, hs], op=ALU.mult),
                         reads=[('E', half), 's2'], writes=['s2'])
                P.op('act', lambda e: e.activation(out=sq[:, :], in_=s2[:, :], func=AF.Square, accum_out=st2[:, 5:6]),
                     reads=['s2'], writes=['sq', 'st2b'])
                emit_rsqrt(P, st2[:, 7:8], st2[:, 5:6], 1.0 / D, ['st2b'], ['st2b'])
                P.op('dve', lambda e: e.scalar_tensor_tensor(out=tmp[:, :], in0=s2[:, :], scalar=st2[:, 7:8], in1=wple[:, :],
                                                             op0=ALU.mult, op1=ALU.mult), reads=['s2', 'st2b', 'wple'], writes=['tmp'])
                P.op('pool', lambda e: e.tensor_tensor(out=tmp[:, :], in0=tmp[:, :], in1=x1[:, :], op=ALU.add),
                     reads=['tmp', 'x1'], writes=['tmp'])
                P.op('sp', lambda e, r0=r0: e.dma_start(out=io['out'][r0:r0 + 128, :], in_=tmp[:, :]), reads=['tmp'], writes=['out'], dma=True)
        P.barrier()


P1_INPUTS = [('x', [S, D], F32), ('w_in', [D, WCOLS], F32), ('norm_pre', [1, D], F32), ('nw', [128, 4], F32),
             ('cw', [128, 6, 5], F32), ('gcst', [1, 8], F32), ('cos', [128, S], F32), ('sin', [128, S], F32),
             ('ident', [128, 128], BF16), ('ones', [128, 128], BF16), ('TRI4', [128, 4, 128], F32),
             ('SM4', [128, 4, 128], F32), ('IM4', [128, 4, 128], F32), ('I4', [128, 4, 128], F32),
             ('ONESF', [128, 128], F32), ('BLK', [128, 128], F32), ('dn_norm', [1, 128], F32)]
P2_INPUTS = [('x2', [TOK2, D], F32), ('p2', [TOK2, 256], F32), ('norm_pre', [1, D], F32), ('norm_post', [1, D], F32),
             ('ple_norm', [1, D], F32), ('ident', [128, 128], BF16), ('w_ba', [D, D], F32), ('w_bd', [D, D], F32),
             ('w_ga', [D, D], F32), ('w_gd', [D, D], F32), ('w_out', [D, D], F32), ('w_pg', [D, D], F32),
             ('w_pp', [256, D], F32)]


DEBUG_SCR = False
NT_LIM = None
ROPE_LVL = 9
DN_STEPS = None
DN_LVL = 9
ROPE_VAR = 0
PARTS = ('rope', 'silu', 'conv', 'tm')
PHASES = ('proj', 'attn', 'dn')


def _scratch(nc):
    scr = {}
    for nm, shape, dt in (('QT', [2, 128, S], BF16), ('KT', [1, 128, S], BF16), ('V', [S, 128], BF16),
                          ('SAZ', [2, 128, S], BF16), ('SDZ', [2, 128, S], BF16), ('DQ', [2, 128, S], BF16),
                          ('DK', [2, 128, S], BF16), ('DV', [2, 128, S], BF16), ('GB', [S, 8], F32)):
        if DEBUG_SCR:
            scr[nm] = nc.dram_tensor("scr_" + nm, shape, dt, kind="ExternalOutput")
        else:
            scr[nm] = nc.dram_tensor("scr_" + nm, shape, dt)
    return scr


def build(mode):
    nc = bass.Bass("TRN2", target_bir_lowering=False)
    io = {}
    names = []
    if mode in ('p1', 'fused'):
        names += P1_INPUTS
    if mode in ('p2', 'fused'):
        names += [n for n in P2_INPUTS if n[0] not in [m[0] for m in names]]
    for nm, shape, dt in names:
        io[nm] = nc.dram_tensor(nm, shape, dt, kind="ExternalInput")
    with contextlib.ExitStack() as stack:
        P = Prog(nc, stack)
        if mode == 'p1':
            scr = _scratch(nc)
            scr['GT'] = nc.dram_tensor("GT", [512, S], BF16, kind="ExternalOutput")
            if 'proj' in PHASES:
                phase_proj(nc, P, io, scr)
            if 'attn' in PHASES:
                phase_attn(nc, P, io, scr)
            if 'dn' in PHASES:
                phase_dn(nc, P, io, scr)
        elif mode == 'fused':
            scr = _scratch(nc)
            scr['GT'] = nc.dram_tensor("GT_int", [512, S], BF16)
            gall = nc.dram_tensor("GALL_int", [8 * 512, S], BF16)
            io['selm'] = nc.dram_tensor("selm", [128, 8, 128], BF16, kind="ExternalInput")
            io['out'] = nc.dram_tensor("out", [TOK2, D], F32, kind="ExternalOutput")
            phase_proj(nc, P, io, scr)
            phase_attn(nc, P, io, scr)
            phase_dn(nc, P, io, scr)
            P.op('pool', lambda e: e.collective_compute("AllGather", ALU.bypass, replica_groups=[list(range(8))],
                                                        ins=[scr['GT'].ap().opt()], outs=[gall.ap().opt()]),
                 reads=['GT'], writes=['GALL'], dma=True, inc=1)
            phase_tail(nc, P, io, None, None, gall=gall)
        elif mode == 'p2':
            io['G2'] = nc.dram_tensor("G2", [16, 128, TOK2], BF16, kind="ExternalInput")
            io['out'] = nc.dram_tensor("out", [TOK2, D], F32, kind="ExternalOutput")
            phase_tail(nc, P, io, lambda kind, kt, t0: io['G2'][kind * 8 + kt, :, t0:t0 + 512], [])
        P.emit()
    return nc


def _consts():
    bf = ml_dtypes.bfloat16
    idx = np.arange(128)
    same = (idx[:, None] // 64) == (idx[None, :] // 64)
    k = idx[:, None]
    i = idx[None, :]
    tri_f = (same & (k <= i)).astype(np.float32)
    tri_b = (same & (k >= i)).astype(np.float32)
    smt_f = (same & (i > k)).astype(np.float32)
    smt_b = (same & (i < k)).astype(np.float32)
    imt_f = (same & (i >= k)).astype(np.float32)
    imt_b = (same & (i <= k)).astype(np.float32)
    eye = np.eye(128, dtype=np.float32)
    st4 = lambda a, b: np.ascontiguousarray(np.stack([a, a, b, b], axis=1))
    c = {
        'ident': np.eye(128, dtype=np.float32).astype(bf),
        'ones': np.ones((128, 128), np.float32).astype(bf),
        'TRI4': st4(tri_f, tri_b), 'SM4': st4(smt_f, smt_b), 'IM4': st4(imt_f, imt_b), 'I4': st4(eye, eye),
        'ONESF': np.ones((128, 128), np.float32), 'BLK': same.astype(np.float32),
    }
    t = np.arange(S)
    row = (t // 64).astype(np.float32)
    col = (t % 64).astype(np.float32)
    inv = (np.float32(10000.0) ** (-np.arange(32, dtype=np.float32) / np.float32(32))).astype(np.float32)
    ang = np.concatenate([row[:, None] * inv[None, :], col[:, None] * inv[None, :]], axis=1).astype(np.float32)
    cosd = np.repeat(np.cos(ang.astype(np.float64)), 2, axis=1).T.astype(np.float32)
    sind = np.repeat(np.sin(ang.astype(np.float64)), 2, axis=1).T.astype(np.float32)
    sign = np.where(np.arange(128) % 2 == 0, -1.0, 1.0).astype(np.float32)[:, None]
    c['cos'] = np.ascontiguousarray(cosd)
    c['sin'] = np.ascontiguousarray(sind * sign)
    return c


def _p1_inputs(c, inputs, consts):
    b, j = c // 4, c % 4
    w_in = inputs['w_in'][0]
    o = np.cumsum([0, 1024, 256, 256, 1024, 1024, 1024, 1024, 16, 16, 1024, 1024, 1024])
    aq, ak, av, az, dq, dk, dv, db, da, dz = [w_in[:, o[i]:o[i + 1]] for i in range(10)]
    kv = j // 2
    sw = np.arange(128) ^ 1
    hs = [2 * j, 2 * j + 1]
    col = lambda m, h: m[:, h * 128:(h + 1) * 128]
    groups = [col(aq, hs[0]), col(aq, hs[1]), col(aq, hs[0])[:, sw], col(aq, hs[1])[:, sw],
              col(ak, kv), col(ak, kv)[:, sw], col(az, hs[0]), col(az, hs[1]),
              col(dq, hs[0]), col(dq, hs[1]), col(dk, hs[0]), col(dk, hs[1]), col(dv, hs[0]), col(dv, hs[1]),
              col(dz, hs[0]), col(dz, hs[1]), col(av, kv)]
    sel = [0 * 8 + hs[0], 0 * 8 + hs[1], 1 * 8 + hs[0], 1 * 8 + hs[1]]
    groups += [db[:, sel], da[:, sel]]
    wsl = np.ascontiguousarray(np.concatenate(groups, axis=1))
    qn, kn = inputs['q_norm'][0], inputs['k_norm'][0]
    nw = np.stack([qn, qn[sw], kn, kn[sw]], axis=1).astype(np.float32)
    conv = inputs['conv_w'][0]
    cg = []
    for base in (0, 1024, 2048):
        for h in hs:
            cg.append(conv[:, base + h * 128: base + (h + 1) * 128].T)
    cw = np.ascontiguousarray(np.stack(cg, axis=1)).astype(np.float32)
    dtb = inputs['dt_bias'][0].reshape(16)[sel]
    alog = inputs['a_log'][0].reshape(16)[sel]
    gcst = np.concatenate([dtb, alog])[None, :].astype(np.float32)
    d = {'x': np.ascontiguousarray(inputs['x'][b]), 'w_in': wsl, 'norm_pre': inputs['norm_pre'], 'nw': np.ascontiguousarray(nw),
         'cw': cw, 'gcst': gcst, 'dn_norm': inputs['dn_norm']}
    for k_ in ('cos', 'sin', 'ident', 'ones', 'TRI4', 'SM4', 'IM4', 'I4', 'ONESF', 'BLK'):
        d[k_] = consts[k_]
    return d


def _p2_inputs(c, inputs, consts):
    b, q = c // 4, c % 4
    w_in = inputs['w_in'][0]
    tok = slice(q * TOK2, (q + 1) * TOK2)
    return {'x2': np.ascontiguousarray(inputs['x'][b, tok]), 'p2': np.ascontiguousarray(inputs['p'][0, b, tok]),
            'norm_pre': inputs['norm_pre'], 'norm_post': inputs['norm_post'], 'ple_norm': inputs['ple_norm'],
            'ident': consts['ident'], 'w_ba': inputs['w_br_att'][0], 'w_bd': inputs['w_br_dn'][0],
            'w_ga': np.ascontiguousarray(w_in[:, 6688:7712]), 'w_gd': np.ascontiguousarray(w_in[:, 7712:8736]),
            'w_out': inputs['w_out'][0], 'w_pg': inputs['w_ple_gate'][0], 'w_pp': inputs['w_ple_proj'][0]}


FUSED = True


def kernel(**inputs):
    inputs = {k: np.asarray(v) for k, v in inputs.items()}
    consts = _consts()
    if FUSED:
        maps = []
        eye = np.eye(128, dtype=np.float32)
        for c in range(8):
            d = _p1_inputs(c, inputs, consts)
            d.update(_p2_inputs(c, inputs, consts))
            selm = np.zeros((128, 8, 128), np.float32)
            selm[:, c, :] = eye
            d['selm'] = selm.astype(ml_dtypes.bfloat16)
            maps.append(d)
        nc = build('fused')
        r = run_bass_kernel_spmd(nc, maps, core_ids=list(range(8)))
        out = np.zeros((2, S, D), np.float32)
        for c in range(8):
            b, q = c // 4, c % 4
            out[b, q * TOK2:(q + 1) * TOK2] = np.asarray(r.results[c]['out'])
        return out
    nc1 = build('p1')
    r1 = run_bass_kernel_spmd(nc1, [_p1_inputs(c, inputs, consts) for c in range(8)], core_ids=list(range(8)))
    GT = [np.asarray(r1.results[c]['GT']) for c in range(8)]
    in2 = []
    for c in range(8):
        b, q = c // 4, c % 4
        d = _p2_inputs(c, inputs, consts)
        tok = slice(q * TOK2, (q + 1) * TOK2)
        tiles = []
        for kind in range(2):
            for kt in range(8):
                src = GT[b * 4 + kt // 2]
                r0 = kind * 256 + (kt % 2) * 128
                tiles.append(src[r0:r0 + 128, tok])
        d['G2'] = np.ascontiguousarray(np.stack(tiles, axis=0))
        in2.append(d)
    nc2 = build('p2')
    r2 = run_bass_kernel_spmd(nc2, in2, core_ids=list(range(8)))
    out = np.zeros((2, S, D), np.float32)
    for c in range(8):
        b, q = c // 4, c % 4
        out[b, q * TOK2:(q + 1) * TOK2] = np.asarray(r2.results[c]['out'])
    return out
```

```python
import contextlib
import numpy as np
import ml_dtypes
import concourse.bass as bass
import concourse.mybir as mybir
from concourse.bass_utils import run_bass_kernel_spmd

F32 = mybir.dt.float32
BF16 = mybir.dt.bfloat16
I32 = mybir.dt.int32
AF = mybir.ActivationFunctionType
ALU = mybir.AluOpType

S = 8192
D = 1024
NT = S // 512
NPAIR = S // 128
EPS = 1e-6
NFM = 16
NTM = 136
WCOLS = NFM * 128 + NTM
TOK2 = 2048
NDMA_SEMS = 40


class Prog:
    NGEN = 6

    def __init__(self, nc, stack):
        self.nc = nc
        self.eng = {'pe': nc.tensor, 'act': nc.scalar, 'dve': nc.vector, 'pool': nc.gpsimd, 'sp': nc.sync}
        self.lists = {e: [] for e in self.eng}
        self.sems = {}
        for g in range(self.NGEN):
            for e in ('pe', 'act', 'dve', 'pool'):
                self.sems[('c', e, g)] = stack.enter_context(nc.semaphore("c_%s_%d" % (e, g)))
        self.dma = []
        for i in range(NDMA_SEMS):
            self.sems[('d', i)] = stack.enter_context(nc.semaphore("d_%d" % i))
            self.dma.append(0)
        self.pools = {'sp': list(range(0, 28)), 'pool': list(range(28, NDMA_SEMS)), 'act': list(range(28, NDMA_SEMS))}
        self.rr = {'sp': 0, 'pool': 0, 'act': 0}
        self.gen = 0
        self.cnt = {e: 0 for e in self.eng}
        self.seen = {e: {} for e in self.eng}
        self.lastw = {}
        self.readers = {}
        self.excl = set()

    def psum_keys(self, *keys):
        self.excl.update(keys)

    @staticmethod
    def _owner(tok):
        s = tok[0]
        return s[1] if s[0] == 'c' else None

    def _need(self, engine, tok, waits):
        s, v = tok
        if s[0] == 'c' and s[2] < self.gen:
            return
        if self.seen[engine].get(s, 0) < v:
            self.seen[engine][s] = v
            waits.append((s, v))

    def op(self, engine, fn, reads=(), writes=(), dma=False, inc=16):
        waits = []
        same_sync = engine in ('act', 'dve', 'pool')
        for k in list(reads) + list(writes):
            t = self.lastw.get(k)
            if t is not None and (self._owner(t) != engine or same_sync):
                self._need(engine, t, waits)
        for k in list(writes) + [k for k in reads if k in self.excl]:
            for t in self.readers.get(k, ()):
                if self._owner(t) != engine:
                    self._need(engine, t, waits)
        if dma:
            pl = self.pools[engine]
            i = pl[self.rr[engine] % len(pl)]
            self.rr[engine] += 1
            s = ('d', i)
            if self.dma[i] > 0:
                self._need(engine, (s, self.dma[i]), waits)
            self.dma[i] += inc
            tok = (s, self.dma[i])
            incv = inc
        else:
            self.cnt[engine] += 1
            s = ('c', engine, self.gen)
            tok = (s, self.cnt[engine])
            incv = 1
        self.lists[engine].append((waits, fn, s, incv))
        for k in writes:
            self.lastw[k] = tok
            self.readers[k] = []
        for k in reads:
            self.readers.setdefault(k, []).append(tok)
        return tok

    def barrier(self):
        toks = [(('c', e, self.gen), self.cnt[e]) for e in ('pe', 'act', 'dve', 'pool') if self.cnt[e] > 0]
        toks += [(('d', i), v) for i, v in enumerate(self.dma) if v > 0]
        for e in self.eng:
            waits = []
            for t in toks:
                if self._owner(t) != e:
                    self._need(e, t, waits)
            if waits:
                self.lists[e].append((waits, None, None, 0))
        self.gen += 1
        assert self.gen < self.NGEN
        for e in self.cnt:
            self.cnt[e] = 0

    def emit(self):
        nc = self.nc
        with nc.Block() as block:
            def run(ename):
                def body(eng):
                    for waits, fn, s, incv in self.lists[ename]:
                        for (ws, wv) in waits:
                            eng.wait_ge(self.sems[ws], wv)
                        if fn is not None:
                            fn(eng).then_inc(self.sems[s], incv)
                return body
            block.tensor(run('pe'))
            block.scalar(run('act'))
            block.vector(run('dve'))
            block.gpsimd(run('pool'))
            block.sync(run('sp'))


def emit_rsqrt(P, out_ap, in_ap, scale, reads, writes):
    P.op('act', lambda e: e.activation(out=out_ap, in_=in_ap, func=AF.Ln, bias=EPS, scale=scale), reads=reads, writes=writes)
    P.op('act', lambda e: e.activation(out=out_ap, in_=out_ap, func=AF.Exp, scale=-0.5), reads=writes, writes=writes)


def emit_rms_hT(P, x_src, wb, ident, xt, xn, hT, ps_tp, stat, tb, key):
    xs = xt[tb % len(xt)]
    xk = ('xt', tb % len(xt))
    P.op('sp', lambda e: e.dma_start(out=xs[:, :], in_=x_src), writes=[xk], dma=True)
    sq = xn[tb % 2]
    nk = ('xn', tb % 2)
    st = stat[tb % 2]
    sk = ('stat', tb % 2)
    P.op('act', lambda e: e.activation(out=sq[:, :], in_=xs[:, :], func=AF.Square, accum_out=st[:, 0:1]),
         reads=[xk], writes=[nk, sk])
    emit_rsqrt(P, st[:, 2:3], st[:, 0:1], 1.0 / D, [sk], [sk])
    P.op('dve', lambda e: e.scalar_tensor_tensor(out=sq[:, :], in0=xs[:, :], scalar=st[:, 2:3], in1=wb[:, :],
                                                 op0=ALU.mult, op1=ALU.mult), reads=[xk, sk, 'wb'], writes=[nk])
    pk = ('ps_tp', tb % 2)
    pt = ps_tp[tb % 2]
    for k in range(8):
        P.op('pe', lambda e, k=k: e.transpose(out=pt[:, k * 128:(k + 1) * 128], in_=sq[:, k * 128:(k + 1) * 128],
                                              identity=ident[:, :]), reads=[nk, 'ident'], writes=[pk])
    P.op('act', lambda e: e.copy(out=hT[:, :, tb * 128:(tb + 1) * 128],
                                 in_=pt[:, :].rearrange("p (k t) -> p k t", k=8)), reads=[pk], writes=[key])


def load_cast(P, dst_ap_fn, src_ap_fn, stg, nchunks, dkey, width):
    for c in range(nchunks):
        sl = c % 2
        P.op('sp', lambda e, c=c, sl=sl: e.dma_start(out=stg[sl][:, 0:width], in_=src_ap_fn(c)),
             writes=[('stg', sl)], dma=True)
        eng = 'dve' if c % 2 == 0 else 'pool'
        P.op(eng, lambda e, c=c, sl=sl: e.tensor_copy(out=dst_ap_fn(c), in_=stg[sl][:, 0:width]),
             reads=[('stg', sl)], writes=[dkey])


def phase_proj(nc, P, io, scr):
    with contextlib.ExitStack() as st:
        sb = lambda name, shape, dt: st.enter_context(nc.sbuf_tensor(name, shape, dt))
        ps = lambda name, shape, dt: st.enter_context(nc.psum_tensor(name, shape, dt))
        W = sb("a_W", [128, 8, WCOLS], BF16)
        stg = [sb("a_stg%d" % i, [128, WCOLS], F32) for i in range(2)]
        wb = sb("a_wb", [128, D], F32)
        ident = sb("a_ident", [128, 128], BF16)
        ones = sb("a_ones", [128, 128], BF16)
        xt = [sb("a_xt%d" % i, [128, D], F32) for i in range(4)]
        xn = [sb("a_xn%d" % i, [128, D], BF16) for i in range(2)]
        stat = [sb("a_stat%d" % i, [128, 4], F32) for i in range(2)]
        hT = [sb("a_hT%d" % i, [128, 8, 512], BF16) for i in range(2)]
        cs = [sb("a_cs%d" % i, [128, 2, 512], F32) for i in range(2)]
        nw = sb("a_nw", [128, 4], F32)
        cw = sb("a_cw", [128, 6, 5], F32)
        cstg = [sb("a_cstg%d" % g, [128, 520], F32) for g in range(6)]
        cacc = [sb("a_cacc%d" % i, [128, 512], F32) for i in range(2)]
        csil = [sb("a_csil%d" % i, [128, 512], F32) for i in range(2)]
        sqb = [sb("a_sqb%d" % i, [128, 512], BF16) for i in range(2)]
        rstd = [sb("a_rstd%d" % i, [128, 512], F32) for i in range(2)]
        t1 = [sb("a_t1%d" % i, [128, 512], F32) for i in range(2)]
        t2 = [sb("a_t2%d" % i, [128, 512], F32) for i in range(2)]
        ob = [sb("a_ob%d" % i, [128, 512], BF16) for i in range(4)]
        vtm = [sb("a_vtm%d" % i, [128, 128], BF16) for i in range(2)]
        gsm = [sb("a_gsm%d" % i, [128, 32], F32) for i in range(2)]
        cst = sb("a_cst", [128, 8], F32)
        ps_tp = [ps("a_ptp%d" % i, [128, D], BF16) for i in range(2)]
        ps_fm = [ps("a_pfm%d" % i, [128, 512], F32) for i in range(4)]
        ps_ss = ps("a_pss", [128, 512], F32)
        ps_tm = ps("a_ptm", [128, 512], F32)
        P.psum_keys(('pfm', 0), ('pfm', 1), ('pfm', 2), ('pfm', 3), 'pss', 'ptm', ('ps_tp', 0), ('ps_tp', 1))

        P.op('sp', lambda e: e.dma_start(out=wb[:, :], in_=io['norm_pre'][0:1, :].partition_broadcast(128)),
             writes=['wb'], dma=True)
        P.op('sp', lambda e: e.dma_start(out=ident[:, :], in_=io['ident'][:, :]), writes=['ident'], dma=True)
        P.op('sp', lambda e: e.dma_start(out=ones[:, :], in_=io['ones'][:, :]), writes=['ones'], dma=True)
        P.op('sp', lambda e: e.dma_start(out=nw[:, :], in_=io['nw'][:, :]), writes=['nw'], dma=True)
        P.op('sp', lambda e: e.dma_start(out=cw[:, :, :], in_=io['cw'][:, :, :]), writes=['cw'], dma=True)
        P.op('sp', lambda e: e.dma_start(out=cst[:, :], in_=io['gcst'][0:1, :].partition_broadcast(128)),
             writes=['cst'], dma=True)
        P.op('act', lambda e: e.activation(out=cst[:, 4:8], in_=cst[:, 4:8], func=AF.Exp), reads=['cst'], writes=['cst'])
        P.op('dve', lambda e: e.tensor_scalar(out=cst[:, 4:8], in0=cst[:, 4:8], scalar1=-1.0, scalar2=None,
                                              op0=ALU.mult), reads=['cst'], writes=['cst'])
        for g in range(6):
            P.op('pool', lambda e, g=g: e.memset(cstg[g][:, :], 0.0), writes=[('cstg', g)])
        load_cast(P, lambda c: W[:, c, :], lambda c: io['w_in'][c * 128:(c + 1) * 128, :], stg, 8, 'W', WCOLS)

        rope_pairs = [(0, 2, 0, ('QT', 0)), (1, 3, 0, ('QT', 1)), (4, 5, 2, ('KT', 0))]
        silu_groups = [(6, ('SAZ', 0)), (7, ('SAZ', 1)), (14, ('SDZ', 0)), (15, ('SDZ', 1))]
        conv_groups = [(8, 'DQ', 0), (9, 'DQ', 1), (10, 'DK', 0), (11, 'DK', 1), (12, 'DV', 0), (13, 'DV', 1)]
        cnt = {'fm': 0, 'x': 0, 'ob': 0}

        def fm_matmul(g, h):
            slot = cnt['fm'] % 4
            cnt['fm'] += 1
            for k in range(8):
                P.op('pe', lambda e, k=k, slot=slot: e.matmul(ps_fm[slot][:, :], lhsT=W[:, k, g * 128:(g + 1) * 128],
                                                              rhs=h[0][:, k, :], start=(k == 0), stop=(k == 7)),
                     reads=['W', h[1]], writes=[('pfm', slot)])
            return slot

        def next_ob():
            i = cnt['ob'] % 4
            cnt['ob'] += 1
            return i

        def sumsq_rstd(src_ap, srckey, scale, i2):
            P.op('act', lambda e: e.activation(out=sqb[i2][:, :], in_=src_ap, func=AF.Square),
                 reads=[srckey], writes=[('sqb', i2)])
            P.op('pe', lambda e: e.matmul(ps_ss[:, :], lhsT=ones[:, :], rhs=sqb[i2][:, :], start=True, stop=True),
                 reads=['ones', ('sqb', i2)], writes=['pss'])
            emit_rsqrt(P, rstd[i2][:, :], ps_ss[:, :], scale, ['pss'], [('rstd', i2)])

        def conv_post(g, name, hh, ncols, ocol0, tok0):
            gi = g - 8
            i2 = gi % 2
            ck = ('cstg', gi)
            acc = cacc[i2]
            ak = ('cacc', i2)
            P.op('pool', lambda e: e.tensor_scalar(out=acc[:, 0:ncols], in0=cstg[gi][:, 0:ncols], scalar1=cw[:, gi, 0:1],
                                                   scalar2=None, op0=ALU.mult), reads=[ck, 'cw'], writes=[ak])
            for k in range(1, 5):
                P.op('dve', lambda e, k=k: e.scalar_tensor_tensor(out=acc[:, 0:ncols], in0=cstg[gi][:, k:k + ncols],
                                                                   scalar=cw[:, gi, k:k + 1], in1=acc[:, 0:ncols],
                                                                   op0=ALU.mult, op1=ALU.add),
                     reads=[ck, 'cw', ak], writes=[ak])
            sil = csil[i2]
            sk = ('csil', i2)
            P.op('act', lambda e: e.activation(out=sil[:, 0:ncols], in_=acc[:, 0:ncols], func=AF.Silu),
                 reads=[ak], writes=[sk])
            oi = next_ob()
            ok = ('ob', oi)
            n = ncols - ocol0
            if name == 'DV':
                P.op('dve', lambda e: e.tensor_copy(out=ob[oi][:, 0:n], in_=sil[:, ocol0:ncols]), reads=[sk], writes=[ok])
            else:
                P.op('act', lambda e: e.activation(out=sqb[i2][:, 0:ncols], in_=sil[:, 0:ncols], func=AF.Square),
                     reads=[sk], writes=[('sqb', i2)])
                P.op('pe', lambda e: e.matmul(ps_ss[:, 0:ncols], lhsT=ones[:, :], rhs=sqb[i2][:, 0:ncols],
                                              start=True, stop=True), reads=['ones', ('sqb', i2)], writes=['pss'])
                emit_rsqrt(P, rstd[i2][:, 0:ncols], ps_ss[:, 0:ncols], 1.0, ['pss'], [('rstd', i2)])
                sc = (128.0 ** -0.5) if name == 'DQ' else 1.0
                P.op('dve', lambda e: e.scalar_tensor_tensor(out=ob[oi][:, 0:n], in0=sil[:, ocol0:ncols], scalar=sc,
                                                             in1=rstd[i2][:, ocol0:ncols], op0=ALU.mult, op1=ALU.mult),
                     reads=[sk, ('rstd', i2)], writes=[ok])
            P.op('pool', lambda e: e.dma_start(out=scr[name][hh, :, tok0:tok0 + n], in_=ob[oi][:, 0:n]),
                 reads=[ok], writes=[(name, hh)], dma=True)

        for T in range(NT if NT_LIM is None else NT_LIM):
            t0 = T * 512
            h = (hT[T % 2], ('hT', T % 2))
            for tb in range(4):
                emit_rms_hT(P, io['x'][t0 + tb * 128:t0 + (tb + 1) * 128, :], wb, ident, xt, xn, h[0], ps_tp, stat,
                            tb, h[1])
            c2 = cs[T % 2]
            ck2 = ('cs', T % 2)
            P.op('sp', lambda e, c2=c2, t0=t0: e.dma_start(out=c2[:, 0, :], in_=io['cos'][:, t0:t0 + 512]),
                 writes=[ck2], dma=True)
            P.op('sp', lambda e, c2=c2, t0=t0: e.dma_start(out=c2[:, 1, :], in_=io['sin'][:, t0:t0 + 512]),
                 writes=[ck2], dma=True)
            for (g, gs, wc, (dn, hh)) in (rope_pairs if 'rope' in PARTS else []):
                sa = fm_matmul(g, h)
                sbk = fm_matmul(gs, h)
                i2 = cnt['x'] % 2
                cnt['x'] += 1
                sumsq_rstd(ps_fm[sa][:, :], ('pfm', sa), 1.0 / 128, i2)
                if ROPE_LVL < 2:
                    continue
                if ROPE_VAR != 3:
                    P.op('dve', lambda e, sa=sa, i2=i2, wc=wc: e.tensor_scalar(
                        out=t1[i2][:, :], in0=ps_fm[sa][:, :], scalar1=(1.0 if ROPE_VAR == 1 else nw[:, wc:wc + 1]), scalar2=None, op0=ALU.mult),
                        reads=[('pfm', sa), 'nw'] + ([('sqb', i2)] if ROPE_VAR == 4 else []), writes=[('t1', i2)])
                if ROPE_VAR != 2:
                    P.op('dve', lambda e, sbk=sbk, i2=i2, wc=wc: e.tensor_scalar(
                        out=t2[i2][:, :], in0=ps_fm[sbk][:, :], scalar1=(1.0 if ROPE_VAR == 1 else nw[:, wc + 1:wc + 2]), scalar2=None, op0=ALU.mult),
                        reads=[('pfm', sbk), 'nw'], writes=[('t2', i2)])
                P.op('pool', lambda e, i2=i2, c2=c2: e.tensor_tensor(out=t1[i2][:, :], in0=t1[i2][:, :], in1=c2[:, 0, :],
                                                                     op=ALU.mult), reads=[('t1', i2), ck2], writes=[('t1', i2)])
                P.op('pool', lambda e, i2=i2, c2=c2: e.tensor_tensor(out=t2[i2][:, :], in0=t2[i2][:, :], in1=c2[:, 1, :],
                                                                     op=ALU.mult), reads=[('t2', i2), ck2], writes=[('t2', i2)])
                if ROPE_LVL < 3:
                    continue
                P.op('pool', lambda e, i2=i2: e.tensor_tensor(out=t1[i2][:, :], in0=t1[i2][:, :], in1=t2[i2][:, :],
                                                              op=ALU.add), reads=[('t1', i2), ('t2', i2)],
                     writes=[('t1', i2)])
                oi = next_ob()
                P.op('pool', lambda e, i2=i2, oi=oi: e.tensor_tensor(out=ob[oi][:, :], in0=t1[i2][:, :],
                                                                     in1=rstd[i2][:, :], op=ALU.mult),
                     reads=[('t1', i2), ('rstd', i2)], writes=[('ob', oi)])
                P.op('pool', lambda e, oi=oi, dn=dn, hh=hh, t0=t0: e.dma_start(out=scr[dn][hh, :, t0:t0 + 512],
                                                                             in_=ob[oi][:, :]),
                     reads=[('ob', oi)], writes=[(dn, hh)], dma=True)
            for (g, (dn, hh)) in (silu_groups if 'silu' in PARTS else []):
                sa = fm_matmul(g, h)
                oi = next_ob()
                P.op('act', lambda e, sa=sa, oi=oi: e.activation(out=ob[oi][:, :], in_=ps_fm[sa][:, :], func=AF.Silu),
                     reads=[('pfm', sa)], writes=[('ob', oi)])
                P.op('pool', lambda e, oi=oi, dn=dn, hh=hh, t0=t0: e.dma_start(out=scr[dn][hh, :, t0:t0 + 512],
                                                                             in_=ob[oi][:, :]),
                     reads=[('ob', oi)], writes=[(dn, hh)], dma=True)
            for (g, name, hh) in (conv_groups if 'conv' in PARTS else []):
                sa = fm_matmul(g, h)
                gi = g - 8
                P.op('act', lambda e, sa=sa, gi=gi: e.copy(out=cstg[gi][:, 4:516], in_=ps_fm[sa][:, :]),
                     reads=[('pfm', sa)], writes=[('cstg', gi)])
                if T == 0:
                    conv_post(g, name, hh, 512, 2, 0)
                else:
                    conv_post(g, name, hh, 512, 0, t0 - 2)
                P.op('pool', lambda e, gi=gi: e.tensor_copy(out=cstg[gi][:, 0:4], in_=cstg[gi][:, 512:516]),
                     reads=[('cstg', gi)], writes=[('cstg', gi)])
                if T == NT - 1:
                    P.op('pool', lambda e, gi=gi: e.tensor_copy(out=cstg[gi][:, 0:66], in_=cstg[gi][:, 450:516]),
                         reads=[('cstg', gi)], writes=[('cstg', gi)])
                    P.op('pool', lambda e, gi=gi: e.memset(cstg[gi][:, 66:72], 0.0), writes=[('cstg', gi)])
                    conv_post(g, name, hh, 64, 0, S - 64)
            for tb in (range(4) if 'tm' in PARTS else []):
                for k in range(8):
                    P.op('pe', lambda e, k=k, tb=tb, h=h: e.matmul(ps_tm[:, 0:NTM], lhsT=h[0][:, k, tb * 128:(tb + 1) * 128],
                                                                   rhs=W[:, k, NFM * 128:WCOLS], start=(k == 0), stop=(k == 7)),
                         reads=['W', h[1]], writes=['ptm'])
                i2 = tb % 2
                vk = ('vtm', i2)
                P.op('act', lambda e, i2=i2: e.copy(out=vtm[i2][:, :], in_=ps_tm[:, 0:128]), reads=['ptm'], writes=[vk])
                r0 = t0 + tb * 128
                P.op('pool', lambda e, i2=i2, r0=r0: e.dma_start(out=scr['V'][r0:r0 + 128, :], in_=vtm[i2][:, :]),
                     reads=[vk], writes=['V'], dma=True)
                g2 = gsm[i2]
                gk = ('gsm', i2)
                P.op('act', lambda e, g2=g2: e.activation(out=g2[:, 4:8], in_=ps_tm[:, 128:132], func=AF.Sigmoid),
                     reads=['ptm'], writes=[gk])
                P.op('dve', lambda e, g2=g2: e.tensor_tensor(out=g2[:, 8:12], in0=ps_tm[:, 132:136], in1=cst[:, 0:4],
                                                             op=ALU.add), reads=['ptm', 'cst'], writes=[gk])
                P.op('act', lambda e, g2=g2: e.activation(out=g2[:, 12:16], in_=g2[:, 8:12], func=AF.Abs), reads=[gk], writes=[gk])
                P.op('act', lambda e, g2=g2: e.activation(out=g2[:, 16:20], in_=g2[:, 12:16], func=AF.Exp, scale=-1.0),
                     reads=[gk], writes=[gk])
                P.op('act', lambda e, g2=g2: e.activation(out=g2[:, 20:24], in_=g2[:, 16:20], func=AF.Ln, bias=1.0),
                     reads=[gk], writes=[gk])
                P.op('act', lambda e, g2=g2: e.activation(out=g2[:, 24:28], in_=g2[:, 8:12], func=AF.Relu), reads=[gk], writes=[gk])
                P.op('dve', lambda e, g2=g2: e.tensor_tensor(out=g2[:, 24:28], in0=g2[:, 24:28], in1=g2[:, 20:24],
                                                             op=ALU.add), reads=[gk], writes=[gk])
                P.op('dve', lambda e, g2=g2: e.tensor_tensor(out=g2[:, 0:4], in0=g2[:, 24:28], in1=cst[:, 4:8],
                                                             op=ALU.mult), reads=[gk, 'cst'], writes=[gk])
                P.op('pool', lambda e, g2=g2, r0=r0: e.dma_start(out=scr['GB'][r0:r0 + 128, :], in_=g2[:, 0:8]),
                     reads=[gk], writes=['GB'], dma=True)
        P.barrier()


def phase_attn(nc, P, io, scr):
    NKB = S // 128
    NQC = S // 512
    with contextlib.ExitStack() as st:
        sb = lambda name, shape, dt: st.enter_context(nc.sbuf_tensor(name, shape, dt))
        ps = lambda name, shape, dt: st.enter_context(nc.psum_tensor(name, shape, dt))
        QT = [sb("b_QT%d" % i, [128, S], BF16) for i in range(2)]
        KT = sb("b_KT", [128, S], BF16)
        V = sb("b_V", [128, NKB, 128], BF16)
        ones = sb("b_ones", [128, 128], BF16)
        pT = [sb("b_pT%d" % i, [128, 512], BF16) for i in range(3)]
        rinv = [sb("b_rinv%d" % i, [128, 512], F32) for i in range(2)]
        of = [sb("b_of%d" % i, [128, 512], F32) for i in range(2)]
        saz = [sb("b_saz%d" % i, [128, 512], BF16) for i in range(2)]
        gb = [sb("b_g%d" % i, [128, 512], BF16) for i in range(2)]
        ps_s = [ps("b_ps%d" % i, [128, 512], F32) for i in range(3)]
        ps_o = [ps("b_po%d" % i, [128, 512], F32) for i in range(2)]
        ps_r = [ps("b_pr%d" % i, [128, 512], F32) for i in range(2)]
        P.psum_keys(('b_ps', 0), ('b_ps', 1), ('b_ps', 2), ('b_po', 0), ('b_po', 1), ('b_pr', 0), ('b_pr', 1))

        P.op('sp', lambda e: e.dma_start(out=ones[:, :], in_=io['ones'][:, :]), writes=['b_ones'], dma=True)
        for c in range(4):
            sl = slice(c * 2048, (c + 1) * 2048)
            for hh in range(2):
                P.op('sp', lambda e, hh=hh, sl=sl: e.dma_start(out=QT[hh][:, sl], in_=scr['QT'][hh, :, sl]),
                     reads=[('QT', hh)], writes=[('b_QT', hh)], dma=True)
            P.op('sp', lambda e, sl=sl: e.dma_start(out=KT[:, sl], in_=scr['KT'][0, :, sl]),
                 reads=[('KT', 0)], writes=['b_KT'], dma=True)
        vsrc = scr['V'].ap().rearrange("(b p) d -> p b d", p=128)
        for c in range(4):
            P.op('sp', lambda e, c=c: e.dma_start(out=V[:, c * 16:(c + 1) * 16, :], in_=vsrc[:, c * 16:(c + 1) * 16, :]),
                 reads=['V'], writes=['b_V'], dma=True)

        scale = 128.0 ** -0.5
        it = 0
        for hh in range(2):
            for qc in range(NQC):
                q0 = qc * 512
                po = ps_o[it % 2]
                pr = ps_r[it % 2]
                pok = ('b_po', it % 2)
                prk = ('b_pr', it % 2)

                def smm(kb, hh=hh, q0=q0):
                    s = kb % 3
                    P.op('pe', lambda e: e.matmul(ps_s[s][:, :], lhsT=KT[:, kb * 128:(kb + 1) * 128],
                                                  rhs=QT[hh][:, q0:q0 + 512], start=True, stop=True),
                         reads=['b_KT', ('b_QT', hh)], writes=[('b_ps', s)])
                    P.op('act', lambda e: e.activation(out=pT[s][:, :], in_=ps_s[s][:, :], func=AF.Exp, scale=scale),
                         reads=[('b_ps', s)], writes=[('b_pT', s)])

                smm(0)
                smm(1)
                for kb in range(NKB):
                    if kb + 2 < NKB:
                        smm(kb + 2)
                    s = kb % 3
                    P.op('pe', lambda e, kb=kb, s=s, po=po: e.matmul(po[:, :], lhsT=V[:, kb, :], rhs=pT[s][:, :],
                                                              start=(kb == 0), stop=(kb == NKB - 1)),
                         reads=['b_V', ('b_pT', s)], writes=[pok])
                    P.op('pe', lambda e, kb=kb, s=s, pr=pr: e.matmul(pr[:, :], lhsT=ones[:, :], rhs=pT[s][:, :],
                                                              start=(kb == 0), stop=(kb == NKB - 1)),
                         reads=['b_ones', ('b_pT', s)], writes=[prk])
                i2 = it % 2
                P.op('sp', lambda e, i2=i2, hh=hh, q0=q0: e.dma_start(out=saz[i2][:, :], in_=scr['SAZ'][hh, :, q0:q0 + 512]),
                     reads=[('SAZ', hh)], writes=[('b_saz', i2)], dma=True)
                P.op('dve', lambda e, i2=i2, pr=pr: e.reciprocal(out=rinv[i2][:, :], in_=pr[:, :]),
                     reads=[prk], writes=[('b_rinv', i2)])
                P.op('dve', lambda e, i2=i2, po=po: e.tensor_tensor(out=of[i2][:, :], in0=po[:, :], in1=rinv[i2][:, :],
                                                                    op=ALU.mult),
                     reads=[pok, ('b_rinv', i2)], writes=[('b_of', i2)])
                P.op('pool', lambda e, i2=i2: e.tensor_tensor(out=gb[i2][:, :], in0=of[i2][:, :], in1=saz[i2][:, :],
                                                              op=ALU.mult),
                     reads=[('b_of', i2), ('b_saz', i2)], writes=[('b_g', i2)])
                P.op('pool', lambda e, i2=i2, hh=hh, q0=q0: e.dma_start(out=scr['GT'][hh * 128:(hh + 1) * 128, q0:q0 + 512],
                                                                      in_=gb[i2][:, :]),
                     reads=[('b_g', i2)], writes=['GT'], dma=True)
                it += 1
        P.barrier()


def phase_dn(nc, P, io, scr):
    with contextlib.ExitStack() as st:
        sb = lambda name, shape, dt: st.enter_context(nc.sbuf_tensor(name, shape, dt))
        ps = lambda name, shape, dt: st.enter_context(nc.psum_tensor(name, shape, dt))
        TRI4 = sb("c_TRI4", [128, 4, 128], F32)
        SM4 = sb("c_SM4", [128, 4, 128], F32)
        IM4 = sb("c_IM4", [128, 4, 128], F32)
        I4 = sb("c_I4", [128, 4, 128], F32)
        ONESF = sb("c_ONESF", [128, 128], F32)
        BLK = sb("c_BLK", [128, 128], F32)
        ident = sb("c_ident", [128, 128], BF16)
        dnw = sb("c_dnw", [128, 128], F32)
        Oacc = sb("c_Oacc", [128, NPAIR, 2, 128], F32)
        Sf = sb("c_Sf", [128, 4, 128], F32)
        Sbf = sb("c_Sbf", [128, 4, 128], BF16)
        qT4_ = [sb("c_qT4%d" % i, [128, 4, 128], BF16) for i in range(2)]
        kT4_ = [sb("c_kT4%d" % i, [128, 4, 128], BF16) for i in range(2)]
        vT4_ = [sb("c_vT4%d" % i, [128, 4, 128], BF16) for i in range(2)]
        gbt_ = [sb("c_gb%d" % i, [128, 8], F32) for i in range(2)]
        sm_ = [sb("c_sm%d" % i, [128, 24], F32) for i in range(2)]
        gtri = sb("c_gtri", [128, 4, 128], F32)
        absz = sb("c_absz", [128, 4, 128], F32)
        W4 = sb("c_W4", [128, 4, 128], F32)
        EROW_ = [sb("c_EROW%d" % i, [128, 4, 128], F32) for i in range(2)]
        Wm = sb("c_Wm", [128, 4, 128], F32)
        Wi = sb("c_Wi", [128, 4, 128], F32)
        Pb = [sb("c_P%d" % i, [128, 4, 128], BF16) for i in range(2)]
        Ptb = [sb("c_Pt%d" % i, [128, 4, 128], BF16) for i in range(2)]
        Xb = [sb("c_X%d" % i, [128, 4, 128], BF16) for i in range(2)]
        aT4_ = [sb("c_aT4%d" % i, [128, 4, 128], BF16) for i in range(2)]
        qg4_ = [sb("c_qg4%d" % i, [128, 4, 128], BF16) for i in range(2)]
        kg4 = sb("c_kg4", [128, 4, 128], BF16)
        kdec4_ = [sb("c_kdec4%d" % i, [128, 4, 128], BF16) for i in range(2)]
        vtok4 = sb("c_vtok4", [128, 4, 128], BF16)
        ub4_ = [sb("c_ub4%d" % i, [128, 4, 128], F32) for i in range(2)]
        wT4_ = [sb("c_wT4%d" % i, [128, 4, 128], BF16) for i in range(2)]
        vnew = sb("c_vnew", [128, 4, 128], BF16)
        fstat = sb("c_fstat", [128, 4], F32)
        fsq = sb("c_fsq", [128, 128], F32)
        fon = sb("c_fon", [128, 128], BF16)
        sdz = sb("c_sdz", [128, 512], BF16)
        gout = sb("c_gout", [128, 512], BF16)
        B0 = ps("c_B0", [128, 8, 128], BF16)
        B1 = ps("c_B1", [128, 4, 128], F32)
        B2 = ps("c_B2", [128, 4, 128], F32)
        B3 = ps("c_B3", [128, 4, 128], F32)
        U = [ps("c_U%d" % i, [128, 4, 128], F32) for i in range(4)]
        P.psum_keys('B0', 'B1', 'B2', 'B3', ('U', 0), ('U', 1), ('U', 2), ('U', 3))

        def ld(t, src, key):
            P.op('sp', lambda e: e.dma_start(out=t, in_=src), writes=[key], dma=True)
        ld(TRI4[:, :, :], io['TRI4'][:, :, :], 'TRI4')
        ld(SM4[:, :, :], io['SM4'][:, :, :], 'SM4')
        ld(IM4[:, :, :], io['IM4'][:, :, :], 'IM4')
        ld(I4[:, :, :], io['I4'][:, :, :], 'I4')
        ld(ONESF[:, :], io['ONESF'][:, :], 'ONESF')
        ld(BLK[:, :], io['BLK'][:, :], 'BLK')
        ld(ident[:, :], io['ident'][:, :], 'c_ident')
        ld(dnw[:, :], io['dn_norm'][0:1, :].partition_broadcast(128), 'dnw')
        P.op('pool', lambda e: e.memset(Sf[:, :, :], 0.0), writes=['Sf'])
        P.op('pool', lambda e: e.memset(Sbf[:, :, :], 0.0), writes=['Sbf'])

        flat = lambda t: t[:, :, :].rearrange("p u t -> p (u t)")

        def pre(p):
            par = p % 2
            pair = [p, p, NPAIR - 1 - p, NPAIR - 1 - p]
            for u in range(4):
                c0 = pair[u] * 128
                hh = u % 2
                P.op('sp', lambda e, u=u, hh=hh, c0=c0: e.dma_start(out=qT4_[par][:, u, :], in_=scr['DQ'][hh, :, c0:c0 + 128]),
                     reads=[('DQ', hh)], writes=[('qT4', par)], dma=True)
                P.op('sp', lambda e, u=u, hh=hh, c0=c0: e.dma_start(out=kT4_[par][:, u, :], in_=scr['DK'][hh, :, c0:c0 + 128]),
                     reads=[('DK', hh)], writes=[('kT4', par)], dma=True)
                P.op('sp', lambda e, u=u, hh=hh, c0=c0: e.dma_start(out=vT4_[par][:, u, :], in_=scr['DV'][hh, :, c0:c0 + 128]),
                     reads=[('DV', hh)], writes=[('vT4', par)], dma=True)
            for d in range(2):
                r0 = pair[2 * d] * 128
                for off in (0, 4):
                    a = off + 2 * d
                    P.op('sp', lambda e, r0=r0, a=a: e.dma_start(out=gbt_[par][:, a:a + 2], in_=scr['GB'][r0:r0 + 128, a:a + 2]),
                         reads=['GB'], writes=[('gbt', par)], dma=True)
            yield
            for u in range(4):
                P.op('dve', lambda e, u=u: e.tensor_scalar(out=gtri[:, u, :], in0=TRI4[:, u, :], scalar1=gbt_[par][:, u:u + 1],
                                                           scalar2=None, op0=ALU.mult), reads=['TRI4', ('gbt', par)], writes=['gtri'])
            P.op('dve', lambda e: e.tensor_scalar(out=sm_[par][:, 20:24], in0=gbt_[par][:, 4:8], scalar1=-1.0, scalar2=None,
                                                  op0=ALU.mult), reads=[('gbt', par)], writes=[('negb', par)])
            P.op('pe', lambda e: e.matmul(B1[:, 0, 0:2], lhsT=TRI4[:, 0, :], rhs=gbt_[par][:, 0:2], start=True, stop=True),
                 reads=['TRI4', ('gbt', par)], writes=['B1'])
            P.op('pe', lambda e: e.matmul(B1[:, 0, 2:4], lhsT=TRI4[:, 2, :], rhs=gbt_[par][:, 2:4], start=True, stop=True),
                 reads=['TRI4', ('gbt', par)], writes=['B1'])
            P.op('pe', lambda e: e.matmul(B1[:, 0, 4:8], lhsT=BLK[:, :], rhs=gbt_[par][:, 0:4], start=True, stop=True),
                 reads=['BLK', ('gbt', par)], writes=['B1'])
            P.op('act', lambda e: e.copy(out=sm_[par][:, 0:8], in_=B1[:, 0, 0:8]), reads=['B1'], writes=[('sm', par)])
            for u in range(4):
                P.op('pe', lambda e, u=u: e.matmul(B1[:, u, :], lhsT=ONESF[:, :], rhs=gtri[:, u, :], start=True, stop=True),
                     reads=['ONESF', 'gtri'], writes=['B1'])
            P.op('dve', lambda e: e.tensor_tensor(out=sm_[par][:, 8:12], in0=sm_[par][:, 4:8], in1=sm_[par][:, 0:4], op=ALU.subtract),
                 reads=[('sm', par)], writes=[('sm2', par)])
            P.op('act', lambda e: e.activation(out=sm_[par][:, 12:16], in_=sm_[par][:, 8:12], func=AF.Exp), reads=[('sm2', par)], writes=[('sm3', par)])
            P.op('act', lambda e: e.activation(out=sm_[par][:, 16:20], in_=sm_[par][:, 0:4], func=AF.Exp), reads=[('sm', par)], writes=[('sm4', par)])
            for u in range(4):
                P.op('dve', lambda e, u=u: e.tensor_scalar(out=absz[:, u, :], in0=B1[:, u, :], scalar1=sm_[par][:, u:u + 1],
                                                           scalar2=None, op0=ALU.subtract),
                     reads=['B1', ('sm', par)], writes=['absz'])
            P.op('act', lambda e: e.activation(out=flat(absz), in_=flat(absz), func=AF.Abs), reads=['absz'], writes=['absz'])
            P.op('act', lambda e: e.activation(out=flat(W4), in_=flat(absz), func=AF.Exp, scale=-1.0),
                 reads=['absz'], writes=['W4'])
            P.op('act', lambda e: e.activation(out=flat(EROW_[par]), in_=B1[:, :, :].rearrange("p u t -> p (u t)"), func=AF.Exp),
                 reads=['B1'], writes=[('EROW', par)])
            yield
            for u in range(4):
                P.op('pe', lambda e, u=u: e.matmul(B2[:, u, :], lhsT=kT4_[par][:, u, :], rhs=kT4_[par][:, u, :], start=True, stop=True),
                     reads=[('kT4', par)], writes=['B2'])
            for u in range(4):
                P.op('pe', lambda e, u=u: e.matmul(B3[:, u, :], lhsT=kT4_[par][:, u, :], rhs=qT4_[par][:, u, :], start=True, stop=True),
                     reads=[('kT4', par), ('qT4', par)], writes=['B3'])
            for u in range(4):
                P.op('pe', lambda e, u=u: e.transpose(out=B0[:, u, :], in_=kT4_[par][:, u, :], identity=ident[:, :]),
                     reads=[('kT4', par), 'c_ident'], writes=['B0'])
            for u in range(4):
                P.op('pe', lambda e, u=u: e.transpose(out=B0[:, 4 + u, :], in_=vT4_[par][:, u, :], identity=ident[:, :]),
                     reads=[('vT4', par), 'c_ident'], writes=['B0'])
            P.op('dve', lambda e: e.tensor_tensor(out=flat(Wm), in0=flat(W4), in1=flat(SM4), op=ALU.mult),
                 reads=['W4', 'SM4'], writes=['Wm'])
            P.op('pool', lambda e: e.tensor_tensor(out=flat(Wi), in0=flat(W4), in1=flat(IM4), op=ALU.mult),
                 reads=['W4', 'IM4'], writes=['Wi'])
            for u in range(4):
                P.op('dve', lambda e, u=u: e.scalar_tensor_tensor(out=Pb[0][:, u, :], in0=B2[:, u, :], scalar=sm_[par][:, 20 + u:21 + u],
                                                                  in1=Wm[:, u, :], op0=ALU.mult, op1=ALU.mult),
                     reads=['B2', ('negb', par), 'Wm'], writes=[('P', 0)])
            P.op('dve', lambda e: e.tensor_tensor(out=flat(aT4_[par]), in0=B3[:, :, :].rearrange("p u t -> p (u t)"), in1=flat(Wi),
                                                  op=ALU.mult), reads=['B3', 'Wi'], writes=[('aT4', par)])
            P.op('dve', lambda e: e.tensor_tensor(out=flat(qg4_[par]), in0=flat(qT4_[par]), in1=flat(EROW_[par]), op=ALU.mult),
                 reads=[('qT4', par), ('EROW', par)], writes=[('qg4', par)])
            for u in range(4):
                P.op('act', lambda e, u=u: e.activation(out=kg4[:, u, :], in_=B0[:, u, :], func=AF.Copy, scale=sm_[par][:, 16 + u:17 + u]),
                     reads=['B0', ('sm4', par)], writes=['kg4'])
                P.op('act', lambda e, u=u: e.activation(out=kdec4_[par][:, u, :], in_=B0[:, u, :], func=AF.Copy, scale=sm_[par][:, 12 + u:13 + u]),
                     reads=['B0', ('sm3', par)], writes=[('kdec4', par)])
            P.op('act', lambda e: e.copy(out=flat(vtok4), in_=B0[:, 4:8, :].rearrange("p u t -> p (u t)")),
                 reads=['B0'], writes=['vtok4'])
            yield
            for u in range(4):
                P.op('pe', lambda e, u=u: e.transpose(out=B0[:, u, :], in_=Pb[0][:, u, :], identity=ident[:, :]),
                     reads=[('P', 0), 'c_ident'], writes=['B0'])
            P.op('act', lambda e: e.copy(out=flat(Ptb[0]), in_=B0[:, 0:4, :].rearrange("p u t -> p (u t)")),
                 reads=['B0'], writes=[('Pt', 0)])
            P.op('pool', lambda e: e.tensor_tensor(out=flat(Xb[0]), in0=flat(Pb[0]), in1=flat(I4), op=ALU.add),
                 reads=[('P', 0), 'I4'], writes=[('X', 0)])
            cur = 0
            for lvl in range(1, 6):
                nxt = 1 - cur
                if lvl < 5:
                    for u in range(4):
                        P.op('pe', lambda e, u=u, cur=cur: e.matmul(B2[:, u, :], lhsT=Ptb[cur][:, u, :], rhs=Pb[cur][:, u, :],
                                                                    start=True, stop=True),
                             reads=[('P', cur), ('Pt', cur)], writes=['B2'])
                for u in range(4):
                    P.op('pe', lambda e, u=u, cur=cur: e.matmul(B3[:, u, :], lhsT=Pb[cur][:, u, :], rhs=Ptb[cur][:, u, :],
                                                                start=True, stop=True),
                         reads=[('P', cur), ('Pt', cur)], writes=['B3'])
                if lvl < 5:
                    P.op('act', lambda e, nxt=nxt: e.copy(out=flat(Pb[nxt]), in_=B2[:, :, :].rearrange("p u t -> p (u t)")),
                         reads=['B2'], writes=[('P', nxt)])
                P.op('dve', lambda e, nxt=nxt: e.tensor_copy(out=flat(Ptb[nxt]), in_=B3[:, :, :].rearrange("p u t -> p (u t)")),
                     reads=['B3'], writes=[('Pt', nxt)])
                for u in range(4):
                    P.op('pe', lambda e, u=u, cur=cur, nxt=nxt: e.matmul(B1[:, u, :], lhsT=Ptb[nxt][:, u, :], rhs=Xb[cur][:, u, :],
                                                                         start=True, stop=True),
                         reads=[('Pt', nxt), ('X', cur)], writes=['B1'])
                P.op('dve', lambda e, cur=cur, nxt=nxt: e.tensor_tensor(out=flat(Xb[nxt]), in0=B1[:, :, :].rearrange("p u t -> p (u t)"),
                                                                        in1=flat(Xb[cur]), op=ALU.add),
                     reads=['B1', ('X', cur)], writes=[('X', nxt)])
                cur = nxt
                yield
            yield
            X = Xb[cur]
            Xk = ('X', cur)
            for u in range(4):
                P.op('pe', lambda e, u=u, X=X: e.matmul(B2[:, u, :], lhsT=X[:, u, :], rhs=vtok4[:, u, :], start=True, stop=True),
                     reads=[Xk, 'vtok4'], writes=['B2'])
            for u in range(4):
                P.op('pe', lambda e, u=u, X=X: e.matmul(B3[:, u, :], lhsT=kg4[:, u, :], rhs=X[:, u, :], start=True, stop=True),
                     reads=[Xk, 'kg4'], writes=['B3'])
            for u in range(4):
                P.op('act', lambda e, u=u: e.activation(out=ub4_[par][:, u, :], in_=B2[:, u, :], func=AF.Copy, scale=gbt_[par][:, 4 + u:5 + u]),
                     reads=['B2', ('gbt', par)], writes=[('ub4', par)])
            P.op('dve', lambda e: e.tensor_copy(out=flat(wT4_[par]), in_=B3[:, :, :].rearrange("p u t -> p (u t)")),
                 reads=['B3'], writes=[('wT4', par)])

            yield
        def rec(p):
            par = p % 2
            pair = [p, p, NPAIR - 1 - p, NPAIR - 1 - p]
            for ci in range(2):
                for u in range(4):
                    fwd = u < 2
                    c = ci if fwd else 1 - ci
                    r = slice(64 * c, 64 * c + 64)
                    col = 64 * c + 63 if fwd else 64 * c
                    sk = ('Sbf', u)
                    uk = ('U', u)
                    P.op('pe', lambda e, u=u: e.matmul(U[u][:, 0, :], lhsT=wT4_[par][:, u, :], rhs=Sbf[:, u, :], start=True, stop=True),
                         reads=[('wT4', par), sk, 'Sbf'], writes=[uk])
                    P.op('dve', lambda e, u=u, r=r: e.scalar_tensor_tensor(out=vnew[r, u, :], in0=U[u][r, 0, :],
                                                                           scalar=sm_[par][r, 20 + u:21 + u], in1=ub4_[par][r, u, :],
                                                                           op0=ALU.mult, op1=ALU.add),
                         reads=[uk, ('negb', par), ('ub4', par)], writes=[('vnew', u)])
                    P.op('pe', lambda e, u=u: e.matmul(U[u][:, 1, :], lhsT=qg4_[par][:, u, :], rhs=Sbf[:, u, :], start=True, stop=False),
                         reads=[('qg4', par), sk, 'Sbf'], writes=[uk])
                    P.op('pe', lambda e, u=u, r=r: e.matmul(U[u][:, 1, :], lhsT=aT4_[par][r, u, :], rhs=vnew[r, u, :], start=False, stop=True),
                         reads=[('aT4', par), ('vnew', u)], writes=[uk])
                    P.op('pe', lambda e, u=u, r=r: e.matmul(U[u][:, 2, :], lhsT=kdec4_[par][r, u, :], rhs=vnew[r, u, :], start=True, stop=True),
                         reads=[('kdec4', par), ('vnew', u)], writes=[uk])
                    hh = u % 2
                    ok = ('Oacc', pair[u], hh)
                    if p < NPAIR // 2:
                        P.op('act', lambda e, u=u, r=r, hh=hh, pu=pair[u]: e.copy(out=Oacc[r, pu, hh, :], in_=U[u][r, 1, :]),
                             reads=[uk], writes=[ok])
                    else:
                        P.op('dve', lambda e, u=u, r=r, hh=hh, pu=pair[u]: e.tensor_tensor(out=Oacc[r, pu, hh, :], in0=U[u][r, 1, :],
                                                                                           in1=Oacc[r, pu, hh, :], op=ALU.add),
                             reads=[uk, ok], writes=[ok])
                    P.op('dve', lambda e, u=u, col=col: e.scalar_tensor_tensor(out=Sf[:, u, :], in0=Sf[:, u, :],
                                                                               scalar=EROW_[par][:, u, col:col + 1], in1=U[u][:, 2, :],
                                                                               op0=ALU.mult, op1=ALU.add),
                         reads=[uk, ('EROW', par), ('Sf', u), 'Sf'], writes=[('Sf', u)])
                    P.op('act', lambda e, u=u: e.copy(out=Sbf[:, u, :], in_=Sf[:, u, :]), reads=[('Sf', u), 'Sf'], writes=[sk])
                    yield

        nsteps = NPAIR if DN_STEPS is None else DN_STEPS
        for _ in pre(0):
            pass
        for p in range(nsteps):
            r = rec(p)
            q = pre(p + 1) if p + 1 < nsteps else iter(())
            ra, qa = True, True
            while ra or qa:
                if ra:
                    ra = next(r, 'END') != 'END'
                if qa:
                    qa = next(q, 'END') != 'END'
                if qa:
                    qa = next(q, 'END') != 'END'

        for hh in (range(2) if DN_LVL >= 6 else []):
            for T in range(NT):
                P.op('sp', lambda e, hh=hh, T=T: e.dma_start(out=sdz[:, :], in_=scr['SDZ'][hh, :, T * 512:(T + 1) * 512]),
                     reads=[('SDZ', hh)], writes=['sdz'], dma=True)
                for j in range(4):
                    pp = T * 4 + j
                    ok = ('Oacc', pp, hh)
                    P.op('act', lambda e, pp=pp, hh=hh: e.activation(out=fsq[:, :], in_=Oacc[:, pp, hh, :], func=AF.Square,
                                                                     accum_out=fstat[:, 0:1]), reads=[ok], writes=['fsq', 'fstat'])
                    emit_rsqrt(P, fstat[:, 2:3], fstat[:, 0:1], 1.0 / 128, ['fstat'], ['fstat'])
                    P.op('dve', lambda e, pp=pp, hh=hh: e.scalar_tensor_tensor(out=fon[:, :], in0=Oacc[:, pp, hh, :],
                                                                               scalar=fstat[:, 2:3], in1=dnw[:, :],
                                                                               op0=ALU.mult, op1=ALU.mult),
                         reads=[ok, 'fstat', 'dnw'], writes=['fon'])
                    P.op('pe', lambda e: e.transpose(out=B0[:, 0, :], in_=fon[:, :], identity=ident[:, :]),
                         reads=['fon', 'c_ident'], writes=['B0'])
                    P.op('dve', lambda e, j=j: e.tensor_tensor(out=gout[:, j * 128:(j + 1) * 128], in0=B0[:, 0, :],
                                                               in1=sdz[:, j * 128:(j + 1) * 128], op=ALU.mult),
                         reads=['B0', 'sdz'], writes=['gout'])
                P.op('pool', lambda e, hh=hh, T=T: e.dma_start(out=scr['GT'][256 + hh * 128:256 + (hh + 1) * 128, T * 512:(T + 1) * 512],
                                                             in_=gout[:, :]), reads=['gout'], writes=['GT'], dma=True)
        P.barrier()


def phase_tail(nc, P, io, gsrc, g_reads, gall=None):
    with contextlib.ExitStack() as st:
        sb = lambda name, shape, dt: st.enter_context(nc.sbuf_tensor(name, shape, dt))
        ps = lambda name, shape, dt: st.enter_context(nc.psum_tensor(name, shape, dt))
        Wn = {}
        for nm in ('w_ba', 'w_bd', 'w_ga', 'w_gd', 'w_out', 'w_pg'):
            Wn[nm] = sb("e_" + nm, [128, 8, D], BF16)
        Wpp = sb("e_wpp", [128, 2, D], BF16)
        stg = [sb("e_stg%d" % i, [128, D], F32) for i in range(2)]
        wb = sb("e_wb", [128, D], F32)
        wpost = sb("e_wpost", [128, D], F32)
        wple = sb("e_wple", [128, D], F32)
        ident = sb("e_ident", [128, 128], BF16)
        xt = [sb("e_xt%d" % i, [128, D], F32) for i in range(2)]
        xn = [sb("e_xn%d" % i, [128, D], BF16) for i in range(2)]
        stat = [sb("e_stat%d" % i, [128, 4], F32) for i in range(2)]
        st2 = sb("e_st2", [128, 8], F32)
        hT = sb("e_hT", [128, 8, 512], BF16)
        GaT = sb("e_GaT", [128, 8, 512], BF16)
        GdT = sb("e_GdT", [128, 8, 512], BF16)
        mixT = sb("e_mixT", [128, 8, 512], BF16)
        sa = sb("e_sa", [128, 512], F32)
        sd = sb("e_sd", [128, 512], F32)
        m1 = sb("e_m1", [128, 512], F32)
        m2 = sb("e_m2", [128, 512], F32)
        tmp = sb("e_tmp", [128, D], F32)
        x1 = sb("e_x1", [128, D], F32)
        x1b = sb("e_x1b", [128, D], BF16)
        x1T = sb("e_x1T", [128, 8, 128], BF16)
        pt = sb("e_pt", [128, 256], F32)
        ptb = sb("e_ptb", [128, 256], BF16)
        pT = sb("e_pT", [128, 2, 128], BF16)
        s2 = sb("e_s2", [128, D], F32)
        sq = sb("e_sq", [128, D], BF16)
        E = [ps("e_E%d" % i, [128, 512], F32) for i in range(4)]
        E45 = ps("e_E45", [128, 2, 512], F32)
        ps_tp = [ps("e_ptp%d" % i, [128, D], BF16) for i in range(2)]
        P.psum_keys(('E', 0), ('E', 1), ('E', 2), ('E', 3), 'E45', ('ps_tp', 0), ('ps_tp', 1))
        if gall is not None:
            selm = sb("e_selm", [128, 8, 128], BF16)
            cbuf = [sb("e_cb%d" % i, [128, 512], BF16) for i in range(6)]
            P.op('sp', lambda e: e.dma_start(out=selm[:, :, :], in_=io['selm'][:, :, :]), writes=['selm'], dma=True)
        ncb = [0]

        def ldb(t, src, key):
            P.op('sp', lambda e: e.dma_start(out=t, in_=src), writes=[key], dma=True)
        ldb(wb[:, :], io['norm_pre'][0:1, :].partition_broadcast(128), 'wb')
        ldb(wpost[:, :], io['norm_post'][0:1, :].partition_broadcast(128), 'wpost')
        ldb(wple[:, :], io['ple_norm'][0:1, :].partition_broadcast(128), 'wple')
        ldb(ident[:, :], io['ident'][:, :], 'ident')
        for nm in Wn:
            load_cast(P, lambda c, nm=nm: Wn[nm][:, c, :], lambda c, nm=nm: io[nm][c * 128:(c + 1) * 128, :], stg, 8, nm, D)
        load_cast(P, lambda c: Wpp[:, c, :], lambda c: io['w_pp'][c * 128:(c + 1) * 128, :], stg, 2, 'w_pp', D)

        for T in range(TOK2 // 512):
            t0 = T * 512
            for tb in range(4):
                emit_rms_hT(P, io['x2'][t0 + tb * 128:t0 + (tb + 1) * 128, :], wb, ident, xt, xn, hT, ps_tp, stat, tb, 'hT')
            for kt in range(8):
                if gall is None:
                    P.op('sp', lambda e, kt=kt, t0=t0: e.dma_start(out=GaT[:, kt, :], in_=gsrc(0, kt, t0)),
                         reads=g_reads, writes=['GaT'], dma=True)
                    P.op('sp', lambda e, kt=kt, t0=t0: e.dma_start(out=GdT[:, kt, :], in_=gsrc(1, kt, t0)),
                         reads=g_reads, writes=['GdT'], dma=True)
                    continue
                for kind, dst, dkey in ((0, GaT, 'GaT'), (1, GdT, 'GdT')):
                    bank = (kt * 2 + kind) % 4
                    for cand in range(8):
                        bb, qq = cand // 4, cand % 4
                        row0 = (bb * 4 + kt // 2) * 512 + kind * 256 + (kt % 2) * 128
                        col0 = qq * TOK2 + t0
                        ci = ncb[0] % 6
                        ncb[0] += 1
                        P.op('sp', lambda e, ci=ci, row0=row0, col0=col0: e.dma_start(
                            out=cbuf[ci][:, :], in_=gall[row0:row0 + 128, col0:col0 + 512]),
                            reads=['GALL'], writes=[('cb', ci)], dma=True)
                        P.op('pe', lambda e, ci=ci, cand=cand, bank=bank: e.matmul(
                            E[bank][:, :], lhsT=selm[:, cand, :], rhs=cbuf[ci][:, :], start=(cand == 0), stop=(cand == 7)),
                            reads=['selm', ('cb', ci)], writes=[('E', bank)])
                    if kind == 0:
                        P.op('act', lambda e, kt=kt, bank=bank, dst=dst: e.copy(out=dst[:, kt, :], in_=E[bank][:, :]),
                             reads=[('E', bank)], writes=[dkey])
                    else:
                        P.op('dve', lambda e, kt=kt, bank=bank, dst=dst: e.tensor_copy(out=dst[:, kt, :], in_=E[bank][:, :]),
                             reads=[('E', bank)], writes=[dkey])
            for fo in range(8):
                fs = slice(fo * 128, (fo + 1) * 128)
                for (bank, wn, src, skey) in ((0, 'w_ba', GaT, 'GaT'), (1, 'w_ga', hT, 'hT'), (2, 'w_bd', GdT, 'GdT'), (3, 'w_gd', hT, 'hT')):
                    for k in range(8):
                        P.op('pe', lambda e, bank=bank, wn=wn, src=src, k=k, fs=fs: e.matmul(
                            E[bank][:, :], lhsT=Wn[wn][:, k, fs], rhs=src[:, k, :], start=(k == 0), stop=(k == 7)),
                            reads=[wn, skey], writes=[('E', bank)])
                P.op('act', lambda e: e.activation(out=sa[:, :], in_=E[1][:, :], func=AF.Sigmoid), reads=[('E', 1)], writes=['sa'])
                P.op('act', lambda e: e.activation(out=sd[:, :], in_=E[3][:, :], func=AF.Sigmoid), reads=[('E', 3)], writes=['sd'])
                P.op('dve', lambda e: e.tensor_tensor(out=m1[:, :], in0=E[0][:, :], in1=sa[:, :], op=ALU.mult),
                     reads=[('E', 0), 'sa'], writes=['m1'])
                P.op('dve', lambda e: e.tensor_tensor(out=m2[:, :], in0=E[2][:, :], in1=sd[:, :], op=ALU.mult),
                     reads=[('E', 2), 'sd'], writes=['m2'])
                P.op('pool', lambda e, fo=fo: e.tensor_tensor(out=mixT[:, fo, :], in0=m1[:, :], in1=m2[:, :], op=ALU.add),
                     reads=['m1', 'm2'], writes=['mixT'])
            for tb in range(4):
                ts_ = slice(tb * 128, (tb + 1) * 128)
                r0 = t0 + tb * 128
                for half in range(2):
                    for k in range(8):
                        P.op('pe', lambda e, half=half, k=k, ts_=ts_: e.matmul(
                            E45[:, half, :], lhsT=mixT[:, k, ts_], rhs=Wn['w_out'][:, k, half * 512:(half + 1) * 512],
                            start=(k == 0), stop=(k == 7)), reads=['mixT', 'w_out'], writes=['E45'])
                for half in range(2):
                    P.op('act', lambda e, half=half: e.activation(out=sq[:, half * 512:(half + 1) * 512], in_=E45[:, half, :],
                                                                  func=AF.Square, accum_out=st2[:, half:half + 1]),
                         reads=['E45'], writes=['sq', 'st2'])
                P.op('dve', lambda e: e.tensor_tensor(out=st2[:, 2:3], in0=st2[:, 0:1], in1=st2[:, 1:2], op=ALU.add),
                     reads=['st2'], writes=['st2'])
                emit_rsqrt(P, st2[:, 4:5], st2[:, 2:3], 1.0 / D, ['st2'], ['st2'])
                P.op('sp', lambda e, r0=r0: e.dma_start(out=x1[:, :], in_=io['x2'][r0:r0 + 128, :]), writes=['x1'], dma=True)
                for half in range(2):
                    hs = slice(half * 512, (half + 1) * 512)
                    P.op('dve', lambda e, half=half, hs=hs: e.scalar_tensor_tensor(out=tmp[:, hs], in0=E45[:, half, :],
                                                                                   scalar=st2[:, 4:5], in1=wpost[:, hs],
                                                                                   op0=ALU.mult, op1=ALU.mult),
                         reads=['E45', 'st2', 'wpost'], writes=['tmp'])
                P.op('pool', lambda e: e.tensor_tensor(out=x1[:, :], in0=x1[:, :], in1=tmp[:, :], op=ALU.add),
                     reads=['x1', 'tmp'], writes=['x1'])
                P.op('pool', lambda e: e.tensor_copy(out=x1b[:, :], in_=x1[:, :]), reads=['x1'], writes=['x1b'])
                ptp = ps_tp[tb % 2]
                pk = ('ps_tp', tb % 2)
                for k in range(8):
                    P.op('pe', lambda e, k=k, ptp=ptp: e.transpose(out=ptp[:, k * 128:(k + 1) * 128],
                                                                   in_=x1b[:, k * 128:(k + 1) * 128], identity=ident[:, :]),
                         reads=['x1b', 'ident'], writes=[pk])
                P.op('act', lambda e, ptp=ptp: e.copy(out=x1T[:, :, :], in_=ptp[:, :].rearrange("p (k t) -> p k t", k=8)),
                     reads=[pk], writes=['x1T'])
                P.op('sp', lambda e, r0=r0: e.dma_start(out=pt[:, :], in_=io['p2'][r0:r0 + 128, :]), writes=['pt'], dma=True)
                P.op('dve', lambda e: e.tensor_copy(out=ptb[:, :], in_=pt[:, :]), reads=['pt'], writes=['ptb'])
                ptp2 = ps_tp[(tb + 1) % 2]
                pk2 = ('ps_tp', (tb + 1) % 2)
                for k in range(2):
                    P.op('pe', lambda e, k=k, ptp2=ptp2: e.transpose(out=ptp2[:, k * 128:(k + 1) * 128],
                                                                     in_=ptb[:, k * 128:(k + 1) * 128], identity=ident[:, :]),
                         reads=['ptb', 'ident'], writes=[pk2])
                P.op('act', lambda e, ptp2=ptp2: e.copy(out=pT[:, :, :], in_=ptp2[:, 0:256].rearrange("p (k t) -> p k t", k=2)),
                     reads=[pk2], writes=['pT'])
                for half in range(2):
                    hs = slice(half * 512, (half + 1) * 512)
                    for k in range(8):
                        P.op('pe', lambda e, half=half, k=k, hs=hs: e.matmul(E45[:, half, :], lhsT=x1T[:, k, :], rhs=Wn['w_pg'][:, k, hs],
                                                                             start=(k == 0), stop=(k == 7)),
                             reads=['x1T', 'w_pg'], writes=['E45'])
                    for k in range(2):
                        P.op('pe', lambda e, half=half, k=k, hs=hs: e.matmul(E[half][:, :], lhsT=pT[:, k, :], rhs=Wpp[:, k, hs],
                                                                             start=(k == 0), stop=(k == 1)),
                             reads=['pT', 'w_pp'], writes=[('E', half)])
                for half in range(2):
                    hs = slice(half * 512, (half + 1) * 512)
                    P.op('act', lambda e, half=half, hs=hs: e.activation(out=s2[:, hs], in_=E45[:, half, :], func=AF.Sigmoid),
                         reads=['E45'], writes=['s2'])
                    P.op('dve', lambda e, half=half, hs=hs: e.tensor_tensor(out=s2[:, hs], in0=E[half][:, :], in1=s2[:, hs], op=ALU.mult),
                         reads=[('E', half), 's2'], writes=['s2'])
                P.op('act', lambda e: e.activation(out=sq[:, :], in_=s2[:, :], func=AF.Square, accum_out=st2[:, 5:6]),
                     reads=['s2'], writes=['sq', 'st2b'])
                emit_rsqrt(P, st2[:, 7:8], st2[:, 5:6], 1.0 / D, ['st2b'], ['st2b'])
                P.op('dve', lambda e: e.scalar_tensor_tensor(out=tmp[:, :], in0=s2[:, :], scalar=st2[:, 7:8], in1=wple[:, :],
                                                             op0=ALU.mult, op1=ALU.mult), reads=['s2', 'st2b', 'wple'], writes=['tmp'])
                P.op('pool', lambda e: e.tensor_tensor(out=tmp[:, :], in0=tmp[:, :], in1=x1[:, :], op=ALU.add),
                     reads=['tmp', 'x1'], writes=['tmp'])
                P.op('pool', lambda e, r0=r0: e.dma_start(out=io['out'][r0:r0 + 128, :], in_=tmp[:, :]), reads=['tmp'], writes=['out'], dma=True)
        P.barrier()


P1_INPUTS = [('x', [S, D], F32), ('w_in', [D, WCOLS], F32), ('norm_pre', [1, D], F32), ('nw', [128, 4], F32),
             ('cw', [128, 6, 5], F32), ('gcst', [1, 8], F32), ('cos', [128, S], F32), ('sin', [128, S], F32),
             ('ident', [128, 128], BF16), ('ones', [128, 128], BF16), ('TRI4', [128, 4, 128], F32),
             ('SM4', [128, 4, 128], F32), ('IM4', [128, 4, 128], F32), ('I4', [128, 4, 128], F32),
             ('ONESF', [128, 128], F32), ('BLK', [128, 128], F32), ('dn_norm', [1, 128], F32)]
P2_INPUTS = [('x2', [TOK2, D], F32), ('p2', [TOK2, 256], F32), ('norm_pre', [1, D], F32), ('norm_post', [1, D], F32),
             ('ple_norm', [1, D], F32), ('ident', [128, 128], BF16), ('w_ba', [D, D], F32), ('w_bd', [D, D], F32),
             ('w_ga', [D, D], F32), ('w_gd', [D, D], F32), ('w_out', [D, D], F32), ('w_pg', [D, D], F32),
             ('w_pp', [256, D], F32)]


DEBUG_SCR = False
NT_LIM = None
ROPE_LVL = 9
DN_STEPS = None
DN_LVL = 9
ROPE_VAR = 0
PARTS = ('rope', 'silu', 'conv', 'tm')
PHASES = ('proj', 'attn', 'dn')


def _scratch(nc):
    scr = {}
    for nm, shape, dt in (('QT', [2, 128, S], BF16), ('KT', [1, 128, S], BF16), ('V', [S, 128], BF16),
                          ('SAZ', [2, 128, S], BF16), ('SDZ', [2, 128, S], BF16), ('DQ', [2, 128, S], BF16),
                          ('DK', [2, 128, S], BF16), ('DV', [2, 128, S], BF16), ('GB', [S, 8], F32)):
        if DEBUG_SCR:
            scr[nm] = nc.dram_tensor("scr_" + nm, shape, dt, kind="ExternalOutput")
        else:
            scr[nm] = nc.dram_tensor("scr_" + nm, shape, dt)
    return scr


def build(mode):
    nc = bass.Bass("TRN2", target_bir_lowering=False)
    io = {}
    names = []
    if mode in ('p1', 'fused'):
        names += P1_INPUTS
    if mode in ('p2', 'fused'):
        names += [n for n in P2_INPUTS if n[0] not in [m[0] for m in names]]
    for nm, shape, dt in names:
        io[nm] = nc.dram_tensor(nm, shape, dt, kind="ExternalInput")
    with contextlib.ExitStack() as stack:
        P = Prog(nc, stack)
        if mode == 'p1':
            scr = _scratch(nc)
            scr['GT'] = nc.dram_tensor("GT", [512, S], BF16, kind="ExternalOutput")
            if 'proj' in PHASES:
                phase_proj(nc, P, io, scr)
            if 'attn' in PHASES:
                phase_attn(nc, P, io, scr)
            if 'dn' in PHASES:
                phase_dn(nc, P, io, scr)
        elif mode == 'fused':
            scr = _scratch(nc)
            scr['GT'] = nc.dram_tensor("GT_int", [512, S], BF16)
            gall = nc.dram_tensor("GALL_int", [8 * 512, S], BF16)
            io['selm'] = nc.dram_tensor("selm", [128, 8, 128], BF16, kind="ExternalInput")
            io['out'] = nc.dram_tensor("out", [TOK2, D], F32, kind="ExternalOutput")
            phase_proj(nc, P, io, scr)
            phase_attn(nc, P, io, scr)
            phase_dn(nc, P, io, scr)
            P.op('pool', lambda e: e.collective_compute("AllGather", ALU.bypass, replica_groups=[list(range(8))],
                                                        ins=[scr['GT'].ap().opt()], outs=[gall.ap().opt()]),
                 reads=['GT'], writes=['GALL'], dma=True, inc=1)
            phase_tail(nc, P, io, None, None, gall=gall)
        elif mode == 'p2':
            io['G2'] = nc.dram_tensor("G2", [16, 128, TOK2], BF16, kind="ExternalInput")
            io['out'] = nc.dram_tensor("out", [TOK2, D], F32, kind="ExternalOutput")
            phase_tail(nc, P, io, lambda kind, kt, t0: io['G2'][kind * 8 + kt, :, t0:t0 + 512], [])
        P.emit()
    return nc


def _consts():
    bf = ml_dtypes.bfloat16
    idx = np.arange(128)
    same = (idx[:, None] // 64) == (idx[None, :] // 64)
    k = idx[:, None]
    i = idx[None, :]
    tri_f = (same & (k <= i)).astype(np.float32)
    tri_b = (same & (k >= i)).astype(np.float32)
    smt_f = (same & (i > k)).astype(np.float32)
    smt_b = (same & (i < k)).astype(np.float32)
    imt_f = (same & (i >= k)).astype(np.float32)
    imt_b = (same & (i <= k)).astype(np.float32)
    eye = np.eye(128, dtype=np.float32)
    st4 = lambda a, b: np.ascontiguousarray(np.stack([a, a, b, b], axis=1))
    c = {
        'ident': np.eye(128, dtype=np.float32).astype(bf),
        'ones': np.ones((128, 128), np.float32).astype(bf),
        'TRI4': st4(tri_f, tri_b), 'SM4': st4(smt_f, smt_b), 'IM4': st4(imt_f, imt_b), 'I4': st4(eye, eye),
        'ONESF': np.ones((128, 128), np.float32), 'BLK': same.astype(np.float32),
    }
    t = np.arange(S)
    row = (t // 64).astype(np.float32)
    col = (t % 64).astype(np.float32)
    inv = (np.float32(10000.0) ** (-np.arange(32, dtype=np.float32) / np.float32(32))).astype(np.float32)
    ang = np.concatenate([row[:, None] * inv[None, :], col[:, None] * inv[None, :]], axis=1).astype(np.float32)
    cosd = np.repeat(np.cos(ang.astype(np.float64)), 2, axis=1).T.astype(np.float32)
    sind = np.repeat(np.sin(ang.astype(np.float64)), 2, axis=1).T.astype(np.float32)
    sign = np.where(np.arange(128) % 2 == 0, -1.0, 1.0).astype(np.float32)[:, None]
    c['cos'] = np.ascontiguousarray(cosd)
    c['sin'] = np.ascontiguousarray(sind * sign)
    return c


def _p1_inputs(c, inputs, consts):
    b, j = c // 4, c % 4
    w_in = inputs['w_in'][0]
    o = np.cumsum([0, 1024, 256, 256, 1024, 1024, 1024, 1024, 16, 16, 1024, 1024, 1024])
    aq, ak, av, az, dq, dk, dv, db, da, dz = [w_in[:, o[i]:o[i + 1]] for i in range(10)]
    kv = j // 2
    sw = np.arange(128) ^ 1
    hs = [2 * j, 2 * j + 1]
    col = lambda m, h: m[:, h * 128:(h + 1) * 128]
    groups = [col(aq, hs[0]), col(aq, hs[1]), col(aq, hs[0])[:, sw], col(aq, hs[1])[:, sw],
              col(ak, kv), col(ak, kv)[:, sw], col(az, hs[0]), col(az, hs[1]),
              col(dq, hs[0]), col(dq, hs[1]), col(dk, hs[0]), col(dk, hs[1]), col(dv, hs[0]), col(dv, hs[1]),
              col(dz, hs[0]), col(dz, hs[1]), col(av, kv)]
    sel = [0 * 8 + hs[0], 0 * 8 + hs[1], 1 * 8 + hs[0], 1 * 8 + hs[1]]
    groups += [db[:, sel], da[:, sel]]
    wsl = np.ascontiguousarray(np.concatenate(groups, axis=1))
    qn, kn = inputs['q_norm'][0], inputs['k_norm'][0]
    nw = np.stack([qn, qn[sw], kn, kn[sw]], axis=1).astype(np.float32)
    conv = inputs['conv_w'][0]
    cg = []
    for base in (0, 1024, 2048):
        for h in hs:
            cg.append(conv[:, base + h * 128: base + (h + 1) * 128].T)
    cw = np.ascontiguousarray(np.stack(cg, axis=1)).astype(np.float32)
    dtb = inputs['dt_bias'][0].reshape(16)[sel]
    alog = inputs['a_log'][0].reshape(16)[sel]
    gcst = np.concatenate([dtb, alog])[None, :].astype(np.float32)
    d = {'x': np.ascontiguousarray(inputs['x'][b]), 'w_in': wsl, 'norm_pre': inputs['norm_pre'], 'nw': np.ascontiguousarray(nw),
         'cw': cw, 'gcst': gcst, 'dn_norm': inputs['dn_norm']}
    for k_ in ('cos', 'sin', 'ident', 'ones', 'TRI4', 'SM4', 'IM4', 'I4', 'ONESF', 'BLK'):
        d[k_] = consts[k_]
    return d


def _p2_inputs(c, inputs, consts):
    b, q = c // 4, c % 4
    w_in = inputs['w_in'][0]
    tok = slice(q * TOK2, (q + 1) * TOK2)
    return {'x2': np.ascontiguousarray(inputs['x'][b, tok]), 'p2': np.ascontiguousarray(inputs['p'][0, b, tok]),
            'norm_pre': inputs['norm_pre'], 'norm_post': inputs['norm_post'], 'ple_norm': inputs['ple_norm'],
            'ident': consts['ident'], 'w_ba': inputs['w_br_att'][0], 'w_bd': inputs['w_br_dn'][0],
            'w_ga': np.ascontiguousarray(w_in[:, 6688:7712]), 'w_gd': np.ascontiguousarray(w_in[:, 7712:8736]),
            'w_out': inputs['w_out'][0], 'w_pg': inputs['w_ple_gate'][0], 'w_pp': inputs['w_ple_proj'][0]}


FUSED = True


def kernel(**inputs):
    inputs = {k: np.asarray(v) for k, v in inputs.items()}
    consts = _consts()
    if FUSED:
        maps = []
        eye = np.eye(128, dtype=np.float32)
        for c in range(8):
            d = _p1_inputs(c, inputs, consts)
            d.update(_p2_inputs(c, inputs, consts))
            selm = np.zeros((128, 8, 128), np.float32)
            selm[:, c, :] = eye
            d['selm'] = selm.astype(ml_dtypes.bfloat16)
            maps.append(d)
        nc = build('fused')
        r = run_bass_kernel_spmd(nc, maps, core_ids=list(range(8)))
        out = np.zeros((2, S, D), np.float32)
        for c in range(8):
            b, q = c // 4, c % 4
            out[b, q * TOK2:(q + 1) * TOK2] = np.asarray(r.results[c]['out'])
        return out
    nc1 = build('p1')
    r1 = run_bass_kernel_spmd(nc1, [_p1_inputs(c, inputs, consts) for c in range(8)], core_ids=list(range(8)))
    GT = [np.asarray(r1.results[c]['GT']) for c in range(8)]
    in2 = []
    for c in range(8):
        b, q = c // 4, c % 4
        d = _p2_inputs(c, inputs, consts)
        tok = slice(q * TOK2, (q + 1) * TOK2)
        tiles = []
        for kind in range(2):
            for kt in range(8):
                src = GT[b * 4 + kt // 2]
                r0 = kind * 256 + (kt % 2) * 128
                tiles.append(src[r0:r0 + 128, tok])
        d['G2'] = np.ascontiguousarray(np.stack(tiles, axis=0))
        in2.append(d)
    nc2 = build('p2')
    r2 = run_bass_kernel_spmd(nc2, in2, core_ids=list(range(8)))
    out = np.zeros((2, S, D), np.float32)
    for c in range(8):
        b, q = c // 4, c % 4
        out[b, q * TOK2:(q + 1) * TOK2] = np.asarray(r2.results[c]['out'])
    return out
```

```python
import contextlib
import numpy as np
import ml_dtypes
import concourse.bass as bass
import concourse.mybir as mybir
from concourse.bass_utils import run_bass_kernel_spmd

F32 = mybir.dt.float32
BF16 = mybir.dt.bfloat16
I32 = mybir.dt.int32
AF = mybir.ActivationFunctionType
ALU = mybir.AluOpType

S = 8192
D = 1024
NT = S // 512
NPAIR = S // 128
EPS = 1e-6
NFM = 16
NTM = 136
WCOLS = NFM * 128 + NTM
TOK2 = 2048
NDMA_SEMS = 40


class Prog:
    NGEN = 6

    def __init__(self, nc, stack):
        self.nc = nc
        self.eng = {'pe': nc.tensor, 'act': nc.scalar, 'dve': nc.vector, 'pool': nc.gpsimd, 'sp': nc.sync}
        self.lists = {e: [] for e in self.eng}
        self.sems = {}
        for g in range(self.NGEN):
            for e in ('pe', 'act', 'dve', 'pool'):
                self.sems[('c', e, g)] = stack.enter_context(nc.semaphore("c_%s_%d" % (e, g)))
        self.dma = []
        for i in range(NDMA_SEMS):
            self.sems[('d', i)] = stack.enter_context(nc.semaphore("d_%d" % i))
            self.dma.append(0)
        self.pools = {'sp': list(range(0, 28)), 'pool': list(range(28, NDMA_SEMS)), 'act': list(range(28, NDMA_SEMS))}
        self.rr = {'sp': 0, 'pool': 0, 'act': 0}
        self.gen = 0
        self.cnt = {e: 0 for e in self.eng}
        self.seen = {e: {} for e in self.eng}
        self.lastw = {}
        self.readers = {}
        self.excl = set()

    def psum_keys(self, *keys):
        self.excl.update(keys)

    @staticmethod
    def _owner(tok):
        s = tok[0]
        return s[1] if s[0] == 'c' else None

    def _need(self, engine, tok, waits):
        s, v = tok
        if s[0] == 'c' and s[2] < self.gen:
            return
        if self.seen[engine].get(s, 0) < v:
            self.seen[engine][s] = v
            waits.append((s, v))

    def op(self, engine, fn, reads=(), writes=(), dma=False, inc=16):
        waits = []
        same_sync = engine in ('act', 'dve', 'pool')
        for k in list(reads) + list(writes):
            t = self.lastw.get(k)
            if t is not None and (self._owner(t) != engine or same_sync):
                self._need(engine, t, waits)
        for k in list(writes) + [k for k in reads if k in self.excl]:
            for t in self.readers.get(k, ()):
                if self._owner(t) != engine:
                    self._need(engine, t, waits)
        if dma:
            pl = self.pools[engine]
            i = pl[self.rr[engine] % len(pl)]
            self.rr[engine] += 1
            s = ('d', i)
            if self.dma[i] > 0:
                self._need(engine, (s, self.dma[i]), waits)
            self.dma[i] += inc
            tok = (s, self.dma[i])
            incv = inc
        else:
            self.cnt[engine] += 1
            s = ('c', engine, self.gen)
            tok = (s, self.cnt[engine])
            incv = 1
        self.lists[engine].append((waits, fn, s, incv))
        for k in writes:
            self.lastw[k] = tok
            self.readers[k] = []
        for k in reads:
            self.readers.setdefault(k, []).append(tok)
        return tok

    def barrier(self):
        toks = [(('c', e, self.gen), self.cnt[e]) for e in ('pe', 'act', 'dve', 'pool') if self.cnt[e] > 0]
        toks += [(('d', i), v) for i, v in enumerate(self.dma) if v > 0]
        for e in self.eng:
            waits = []
            for t in toks:
                if self._owner(t) != e:
                    self._need(e, t, waits)
            if waits:
                self.lists[e].append((waits, None, None, 0))
        self.gen += 1
        assert self.gen < self.NGEN
        for e in self.cnt:
            self.cnt[e] = 0

    def emit(self):
        nc = self.nc
        with nc.Block() as block:
            def run(ename):
                def body(eng):
                    for waits, fn, s, incv in self.lists[ename]:
                        for (ws, wv) in waits:
                            eng.wait_ge(self.sems[ws], wv)
                        if fn is not None:
                            fn(eng).then_inc(self.sems[s], incv)
                return body
            block.tensor(run('pe'))
            block.scalar(run('act'))
            block.vector(run('dve'))
            block.gpsimd(run('pool'))
            block.sync(run('sp'))


def emit_rsqrt(P, out_ap, in_ap, scale, reads, writes):
    P.op('act', lambda e: e.activation(out=out_ap, in_=in_ap, func=AF.Ln, bias=EPS, scale=scale), reads=reads, writes=writes)
    P.op('act', lambda e: e.activation(out=out_ap, in_=out_ap, func=AF.Exp, scale=-0.5), reads=writes, writes=writes)


def emit_rms_hT(P, x_src, wb, ident, xt, xn, hT, ps_tp, stat, tb, key):
    xs = xt[tb % len(xt)]
    xk = ('xt', tb % len(xt))
    P.op('sp', lambda e: e.dma_start(out=xs[:, :], in_=x_src), writes=[xk], dma=True)
    sq = xn[tb % 2]
    nk = ('xn', tb % 2)
    st = stat[tb % 2]
    sk = ('stat', tb % 2)
    P.op('act', lambda e: e.activation(out=sq[:, :], in_=xs[:, :], func=AF.Square, accum_out=st[:, 0:1]),
         reads=[xk], writes=[nk, sk])
    emit_rsqrt(P, st[:, 2:3], st[:, 0:1], 1.0 / D, [sk], [sk])
    P.op('dve', lambda e: e.scalar_tensor_tensor(out=sq[:, :], in0=xs[:, :], scalar=st[:, 2:3], in1=wb[:, :],
                                                 op0=ALU.mult, op1=ALU.mult), reads=[xk, sk, 'wb'], writes=[nk])
    pk = ('ps_tp', tb % 2)
    pt = ps_tp[tb % 2]
    for k in range(8):
        P.op('pe', lambda e, k=k: e.transpose(out=pt[:, k * 128:(k + 1) * 128], in_=sq[:, k * 128:(k + 1) * 128],
                                              identity=ident[:, :]), reads=[nk, 'ident'], writes=[pk])
    P.op('act', lambda e: e.copy(out=hT[:, :, tb * 128:(tb + 1) * 128],
                                 in_=pt[:, :].rearrange("p (k t) -> p k t", k=8)), reads=[pk], writes=[key])


def load_cast(P, dst_ap_fn, src_ap_fn, stg, nchunks, dkey, width):
    for c in range(nchunks):
        sl = c % 2
        P.op('sp', lambda e, c=c, sl=sl: e.dma_start(out=stg[sl][:, 0:width], in_=src_ap_fn(c)),
             writes=[('stg', sl)], dma=True)
        eng = 'dve' if c % 2 == 0 else 'pool'
        P.op(eng, lambda e, c=c, sl=sl: e.tensor_copy(out=dst_ap_fn(c), in_=stg[sl][:, 0:width]),
             reads=[('stg', sl)], writes=[dkey])


def phase_proj(nc, P, io, scr):
    with contextlib.ExitStack() as st:
        sb = lambda name, shape, dt: st.enter_context(nc.sbuf_tensor(name, shape, dt))
        ps = lambda name, shape, dt: st.enter_context(nc.psum_tensor(name, shape, dt))
        W = sb("a_W", [128, 8, WCOLS], BF16)
        stg = [sb("a_stg%d" % i, [128, WCOLS], F32) for i in range(2)]
        wb = sb("a_wb", [128, D], F32)
        ident = sb("a_ident", [128, 128], BF16)
        ones = sb("a_ones", [128, 128], BF16)
        xt = [sb("a_xt%d" % i, [128, D], F32) for i in range(4)]
        xn = [sb("a_xn%d" % i, [128, D], BF16) for i in range(2)]
        stat = [sb("a_stat%d" % i, [128, 4], F32) for i in range(2)]
        hT = [sb("a_hT%d" % i, [128, 8, 512], BF16) for i in range(2)]
        cs = [sb("a_cs%d" % i, [128, 2, 512], F32) for i in range(2)]
        nw = sb("a_nw", [128, 4], F32)
        cw = sb("a_cw", [128, 6, 5], F32)
        cstg = [sb("a_cstg%d" % g, [128, 520], F32) for g in range(6)]
        cacc = [sb("a_cacc%d" % i, [128, 512], F32) for i in range(2)]
        csil = [sb("a_csil%d" % i, [128, 512], F32) for i in range(2)]
        sqb = [sb("a_sqb%d" % i, [128, 512], BF16) for i in range(2)]
        rstd = [sb("a_rstd%d" % i, [128, 512], F32) for i in range(2)]
        t1 = [sb("a_t1%d" % i, [128, 512], F32) for i in range(2)]
        t2 = [sb("a_t2%d" % i, [128, 512], F32) for i in range(2)]
        ob = [sb("a_ob%d" % i, [128, 512], BF16) for i in range(4)]
        vtm = [sb("a_vtm%d" % i, [128, 128], BF16) for i in range(2)]
        gsm = [sb("a_gsm%d" % i, [128, 32], F32) for i in range(2)]
        cst = sb("a_cst", [128, 8], F32)
        ps_tp = [ps("a_ptp%d" % i, [128, D], BF16) for i in range(2)]
        ps_fm = [ps("a_pfm%d" % i, [128, 512], F32) for i in range(4)]
        ps_ss = ps("a_pss", [128, 512], F32)
        ps_tm = ps("a_ptm", [128, 512], F32)
        P.psum_keys(('pfm', 0), ('pfm', 1), ('pfm', 2), ('pfm', 3), 'pss', 'ptm', ('ps_tp', 0), ('ps_tp', 1))

        P.op('sp', lambda e: e.dma_start(out=wb[:, :], in_=io['norm_pre'][0:1, :].partition_broadcast(128)),
             writes=['wb'], dma=True)
        P.op('sp', lambda e: e.dma_start(out=ident[:, :], in_=io['ident'][:, :]), writes=['ident'], dma=True)
        P.op('sp', lambda e: e.dma_start(out=ones[:, :], in_=io['ones'][:, :]), writes=['ones'], dma=True)
        P.op('sp', lambda e: e.dma_start(out=nw[:, :], in_=io['nw'][:, :]), writes=['nw'], dma=True)
        P.op('sp', lambda e: e.dma_start(out=cw[:, :, :], in_=io['cw'][:, :, :]), writes=['cw'], dma=True)
        P.op('sp', lambda e: e.dma_start(out=cst[:, :], in_=io['gcst'][0:1, :].partition_broadcast(128)),
             writes=['cst'], dma=True)
        P.op('act', lambda e: e.activation(out=cst[:, 4:8], in_=cst[:, 4:8], func=AF.Exp), reads=['cst'], writes=['cst'])
        P.op('dve', lambda e: e.tensor_scalar(out=cst[:, 4:8], in0=cst[:, 4:8], scalar1=-1.0, scalar2=None,
                                              op0=ALU.mult), reads=['cst'], writes=['cst'])
        for g in range(6):
            P.op('pool', lambda e, g=g: e.memset(cstg[g][:, :], 0.0), writes=[('cstg', g)])
        load_cast(P, lambda c: W[:, c, :], lambda c: io['w_in'][c * 128:(c + 1) * 128, :], stg, 8, 'W', WCOLS)

        rope_pairs = [(0, 2, 0, ('QT', 0)), (1, 3, 0, ('QT', 1)), (4, 5, 2, ('KT', 0))]
        silu_groups = [(6, ('SAZ', 0)), (7, ('SAZ', 1)), (14, ('SDZ', 0)), (15, ('SDZ', 1))]
        conv_groups = [(8, 'DQ', 0), (9, 'DQ', 1), (10, 'DK', 0), (11, 'DK', 1), (12, 'DV', 0), (13, 'DV', 1)]
        cnt = {'fm': 0, 'x': 0, 'ob': 0}

        def fm_matmul(g, h):
            slot = cnt['fm'] % 4
            cnt['fm'] += 1
            for k in range(8):
                P.op('pe', lambda e, k=k, slot=slot: e.matmul(ps_fm[slot][:, :], lhsT=W[:, k, g * 128:(g + 1) * 128],
                                                              rhs=h[0][:, k, :], start=(k == 0), stop=(k == 7)),
                     reads=['W', h[1]], writes=[('pfm', slot)])
            return slot

        def next_ob():
            i = cnt['ob'] % 4
            cnt['ob'] += 1
            return i

        def sumsq_rstd(src_ap, srckey, scale, i2):
            P.op('act', lambda e: e.activation(out=sqb[i2][:, :], in_=src_ap, func=AF.Square),
                 reads=[srckey], writes=[('sqb', i2)])
            P.op('pe', lambda e: e.matmul(ps_ss[:, :], lhsT=ones[:, :], rhs=sqb[i2][:, :], start=True, stop=True),
                 reads=['ones', ('sqb', i2)], writes=['pss'])
            emit_rsqrt(P, rstd[i2][:, :], ps_ss[:, :], scale, ['pss'], [('rstd', i2)])

        def conv_post(g, name, hh, ncols, ocol0, tok0):
            gi = g - 8
            i2 = gi % 2
            ck = ('cstg', gi)
            acc = cacc[i2]
            ak = ('cacc', i2)
            P.op('pool', lambda e: e.tensor_scalar(out=acc[:, 0:ncols], in0=cstg[gi][:, 0:ncols], scalar1=cw[:, gi, 0:1],
                                                   scalar2=None, op0=ALU.mult), reads=[ck, 'cw'], writes=[ak])
            for k in range(1, 5):
                P.op('dve', lambda e, k=k: e.scalar_tensor_tensor(out=acc[:, 0:ncols], in0=cstg[gi][:, k:k + ncols],
                                                                   scalar=cw[:, gi, k:k + 1], in1=acc[:, 0:ncols],
                                                                   op0=ALU.mult, op1=ALU.add),
                     reads=[ck, 'cw', ak], writes=[ak])
            sil = csil[i2]
            sk = ('csil', i2)
            P.op('act', lambda e: e.activation(out=sil[:, 0:ncols], in_=acc[:, 0:ncols], func=AF.Silu),
                 reads=[ak], writes=[sk])
            oi = next_ob()
            ok = ('ob', oi)
            n = ncols - ocol0
            if name == 'DV':
                P.op('dve', lambda e: e.tensor_copy(out=ob[oi][:, 0:n], in_=sil[:, ocol0:ncols]), reads=[sk], writes=[ok])
            else:
                P.op('act', lambda e: e.activation(out=sqb[i2][:, 0:ncols], in_=sil[:, 0:ncols], func=AF.Square),
                     reads=[sk], writes=[('sqb', i2)])
                P.op('pe', lambda e: e.matmul(ps_ss[:, 0:ncols], lhsT=ones[:, :], rhs=sqb[i2][:, 0:ncols],
                                              start=True, stop=True), reads=['ones', ('sqb', i2)], writes=['pss'])
                emit_rsqrt(P, rstd[i2][:, 0:ncols], ps_ss[:, 0:ncols], 1.0, ['pss'], [('rstd', i2)])
                sc = (128.0 ** -0.5) if name == 'DQ' else 1.0
                P.op('dve', lambda e: e.scalar_tensor_tensor(out=ob[oi][:, 0:n], in0=sil[:, ocol0:ncols], scalar=sc,
                                                             in1=rstd[i2][:, ocol0:ncols], op0=ALU.mult, op1=ALU.mult),
                     reads=[sk, ('rstd', i2)], writes=[ok])
            P.op('pool', lambda e: e.dma_start(out=scr[name][hh, :, tok0:tok0 + n], in_=ob[oi][:, 0:n]),
                 reads=[ok], writes=[(name, hh)], dma=True)

        for T in range(NT if NT_LIM is None else NT_LIM):
            t0 = T * 512
            h = (hT[T % 2], ('hT', T % 2))
            for tb in range(4):
                emit_rms_hT(P, io['x'][t0 + tb * 128:t0 + (tb + 1) * 128, :], wb, ident, xt, xn, h[0], ps_tp, stat,
                            tb, h[1])
            c2 = cs[T % 2]
            ck2 = ('cs', T % 2)
            P.op('sp', lambda e, c2=c2, t0=t0: e.dma_start(out=c2[:, 0, :], in_=io['cos'][:, t0:t0 + 512]),
                 writes=[ck2], dma=True)
            P.op('sp', lambda e, c2=c2, t0=t0: e.dma_start(out=c2[:, 1, :], in_=io['sin'][:, t0:t0 + 512]),
                 writes=[ck2], dma=True)
            for (g, gs, wc, (dn, hh)) in (rope_pairs if 'rope' in PARTS else []):
                sa = fm_matmul(g, h)
                sbk = fm_matmul(gs, h)
                i2 = cnt['x'] % 2
                cnt['x'] += 1
                sumsq_rstd(ps_fm[sa][:, :], ('pfm', sa), 1.0 / 128, i2)
                if ROPE_LVL < 2:
                    continue
                if ROPE_VAR != 3:
                    P.op('dve', lambda e, sa=sa, i2=i2, wc=wc: e.tensor_scalar(
                        out=t1[i2][:, :], in0=ps_fm[sa][:, :], scalar1=(1.0 if ROPE_VAR == 1 else nw[:, wc:wc + 1]), scalar2=None, op0=ALU.mult),
                        reads=[('pfm', sa), 'nw'] + ([('sqb', i2)] if ROPE_VAR == 4 else []), writes=[('t1', i2)])
                if ROPE_VAR != 2:
                    P.op('dve', lambda e, sbk=sbk, i2=i2, wc=wc: e.tensor_scalar(
                        out=t2[i2][:, :], in0=ps_fm[sbk][:, :], scalar1=(1.0 if ROPE_VAR == 1 else nw[:, wc + 1:wc + 2]), scalar2=None, op0=ALU.mult),
                        reads=[('pfm', sbk), 'nw'], writes=[('t2', i2)])
                P.op('pool', lambda e, i2=i2, c2=c2: e.tensor_tensor(out=t1[i2][:, :], in0=t1[i2][:, :], in1=c2[:, 0, :],
                                                                     op=ALU.mult), reads=[('t1', i2), ck2], writes=[('t1', i2)])
                P.op('pool', lambda e, i2=i2, c2=c2: e.tensor_tensor(out=t2[i2][:, :], in0=t2[i2][:, :], in1=c2[:, 1, :],
                                                                     op=ALU.mult), reads=[('t2', i2), ck2], writes=[('t2', i2)])
                if ROPE_LVL < 3:
                    continue
                P.op('pool', lambda e, i2=i2: e.tensor_tensor(out=t1[i2][:, :], in0=t1[i2][:, :], in1=t2[i2][:, :],
                                                              op=ALU.add), reads=[('t1', i2), ('t2', i2)],
                     writes=[('t1', i2)])
                oi = next_ob()
                P.op('pool', lambda e, i2=i2, oi=oi: e.tensor_tensor(out=ob[oi][:, :], in0=t1[i2][:, :],
                                                                     in1=rstd[i2][:, :], op=ALU.mult),
                     reads=[('t1', i2), ('rstd', i2)], writes=[('ob', oi)])
                P.op('pool', lambda e, oi=oi, dn=dn, hh=hh, t0=t0: e.dma_start(out=scr[dn][hh, :, t0:t0 + 512],
                                                                             in_=ob[oi][:, :]),
                     reads=[('ob', oi)], writes=[(dn, hh)], dma=True)
            for (g, (dn, hh)) in (silu_groups if 'silu' in PARTS else []):
                sa = fm_matmul(g, h)
                oi = next_ob()
                P.op('act', lambda e, sa=sa, oi=oi: e.activation(out=ob[oi][:, :], in_=ps_fm[sa][:, :], func=AF.Silu),
                     reads=[('pfm', sa)], writes=[('ob', oi)])
                P.op('pool', lambda e, oi=oi, dn=dn, hh=hh, t0=t0: e.dma_start(out=scr[dn][hh, :, t0:t0 + 512],
                                                                             in_=ob[oi][:, :]),
                     reads=[('ob', oi)], writes=[(dn, hh)], dma=True)
            for (g, name, hh) in (conv_groups if 'conv' in PARTS else []):
                sa = fm_matmul(g, h)
                gi = g - 8
                P.op('act', lambda e, sa=sa, gi=gi: e.copy(out=cstg[gi][:, 4:516], in_=ps_fm[sa][:, :]),
                     reads=[('pfm', sa)], writes=[('cstg', gi)])
                if T == 0:
                    conv_post(g, name, hh, 512, 2, 0)
                else:
                    conv_post(g, name, hh, 512, 0, t0 - 2)
                P.op('pool', lambda e, gi=gi: e.tensor_copy(out=cstg[gi][:, 0:4], in_=cstg[gi][:, 512:516]),
                     reads=[('cstg', gi)], writes=[('cstg', gi)])
                if T == NT - 1:
                    P.op('pool', lambda e, gi=gi: e.tensor_copy(out=cstg[gi][:, 0:66], in_=cstg[gi][:, 450:516]),
                         reads=[('cstg', gi)], writes=[('cstg', gi)])
                    P.op('pool', lambda e, gi=gi: e.memset(cstg[gi][:, 66:72], 0.0), writes=[('cstg', gi)])
                    conv_post(g, name, hh, 64, 0, S - 64)
            for tb in (range(4) if 'tm' in PARTS else []):
                for k in range(8):
                    P.op('pe', lambda e, k=k, tb=tb, h=h: e.matmul(ps_tm[:, 0:NTM], lhsT=h[0][:, k, tb * 128:(tb + 1) * 128],
                                                                   rhs=W[:, k, NFM * 128:WCOLS], start=(k == 0), stop=(k == 7)),
                         reads=['W', h[1]], writes=['ptm'])
                i2 = tb % 2
                vk = ('vtm', i2)
                P.op('act', lambda e, i2=i2: e.copy(out=vtm[i2][:, :], in_=ps_tm[:, 0:128]), reads=['ptm'], writes=[vk])
                r0 = t0 + tb * 128
                P.op('pool', lambda e, i2=i2, r0=r0: e.dma_start(out=scr['V'][r0:r0 + 128, :], in_=vtm[i2][:, :]),
                     reads=[vk], writes=['V'], dma=True)
                g2 = gsm[i2]
                gk = ('gsm', i2)
                P.op('act', lambda e, g2=g2: e.activation(out=g2[:, 4:8], in_=ps_tm[:, 128:132], func=AF.Sigmoid),
                     reads=['ptm'], writes=[gk])
                P.op('dve', lambda e, g2=g2: e.tensor_tensor(out=g2[:, 8:12], in0=ps_tm[:, 132:136], in1=cst[:, 0:4],
                                                             op=ALU.add), reads=['ptm', 'cst'], writes=[gk])
                P.op('act', lambda e, g2=g2: e.activation(out=g2[:, 12:16], in_=g2[:, 8:12], func=AF.Abs), reads=[gk], writes=[gk])
                P.op('act', lambda e, g2=g2: e.activation(out=g2[:, 16:20], in_=g2[:, 12:16], func=AF.Exp, scale=-1.0),
                     reads=[gk], writes=[gk])
                P.op('act', lambda e, g2=g2: e.activation(out=g2[:, 20:24], in_=g2[:, 16:20], func=AF.Ln, bias=1.0),
                     reads=[gk], writes=[gk])
                P.op('act', lambda e, g2=g2: e.activation(out=g2[:, 24:28], in_=g2[:, 8:12], func=AF.Relu), reads=[gk], writes=[gk])
                P.op('dve', lambda e, g2=g2: e.tensor_tensor(out=g2[:, 24:28], in0=g2[:, 24:28], in1=g2[:, 20:24],
                                                             op=ALU.add), reads=[gk], writes=[gk])
                P.op('dve', lambda e, g2=g2: e.tensor_tensor(out=g2[:, 0:4], in0=g2[:, 24:28], in1=cst[:, 4:8],
                                                             op=ALU.mult), reads=[gk, 'cst'], writes=[gk])
                P.op('pool', lambda e, g2=g2, r0=r0: e.dma_start(out=scr['GB'][r0:r0 + 128, :], in_=g2[:, 0:8]),
                     reads=[gk], writes=['GB'], dma=True)
        P.barrier()


def phase_attn(nc, P, io, scr):
    NKB = S // 128
    NQC = S // 512
    with contextlib.ExitStack() as st:
        sb = lambda name, shape, dt: st.enter_context(nc.sbuf_tensor(name, shape, dt))
        ps = lambda name, shape, dt: st.enter_context(nc.psum_tensor(name, shape, dt))
        QT = [sb("b_QT%d" % i, [128, S], BF16) for i in range(2)]
        KT = sb("b_KT", [128, S], BF16)
        V = sb("b_V", [128, NKB, 128], BF16)
        ones = sb("b_ones", [128, 128], BF16)
        pT = [sb("b_pT%d" % i, [128, 512], BF16) for i in range(3)]
        rinv = [sb("b_rinv%d" % i, [128, 512], F32) for i in range(2)]
        of = [sb("b_of%d" % i, [128, 512], F32) for i in range(2)]
        saz = [sb("b_saz%d" % i, [128, 512], BF16) for i in range(2)]
        gb = [sb("b_g%d" % i, [128, 512], BF16) for i in range(2)]
        ps_s = [ps("b_ps%d" % i, [128, 512], F32) for i in range(3)]
        ps_o = [ps("b_po%d" % i, [128, 512], F32) for i in range(2)]
        ps_r = [ps("b_pr%d" % i, [128, 512], F32) for i in range(2)]
        P.psum_keys(('b_ps', 0), ('b_ps', 1), ('b_ps', 2), ('b_po', 0), ('b_po', 1), ('b_pr', 0), ('b_pr', 1))

        P.op('sp', lambda e: e.dma_start(out=ones[:, :], in_=io['ones'][:, :]), writes=['b_ones'], dma=True)
        for c in range(4):
            sl = slice(c * 2048, (c + 1) * 2048)
            for hh in range(2):
                P.op('sp', lambda e, hh=hh, sl=sl: e.dma_start(out=QT[hh][:, sl], in_=scr['QT'][hh, :, sl]),
                     reads=[('QT', hh)], writes=[('b_QT', hh)], dma=True)
            P.op('sp', lambda e, sl=sl: e.dma_start(out=KT[:, sl], in_=scr['KT'][0, :, sl]),
                 reads=[('KT', 0)], writes=['b_KT'], dma=True)
        vsrc = scr['V'].ap().rearrange("(b p) d -> p b d", p=128)
        for c in range(4):
            P.op('sp', lambda e, c=c: e.dma_start(out=V[:, c * 16:(c + 1) * 16, :], in_=vsrc[:, c * 16:(c + 1) * 16, :]),
                 reads=['V'], writes=['b_V'], dma=True)

        scale = 128.0 ** -0.5
        it = 0
        for hh in range(2):
            for qc in range(NQC):
                q0 = qc * 512
                po = ps_o[it % 2]
                pr = ps_r[it % 2]
                pok = ('b_po', it % 2)
                prk = ('b_pr', it % 2)

                def smm(kb, hh=hh, q0=q0):
                    s = kb % 3
                    P.op('pe', lambda e: e.matmul(ps_s[s][:, :], lhsT=KT[:, kb * 128:(kb + 1) * 128],
                                                  rhs=QT[hh][:, q0:q0 + 512], start=True, stop=True),
                         reads=['b_KT', ('b_QT', hh)], writes=[('b_ps', s)])
                    P.op('act', lambda e: e.activation(out=pT[s][:, :], in_=ps_s[s][:, :], func=AF.Exp, scale=scale),
                         reads=[('b_ps', s)], writes=[('b_pT', s)])

                smm(0)
                smm(1)
                for kb in range(NKB):
                    if kb + 2 < NKB:
                        smm(kb + 2)
                    s = kb % 3
                    P.op('pe', lambda e, kb=kb, s=s, po=po: e.matmul(po[:, :], lhsT=V[:, kb, :], rhs=pT[s][:, :],
                                                              start=(kb == 0), stop=(kb == NKB - 1)),
                         reads=['b_V', ('b_pT', s)], writes=[pok])
                    P.op('pe', lambda e, kb=kb, s=s, pr=pr: e.matmul(pr[:, :], lhsT=ones[:, :], rhs=pT[s][:, :],
                                                              start=(kb == 0), stop=(kb == NKB - 1)),
                         reads=['b_ones', ('b_pT', s)], writes=[prk])
                i2 = it % 2
                P.op('sp', lambda e, i2=i2, hh=hh, q0=q0: e.dma_start(out=saz[i2][:, :], in_=scr['SAZ'][hh, :, q0:q0 + 512]),
                     reads=[('SAZ', hh)], writes=[('b_saz', i2)], dma=True)
                P.op('dve', lambda e, i2=i2, pr=pr: e.reciprocal(out=rinv[i2][:, :], in_=pr[:, :]),
                     reads=[prk], writes=[('b_rinv', i2)])
                P.op('dve', lambda e, i2=i2, po=po: e.tensor_tensor(out=of[i2][:, :], in0=po[:, :], in1=rinv[i2][:, :],
                                                                    op=ALU.mult),
                     reads=[pok, ('b_rinv', i2)], writes=[('b_of', i2)])
                P.op('pool', lambda e, i2=i2: e.tensor_tensor(out=gb[i2][:, :], in0=of[i2][:, :], in1=saz[i2][:, :],
                                                              op=ALU.mult),
                     reads=[('b_of', i2), ('b_saz', i2)], writes=[('b_g', i2)])
                P.op('pool', lambda e, i2=i2, hh=hh, q0=q0: e.dma_start(out=scr['GT'][hh * 128:(hh + 1) * 128, q0:q0 + 512],
                                                                      in_=gb[i2][:, :]),
                     reads=[('b_g', i2)], writes=['GT'], dma=True)
                it += 1
        P.barrier()


def phase_dn(nc, P, io, scr):
    with contextlib.ExitStack() as st:
        sb = lambda name, shape, dt: st.enter_context(nc.sbuf_tensor(name, shape, dt))
        ps = lambda name, shape, dt: st.enter_context(nc.psum_tensor(name, shape, dt))
        TRI4 = sb("c_TRI4", [128, 4, 128], F32)
        SM4 = sb("c_SM4", [128, 4, 128], F32)
        IM4 = sb("c_IM4", [128, 4, 128], F32)
        I4 = sb("c_I4", [128, 4, 128], F32)
        ONESF = sb("c_ONESF", [128, 128], F32)
        BLK = sb("c_BLK", [128, 128], F32)
        ident = sb("c_ident", [128, 128], BF16)
        dnw = sb("c_dnw", [128, 128], F32)
        Oacc = sb("c_Oacc", [128, NPAIR, 2, 128], F32)
        Sf = sb("c_Sf", [128, 4, 128], F32)
        Sbf = sb("c_Sbf", [128, 4, 128], BF16)
        qT4_ = [sb("c_qT4%d" % i, [128, 4, 128], BF16) for i in range(2)]
        kT4_ = [sb("c_kT4%d" % i, [128, 4, 128], BF16) for i in range(2)]
        vT4_ = [sb("c_vT4%d" % i, [128, 4, 128], BF16) for i in range(2)]
        gbt_ = [sb("c_gb%d" % i, [128, 8], F32) for i in range(2)]
        sm_ = [sb("c_sm%d" % i, [128, 24], F32) for i in range(2)]
        gtri = sb("c_gtri", [128, 4, 128], F32)
        absz = sb("c_absz", [128, 4, 128], F32)
        W4 = sb("c_W4", [128, 4, 128], F32)
        EROW_ = [sb("c_EROW%d" % i, [128, 4, 128], F32) for i in range(2)]
        Wm = sb("c_Wm", [128, 4, 128], F32)
        Wi = sb("c_Wi", [128, 4, 128], F32)
        Pb = [sb("c_P%d" % i, [128, 4, 128], BF16) for i in range(2)]
        Ptb = [sb("c_Pt%d" % i, [128, 4, 128], BF16) for i in range(2)]
        Xb = [sb("c_X%d" % i, [128, 4, 128], BF16) for i in range(2)]
        aT4_ = [sb("c_aT4%d" % i, [128, 4, 128], BF16) for i in range(2)]
        qg4_ = [sb("c_qg4%d" % i, [128, 4, 128], BF16) for i in range(2)]
        kg4 = sb("c_kg4", [128, 4, 128], BF16)
        kdec4_ = [sb("c_kdec4%d" % i, [128, 4, 128], BF16) for i in range(2)]
        vtok4 = sb("c_vtok4", [128, 4, 128], BF16)
        ub4_ = [sb("c_ub4%d" % i, [128, 4, 128], F32) for i in range(2)]
        wT4_ = [sb("c_wT4%d" % i, [128, 4, 128], BF16) for i in range(2)]
        vnew = sb("c_vnew", [128, 4, 128], BF16)
        fstat = sb("c_fstat", [128, 4], F32)
        fsq = sb("c_fsq", [128, 128], F32)
        fon = sb("c_fon", [128, 128], BF16)
        sdz = sb("c_sdz", [128, 512], BF16)
        gout = sb("c_gout", [128, 512], BF16)
        B0 = ps("c_B0", [128, 8, 128], BF16)
        B1 = ps("c_B1", [128, 4, 128], F32)
        B2 = ps("c_B2", [128, 4, 128], F32)
        B3 = ps("c_B3", [128, 4, 128], F32)
        U = [ps("c_U%d" % i, [128, 4, 128], F32) for i in range(4)]
        P.psum_keys('B0', 'B1', 'B2', 'B3', ('U', 0), ('U', 1), ('U', 2), ('U', 3))

        def ld(t, src, key):
            P.op('sp', lambda e: e.dma_start(out=t, in_=src), writes=[key], dma=True)
        ld(TRI4[:, :, :], io['TRI4'][:, :, :], 'TRI4')
        ld(SM4[:, :, :], io['SM4'][:, :, :], 'SM4')
        ld(IM4[:, :, :], io['IM4'][:, :, :], 'IM4')
        ld(I4[:, :, :], io['I4'][:, :, :], 'I4')
        ld(ONESF[:, :], io['ONESF'][:, :], 'ONESF')
        ld(BLK[:, :], io['BLK'][:, :], 'BLK')
        ld(ident[:, :], io['ident'][:, :], 'c_ident')
        ld(dnw[:, :], io['dn_norm'][0:1, :].partition_broadcast(128), 'dnw')
        P.op('pool', lambda e: e.memset(Sf[:, :, :], 0.0), writes=['Sf'])
        P.op('pool', lambda e: e.memset(Sbf[:, :, :], 0.0), writes=['Sbf'])

        flat = lambda t: t[:, :, :].rearrange("p u t -> p (u t)")

        def pre(p):
            par = p % 2
            pair = [p, p, NPAIR - 1 - p, NPAIR - 1 - p]
            for u in range(4):
                c0 = pair[u] * 128
                hh = u % 2
                P.op('sp', lambda e, u=u, hh=hh, c0=c0: e.dma_start(out=qT4_[par][:, u, :], in_=scr['DQ'][hh, :, c0:c0 + 128]),
                     reads=[('DQ', hh)], writes=[('qT4', par)], dma=True)
                P.op('sp', lambda e, u=u, hh=hh, c0=c0: e.dma_start(out=kT4_[par][:, u, :], in_=scr['DK'][hh, :, c0:c0 + 128]),
                     reads=[('DK', hh)], writes=[('kT4', par)], dma=True)
                P.op('sp', lambda e, u=u, hh=hh, c0=c0: e.dma_start(out=vT4_[par][:, u, :], in_=scr['DV'][hh, :, c0:c0 + 128]),
                     reads=[('DV', hh)], writes=[('vT4', par)], dma=True)
            for d in range(2):
                r0 = pair[2 * d] * 128
                for off in (0, 4):
                    a = off + 2 * d
                    P.op('sp', lambda e, r0=r0, a=a: e.dma_start(out=gbt_[par][:, a:a + 2], in_=scr['GB'][r0:r0 + 128, a:a + 2]),
                         reads=['GB'], writes=[('gbt', par)], dma=True)
            yield
            for u in range(4):
                P.op('dve', lambda e, u=u: e.tensor_scalar(out=gtri[:, u, :], in0=TRI4[:, u, :], scalar1=gbt_[par][:, u:u + 1],
                                                           scalar2=None, op0=ALU.mult), reads=['TRI4', ('gbt', par)], writes=['gtri'])
            P.op('dve', lambda e: e.tensor_scalar(out=sm_[par][:, 20:24], in0=gbt_[par][:, 4:8], scalar1=-1.0, scalar2=None,
                                                  op0=ALU.mult), reads=[('gbt', par)], writes=[('negb', par)])
            P.op('pe', lambda e: e.matmul(B1[:, 0, 0:2], lhsT=TRI4[:, 0, :], rhs=gbt_[par][:, 0:2], start=True, stop=True),
                 reads=['TRI4', ('gbt', par)], writes=['B1'])
            P.op('pe', lambda e: e.matmul(B1[:, 0, 2:4], lhsT=TRI4[:, 2, :], rhs=gbt_[par][:, 2:4], start=True, stop=True),
                 reads=['TRI4', ('gbt', par)], writes=['B1'])
            P.op('pe', lambda e: e.matmul(B1[:, 0, 4:8], lhsT=BLK[:, :], rhs=gbt_[par][:, 0:4], start=True, stop=True),
                 reads=['BLK', ('gbt', par)], writes=['B1'])
            P.op('act', lambda e: e.copy(out=sm_[par][:, 0:8], in_=B1[:, 0, 0:8]), reads=['B1'], writes=[('sm', par)])
            for u in range(4):
                P.op('pe', lambda e, u=u: e.matmul(B1[:, u, :], lhsT=ONESF[:, :], rhs=gtri[:, u, :], start=True, stop=True),
                     reads=['ONESF', 'gtri'], writes=['B1'])
            P.op('dve', lambda e: e.tensor_tensor(out=sm_[par][:, 8:12], in0=sm_[par][:, 4:8], in1=sm_[par][:, 0:4], op=ALU.subtract),
                 reads=[('sm', par)], writes=[('sm2', par)])
            P.op('act', lambda e: e.activation(out=sm_[par][:, 12:16], in_=sm_[par][:, 8:12], func=AF.Exp), reads=[('sm2', par)], writes=[('sm3', par)])
            P.op('act', lambda e: e.activation(out=sm_[par][:, 16:20], in_=sm_[par][:, 0:4], func=AF.Exp), reads=[('sm', par)], writes=[('sm4', par)])
            for u in range(4):
                P.op('dve', lambda e, u=u: e.tensor_scalar(out=absz[:, u, :], in0=B1[:, u, :], scalar1=sm_[par][:, u:u + 1],
                                                           scalar2=None, op0=ALU.subtract),
                     reads=['B1', ('sm', par)], writes=['absz'])
            P.op('act', lambda e: e.activation(out=flat(absz), in_=flat(absz), func=AF.Abs), reads=['absz'], writes=['absz'])
            P.op('act', lambda e: e.activation(out=flat(W4), in_=flat(absz), func=AF.Exp, scale=-1.0),
                 reads=['absz'], writes=['W4'])
            P.op('act', lambda e: e.activation(out=flat(EROW_[par]), in_=B1[:, :, :].rearrange("p u t -> p (u t)"), func=AF.Exp),
                 reads=['B1'], writes=[('EROW', par)])
            yield
            for u in range(4):
                P.op('pe', lambda e, u=u: e.matmul(B2[:, u, :], lhsT=kT4_[par][:, u, :], rhs=kT4_[par][:, u, :], start=True, stop=True),
                     reads=[('kT4', par)], writes=['B2'])
            for u in range(4):
                P.op('pe', lambda e, u=u: e.matmul(B3[:, u, :], lhsT=kT4_[par][:, u, :], rhs=qT4_[par][:, u, :], start=True, stop=True),
                     reads=[('kT4', par), ('qT4', par)], writes=['B3'])
            for u in range(4):
                P.op('pe', lambda e, u=u: e.transpose(out=B0[:, u, :], in_=kT4_[par][:, u, :], identity=ident[:, :]),
                     reads=[('kT4', par), 'c_ident'], writes=['B0'])
            for u in range(4):
                P.op('pe', lambda e, u=u: e.transpose(out=B0[:, 4 + u, :], in_=vT4_[par][:, u, :], identity=ident[:, :]),
                     reads=[('vT4', par), 'c_ident'], writes=['B0'])
            P.op('dve', lambda e: e.tensor_tensor(out=flat(Wm), in0=flat(W4), in1=flat(SM4), op=ALU.mult),
                 reads=['W4', 'SM4'], writes=['Wm'])
            P.op('pool', lambda e: e.tensor_tensor(out=flat(Wi), in0=flat(W4), in1=flat(IM4), op=ALU.mult),
                 reads=['W4', 'IM4'], writes=['Wi'])
            for u in range(4):
                P.op('dve', lambda e, u=u: e.scalar_tensor_tensor(out=Pb[0][:, u, :], in0=B2[:, u, :], scalar=sm_[par][:, 20 + u:21 + u],
                                                                  in1=Wm[:, u, :], op0=ALU.mult, op1=ALU.mult),
                     reads=['B2', ('negb', par), 'Wm'], writes=[('P', 0)])
            P.op('dve', lambda e: e.tensor_tensor(out=flat(aT4_[par]), in0=B3[:, :, :].rearrange("p u t -> p (u t)"), in1=flat(Wi),
                                                  op=ALU.mult), reads=['B3', 'Wi'], writes=[('aT4', par)])
            P.op('dve', lambda e: e.tensor_tensor(out=flat(qg4_[par]), in0=flat(qT4_[par]), in1=flat(EROW_[par]), op=ALU.mult),
                 reads=[('qT4', par), ('EROW', par)], writes=[('qg4', par)])
            for u in range(4):
                P.op('act', lambda e, u=u: e.activation(out=kg4[:, u, :], in_=B0[:, u, :], func=AF.Copy, scale=sm_[par][:, 16 + u:17 + u]),
                     reads=['B0', ('sm4', par)], writes=['kg4'])
                P.op('act', lambda e, u=u: e.activation(out=kdec4_[par][:, u, :], in_=B0[:, u, :], func=AF.Copy, scale=sm_[par][:, 12 + u:13 + u]),
                     reads=['B0', ('sm3', par)], writes=[('kdec4', par)])
            P.op('act', lambda e: e.copy(out=flat(vtok4), in_=B0[:, 4:8, :].rearrange("p u t -> p (u t)")),
                 reads=['B0'], writes=['vtok4'])
            yield
            for u in range(4):
                P.op('pe', lambda e, u=u: e.transpose(out=B0[:, u, :], in_=Pb[0][:, u, :], identity=ident[:, :]),
                     reads=[('P', 0), 'c_ident'], writes=['B0'])
            P.op('act', lambda e: e.copy(out=flat(Ptb[0]), in_=B0[:, 0:4, :].rearrange("p u t -> p (u t)")),
                 reads=['B0'], writes=[('Pt', 0)])
            P.op('pool', lambda e: e.tensor_tensor(out=flat(Xb[0]), in0=flat(Pb[0]), in1=flat(I4), op=ALU.add),
                 reads=[('P', 0), 'I4'], writes=[('X', 0)])
            pc, xc = 0, 0
            for b in range(1, 7):
                if b <= 4:
                    for u in range(4):
                        P.op('pe', lambda e, u=u, pc=pc: e.matmul(B2[:, u, :], lhsT=Ptb[pc][:, u, :], rhs=Pb[pc][:, u, :],
                                                                  start=True, stop=True),
                             reads=[('P', pc), ('Pt', pc)], writes=['B2'])
                if b <= 5:
                    for u in range(4):
                        P.op('pe', lambda e, u=u, pc=pc: e.matmul(B3[:, u, :], lhsT=Pb[pc][:, u, :], rhs=Ptb[pc][:, u, :],
                                                                  start=True, stop=True),
                             reads=[('P', pc), ('Pt', pc)], writes=['B3'])
                if b >= 2:
                    for u in range(4):
                        P.op('pe', lambda e, u=u, pc=pc, xc=xc: e.matmul(B1[:, u, :], lhsT=Ptb[pc][:, u, :], rhs=Xb[xc][:, u, :],
                                                                         start=True, stop=True),
                             reads=[('Pt', pc), ('X', xc)], writes=['B1'])
                if b <= 4:
                    P.op('act', lambda e, pc=pc: e.copy(out=flat(Pb[1 - pc]), in_=B2[:, :, :].rearrange("p u t -> p (u t)")),
                         reads=['B2'], writes=[('P', 1 - pc)])
                if b <= 5:
                    P.op('dve', lambda e, pc=pc: e.tensor_copy(out=flat(Ptb[1 - pc]), in_=B3[:, :, :].rearrange("p u t -> p (u t)")),
                         reads=['B3'], writes=[('Pt', 1 - pc)])
                if b >= 2:
                    P.op('dve', lambda e, xc=xc: e.tensor_tensor(out=flat(Xb[1 - xc]), in0=B1[:, :, :].rearrange("p u t -> p (u t)"),
                                                                 in1=flat(Xb[xc]), op=ALU.add),
                         reads=['B1', ('X', xc)], writes=[('X', 1 - xc)])
                    xc = 1 - xc
                if b <= 5:
                    pc = 1 - pc
                yield
            yield
            cur = xc
            X = Xb[cur]
            Xk = ('X', cur)
            for u in range(4):
                P.op('pe', lambda e, u=u, X=X: e.matmul(B2[:, u, :], lhsT=X[:, u, :], rhs=vtok4[:, u, :], start=True, stop=True),
                     reads=[Xk, 'vtok4'], writes=['B2'])
            for u in range(4):
                P.op('pe', lambda e, u=u, X=X: e.matmul(B3[:, u, :], lhsT=kg4[:, u, :], rhs=X[:, u, :], start=True, stop=True),
                     reads=[Xk, 'kg4'], writes=['B3'])
            for u in range(4):
                P.op('act', lambda e, u=u: e.activation(out=ub4_[par][:, u, :], in_=B2[:, u, :], func=AF.Copy, scale=gbt_[par][:, 4 + u:5 + u]),
                     reads=['B2', ('gbt', par)], writes=[('ub4', par)])
            P.op('dve', lambda e: e.tensor_copy(out=flat(wT4_[par]), in_=B3[:, :, :].rearrange("p u t -> p (u t)")),
                 reads=['B3'], writes=[('wT4', par)])

            yield
        def rec(p):
            par = p % 2
            pair = [p, p, NPAIR - 1 - p, NPAIR - 1 - p]
            for ci in range(2):
                for u in range(4):
                    fwd = u < 2
                    c = ci if fwd else 1 - ci
                    r = slice(64 * c, 64 * c + 64)
                    col = 64 * c + 63 if fwd else 64 * c
                    sk = ('Sbf', u)
                    uk = ('U', u)
                    P.op('pe', lambda e, u=u: e.matmul(U[u][:, 0, :], lhsT=wT4_[par][:, u, :], rhs=Sbf[:, u, :], start=True, stop=True),
                         reads=[('wT4', par), sk, 'Sbf'], writes=[uk])
                    P.op('dve', lambda e, u=u, r=r: e.scalar_tensor_tensor(out=vnew[r, u, :], in0=U[u][r, 0, :],
                                                                           scalar=sm_[par][r, 20 + u:21 + u], in1=ub4_[par][r, u, :],
                                                                           op0=ALU.mult, op1=ALU.add),
                         reads=[uk, ('negb', par), ('ub4', par)], writes=[('vnew', u)])
                    P.op('pe', lambda e, u=u: e.matmul(U[u][:, 1, :], lhsT=qg4_[par][:, u, :], rhs=Sbf[:, u, :], start=True, stop=False),
                         reads=[('qg4', par), sk, 'Sbf'], writes=[uk])
                    P.op('pe', lambda e, u=u, r=r: e.matmul(U[u][:, 1, :], lhsT=aT4_[par][r, u, :], rhs=vnew[r, u, :], start=False, stop=True),
                         reads=[('aT4', par), ('vnew', u)], writes=[uk])
                    P.op('pe', lambda e, u=u, r=r: e.matmul(U[u][:, 2, :], lhsT=kdec4_[par][r, u, :], rhs=vnew[r, u, :], start=True, stop=True),
                         reads=[('kdec4', par), ('vnew', u)], writes=[uk])
                    hh = u % 2
                    ok = ('Oacc', pair[u], hh)
                    if p < NPAIR // 2:
                        P.op('act', lambda e, u=u, r=r, hh=hh, pu=pair[u]: e.copy(out=Oacc[r, pu, hh, :], in_=U[u][r, 1, :]),
                             reads=[uk], writes=[ok])
                    else:
                        P.op('dve', lambda e, u=u, r=r, hh=hh, pu=pair[u]: e.tensor_tensor(out=Oacc[r, pu, hh, :], in0=U[u][r, 1, :],
                                                                                           in1=Oacc[r, pu, hh, :], op=ALU.add),
                             reads=[uk, ok], writes=[ok])
                    P.op('dve', lambda e, u=u, col=col: e.scalar_tensor_tensor(out=Sf[:, u, :], in0=Sf[:, u, :],
                                                                               scalar=EROW_[par][:, u, col:col + 1], in1=U[u][:, 2, :],
                                                                               op0=ALU.mult, op1=ALU.add),
                         reads=[uk, ('EROW', par), ('Sf', u), 'Sf'], writes=[('Sf', u)])
                    P.op('act', lambda e, u=u: e.copy(out=Sbf[:, u, :], in_=Sf[:, u, :]), reads=[('Sf', u), 'Sf'], writes=[sk])
                    yield

        nsteps = NPAIR if DN_STEPS is None else DN_STEPS
        for _ in pre(0):
            pass
        for p in range(nsteps):
            r = rec(p)
            q = pre(p + 1) if p + 1 < nsteps else iter(())
            ra, qa = True, True
            while ra or qa:
                if ra:
                    ra = next(r, 'END') != 'END'
                if qa:
                    qa = next(q, 'END') != 'END'
                if qa:
                    qa = next(q, 'END') != 'END'

        for hh in (range(2) if DN_LVL >= 6 else []):
            for T in range(NT):
                P.op('sp', lambda e, hh=hh, T=T: e.dma_start(out=sdz[:, :], in_=scr['SDZ'][hh, :, T * 512:(T + 1) * 512]),
                     reads=[('SDZ', hh)], writes=['sdz'], dma=True)
                for j in range(4):
                    pp = T * 4 + j
                    ok = ('Oacc', pp, hh)
                    P.op('act', lambda e, pp=pp, hh=hh: e.activation(out=fsq[:, :], in_=Oacc[:, pp, hh, :], func=AF.Square,
                                                                     accum_out=fstat[:, 0:1]), reads=[ok], writes=['fsq', 'fstat'])
                    emit_rsqrt(P, fstat[:, 2:3], fstat[:, 0:1], 1.0 / 128, ['fstat'], ['fstat'])
                    P.op('dve', lambda e, pp=pp, hh=hh: e.scalar_tensor_tensor(out=fon[:, :], in0=Oacc[:, pp, hh, :],
                                                                               scalar=fstat[:, 2:3], in1=dnw[:, :],
                                                                               op0=ALU.mult, op1=ALU.mult),
                         reads=[ok, 'fstat', 'dnw'], writes=['fon'])
                    P.op('pe', lambda e: e.transpose(out=B0[:, 0, :], in_=fon[:, :], identity=ident[:, :]),
                         reads=['fon', 'c_ident'], writes=['B0'])
                    P.op('dve', lambda e, j=j: e.tensor_tensor(out=gout[:, j * 128:(j + 1) * 128], in0=B0[:, 0, :],
                                                               in1=sdz[:, j * 128:(j + 1) * 128], op=ALU.mult),
                         reads=['B0', 'sdz'], writes=['gout'])
                P.op('pool', lambda e, hh=hh, T=T: e.dma_start(out=scr['GT'][256 + hh * 128:256 + (hh + 1) * 128, T * 512:(T + 1) * 512],
                                                             in_=gout[:, :]), reads=['gout'], writes=['GT'], dma=True)
        P.barrier()


def phase_tail(nc, P, io, gsrc, g_reads, gall=None):
    with contextlib.ExitStack() as st:
        sb = lambda name, shape, dt: st.enter_context(nc.sbuf_tensor(name, shape, dt))
        ps = lambda name, shape, dt: st.enter_context(nc.psum_tensor(name, shape, dt))
        Wn = {}
        for nm in ('w_ba', 'w_bd', 'w_ga', 'w_gd', 'w_out', 'w_pg'):
            Wn[nm] = sb("e_" + nm, [128, 8, D], BF16)
        Wpp = sb("e_wpp", [128, 2, D], BF16)
        stg = [sb("e_stg%d" % i, [128, D], F32) for i in range(2)]
        wb = sb("e_wb", [128, D], F32)
        wpost = sb("e_wpost", [128, D], F32)
        wple = sb("e_wple", [128, D], F32)
        ident = sb("e_ident", [128, 128], BF16)
        xt = [sb("e_xt%d" % i, [128, D], F32) for i in range(2)]
        xn = [sb("e_xn%d" % i, [128, D], BF16) for i in range(2)]
        stat = [sb("e_stat%d" % i, [128, 4], F32) for i in range(2)]
        st2 = sb("e_st2", [128, 8], F32)
        hT = sb("e_hT", [128, 8, 512], BF16)
        GaT = sb("e_GaT", [128, 8, 512], BF16)
        GdT = sb("e_GdT", [128, 8, 512], BF16)
        mixT = sb("e_mixT", [128, 8, 512], BF16)
        sa = sb("e_sa", [128, 512], F32)
        sd = sb("e_sd", [128, 512], F32)
        m1 = sb("e_m1", [128, 512], F32)
        m2 = sb("e_m2", [128, 512], F32)
        tmp = sb("e_tmp", [128, D], F32)
        x1 = sb("e_x1", [128, D], F32)
        x1b = sb("e_x1b", [128, D], BF16)
        x1T = sb("e_x1T", [128, 8, 128], BF16)
        pt = sb("e_pt", [128, 256], F32)
        ptb = sb("e_ptb", [128, 256], BF16)
        pT = sb("e_pT", [128, 2, 128], BF16)
        s2 = sb("e_s2", [128, D], F32)
        sq = sb("e_sq", [128, D], BF16)
        E = [ps("e_E%d" % i, [128, 512], F32) for i in range(4)]
        E45 = ps("e_E45", [128, 2, 512], F32)
        ps_tp = [ps("e_ptp%d" % i, [128, D], BF16) for i in range(2)]
        P.psum_keys(('E', 0), ('E', 1), ('E', 2), ('E', 3), 'E45', ('ps_tp', 0), ('ps_tp', 1))
        if gall is not None:
            selm = sb("e_selm", [128, 8, 128], BF16)
            cbuf = [sb("e_cb%d" % i, [128, 512], BF16) for i in range(6)]
            P.op('sp', lambda e: e.dma_start(out=selm[:, :, :], in_=io['selm'][:, :, :]), writes=['selm'], dma=True)
        ncb = [0]

        def ldb(t, src, key):
            P.op('sp', lambda e: e.dma_start(out=t, in_=src), writes=[key], dma=True)
        ldb(wb[:, :], io['norm_pre'][0:1, :].partition_broadcast(128), 'wb')
        ldb(wpost[:, :], io['norm_post'][0:1, :].partition_broadcast(128), 'wpost')
        ldb(wple[:, :], io['ple_norm'][0:1, :].partition_broadcast(128), 'wple')
        ldb(ident[:, :], io['ident'][:, :], 'ident')
        for nm in Wn:
            load_cast(P, lambda c, nm=nm: Wn[nm][:, c, :], lambda c, nm=nm: io[nm][c * 128:(c + 1) * 128, :], stg, 8, nm, D)
        load_cast(P, lambda c: Wpp[:, c, :], lambda c: io['w_pp'][c * 128:(c + 1) * 128, :], stg, 2, 'w_pp', D)

        for T in range(TOK2 // 512):
            t0 = T * 512
            for tb in range(4):
                emit_rms_hT(P, io['x2'][t0 + tb * 128:t0 + (tb + 1) * 128, :], wb, ident, xt, xn, hT, ps_tp, stat, tb, 'hT')
            for kt in range(8):
                if gall is None:
                    P.op('sp', lambda e, kt=kt, t0=t0: e.dma_start(out=GaT[:, kt, :], in_=gsrc(0, kt, t0)),
                         reads=g_reads, writes=['GaT'], dma=True)
                    P.op('sp', lambda e, kt=kt, t0=t0: e.dma_start(out=GdT[:, kt, :], in_=gsrc(1, kt, t0)),
                         reads=g_reads, writes=['GdT'], dma=True)
                    continue
                for kind, dst, dkey in ((0, GaT, 'GaT'), (1, GdT, 'GdT')):
                    bank = (kt * 2 + kind) % 4
                    for cand in range(8):
                        bb, qq = cand // 4, cand % 4
                        row0 = (bb * 4 + kt // 2) * 512 + kind * 256 + (kt % 2) * 128
                        col0 = qq * TOK2 + t0
                        ci = ncb[0] % 6
                        ncb[0] += 1
                        P.op('sp', lambda e, ci=ci, row0=row0, col0=col0: e.dma_start(
                            out=cbuf[ci][:, :], in_=gall[row0:row0 + 128, col0:col0 + 512]),
                            reads=['GALL'], writes=[('cb', ci)], dma=True)
                        P.op('pe', lambda e, ci=ci, cand=cand, bank=bank: e.matmul(
                            E[bank][:, :], lhsT=selm[:, cand, :], rhs=cbuf[ci][:, :], start=(cand == 0), stop=(cand == 7)),
                            reads=['selm', ('cb', ci)], writes=[('E', bank)])
                    if kind == 0:
                        P.op('act', lambda e, kt=kt, bank=bank, dst=dst: e.copy(out=dst[:, kt, :], in_=E[bank][:, :]),
                             reads=[('E', bank)], writes=[dkey])
                    else:
                        P.op('dve', lambda e, kt=kt, bank=bank, dst=dst: e.tensor_copy(out=dst[:, kt, :], in_=E[bank][:, :]),
                             reads=[('E', bank)], writes=[dkey])
            for fo in range(8):
                fs = slice(fo * 128, (fo + 1) * 128)
                for (bank, wn, src, skey) in ((0, 'w_ba', GaT, 'GaT'), (1, 'w_ga', hT, 'hT'), (2, 'w_bd', GdT, 'GdT'), (3, 'w_gd', hT, 'hT')):
                    for k in range(8):
                        P.op('pe', lambda e, bank=bank, wn=wn, src=src, k=k, fs=fs: e.matmul(
                            E[bank][:, :], lhsT=Wn[wn][:, k, fs], rhs=src[:, k, :], start=(k == 0), stop=(k == 7)),
                            reads=[wn, skey], writes=[('E', bank)])
                P.op('act', lambda e: e.activation(out=sa[:, :], in_=E[1][:, :], func=AF.Sigmoid), reads=[('E', 1)], writes=['sa'])
                P.op('act', lambda e: e.activation(out=sd[:, :], in_=E[3][:, :], func=AF.Sigmoid), reads=[('E', 3)], writes=['sd'])
                P.op('dve', lambda e: e.tensor_tensor(out=m1[:, :], in0=E[0][:, :], in1=sa[:, :], op=ALU.mult),
                     reads=[('E', 0), 'sa'], writes=['m1'])
                P.op('dve', lambda e: e.tensor_tensor(out=m2[:, :], in0=E[2][:, :], in1=sd[:, :], op=ALU.mult),
                     reads=[('E', 2), 'sd'], writes=['m2'])
                P.op('pool', lambda e, fo=fo: e.tensor_tensor(out=mixT[:, fo, :], in0=m1[:, :], in1=m2[:, :], op=ALU.add),
                     reads=['m1', 'm2'], writes=['mixT'])
            for tb in range(4):
                ts_ = slice(tb * 128, (tb + 1) * 128)
                r0 = t0 + tb * 128
                for half in range(2):
                    for k in range(8):
                        P.op('pe', lambda e, half=half, k=k, ts_=ts_: e.matmul(
                            E45[:, half, :], lhsT=mixT[:, k, ts_], rhs=Wn['w_out'][:, k, half * 512:(half + 1) * 512],
                            start=(k == 0), stop=(k == 7)), reads=['mixT', 'w_out'], writes=['E45'])
                for half in range(2):
                    P.op('act', lambda e, half=half: e.activation(out=sq[:, half * 512:(half + 1) * 512], in_=E45[:, half, :],
                                                                  func=AF.Square, accum_out=st2[:, half:half + 1]),
                         reads=['E45'], writes=['sq', 'st2'])
                P.op('dve', lambda e: e.tensor_tensor(out=st2[:, 2:3], in0=st2[:, 0:1], in1=st2[:, 1:2], op=ALU.add),
                     reads=['st2'], writes=['st2'])
                emit_rsqrt(P, st2[:, 4:5], st2[:, 2:3], 1.0 / D, ['st2'], ['st2'])
                P.op('sp', lambda e, r0=r0: e.dma_start(out=x1[:, :], in_=io['x2'][r0:r0 + 128, :]), writes=['x1'], dma=True)
                for half in range(2):
                    hs = slice(half * 512, (half + 1) * 512)
                    P.op('dve', lambda e, half=half, hs=hs: e.scalar_tensor_tensor(out=tmp[:, hs], in0=E45[:, half, :],
                                                                                   scalar=st2[:, 4:5], in1=wpost[:, hs],
                                                                                   op0=ALU.mult, op1=ALU.mult),
                         reads=['E45', 'st2', 'wpost'], writes=['tmp'])
                P.op('pool', lambda e: e.tensor_tensor(out=x1[:, :], in0=x1[:, :], in1=tmp[:, :], op=ALU.add),
                     reads=['x1', 'tmp'], writes=['x1'])
                P.op('pool', lambda e: e.tensor_copy(out=x1b[:, :], in_=x1[:, :]), reads=['x1'], writes=['x1b'])
                ptp = ps_tp[tb % 2]
                pk = ('ps_tp', tb % 2)
                for k in range(8):
                    P.op('pe', lambda e, k=k, ptp=ptp: e.transpose(out=ptp[:, k * 128:(k + 1) * 128],
                                                                   in_=x1b[:, k * 128:(k + 1) * 128], identity=ident[:, :]),
                         reads=['x1b', 'ident'], writes=[pk])
                P.op('act', lambda e, ptp=ptp: e.copy(out=x1T[:, :, :], in_=ptp[:, :].rearrange("p (k t) -> p k t", k=8)),
                     reads=[pk], writes=['x1T'])
                P.op('sp', lambda e, r0=r0: e.dma_start(out=pt[:, :], in_=io['p2'][r0:r0 + 128, :]), writes=['pt'], dma=True)
                P.op('dve', lambda e: e.tensor_copy(out=ptb[:, :], in_=pt[:, :]), reads=['pt'], writes=['ptb'])
                ptp2 = ps_tp[(tb + 1) % 2]
                pk2 = ('ps_tp', (tb + 1) % 2)
                for k in range(2):
                    P.op('pe', lambda e, k=k, ptp2=ptp2: e.transpose(out=ptp2[:, k * 128:(k + 1) * 128],
                                                                     in_=ptb[:, k * 128:(k + 1) * 128], identity=ident[:, :]),
                         reads=['ptb', 'ident'], writes=[pk2])
                P.op('act', lambda e, ptp2=ptp2: e.copy(out=pT[:, :, :], in_=ptp2[:, 0:256].rearrange("p (k t) -> p k t", k=2)),
                     reads=[pk2], writes=['pT'])
                for half in range(2):
                    hs = slice(half * 512, (half + 1) * 512)
                    for k in range(8):
                        P.op('pe', lambda e, half=half, k=k, hs=hs: e.matmul(E45[:, half, :], lhsT=x1T[:, k, :], rhs=Wn['w_pg'][:, k, hs],
                                                                             start=(k == 0), stop=(k == 7)),
                             reads=['x1T', 'w_pg'], writes=['E45'])
                    for k in range(2):
                        P.op('pe', lambda e, half=half, k=k, hs=hs: e.matmul(E[half][:, :], lhsT=pT[:, k, :], rhs=Wpp[:, k, hs],
                                                                             start=(k == 0), stop=(k == 1)),
                             reads=['pT', 'w_pp'], writes=[('E', half)])
                for half in range(2):
                    hs = slice(half * 512, (half + 1) * 512)
                    P.op('act', lambda e, half=half, hs=hs: e.activation(out=s2[:, hs], in_=E45[:, half, :], func=AF.Sigmoid),
                         reads=['E45'], writes=['s2'])
                    P.op('dve', lambda e, half=half, hs=hs: e.tensor_tensor(out=s2[:, hs], in0=E[half][:, :], in1=s2[:, hs], op=ALU.mult),
                         reads=[('E', half), 's2'], writes=['s2'])
                P.op('act', lambda e: e.activation(out=sq[:, :], in_=s2[:, :], func=AF.Square, accum_out=st2[:, 5:6]),
                     reads=['s2'], writes=['sq', 'st2b'])
                emit_rsqrt(P, st2[:, 7:8], st2[:, 5:6], 1.0 / D, ['st2b'], ['st2b'])
                P.op('dve', lambda e: e.scalar_tensor_tensor(out=tmp[:, :], in0=s2[:, :], scalar=st2[:, 7:8], in1=wple[:, :],
                                                             op0=ALU.mult, op1=ALU.mult), reads=['s2', 'st2b', 'wple'], writes=['tmp'])
                P.op('pool', lambda e: e.tensor_tensor(out=tmp[:, :], in0=tmp[:, :], in1=x1[:, :], op=ALU.add),
                     reads=['tmp', 'x1'], writes=['tmp'])
                P.op('pool', lambda e, r0=r0: e.dma_start(out=io['out'][r0:r0 + 128, :], in_=tmp[:, :]), reads=['tmp'], writes=['out'], dma=True)
        P.barrier()


P1_INPUTS = [('x', [S, D], F32), ('w_in', [D, WCOLS], F32), ('norm_pre', [1, D], F32), ('nw', [128, 4], F32),
             ('cw', [128, 6, 5], F32), ('gcst', [1, 8], F32), ('cos', [128, S], F32), ('sin', [128, S], F32),
             ('ident', [128, 128], BF16), ('ones', [128, 128], BF16), ('TRI4', [128, 4, 128], F32),
             ('SM4', [128, 4, 128], F32), ('IM4', [128, 4, 128], F32), ('I4', [128, 4, 128], F32),
             ('ONESF', [128, 128], F32), ('BLK', [128, 128], F32), ('dn_norm', [1, 128], F32)]
P2_INPUTS = [('x2', [TOK2, D], F32), ('p2', [TOK2, 256], F32), ('norm_pre', [1, D], F32), ('norm_post', [1, D], F32),
             ('ple_norm', [1, D], F32), ('ident', [128, 128], BF16), ('w_ba', [D, D], F32), ('w_bd', [D, D], F32),
             ('w_ga', [D, D], F32), ('w_gd', [D, D], F32), ('w_out', [D, D], F32), ('w_pg', [D, D], F32),
             ('w_pp', [256, D], F32)]


DEBUG_SCR = False
NT_LIM = None
ROPE_LVL = 9
DN_STEPS = None
DN_LVL = 9
ROPE_VAR = 0
PARTS = ('rope', 'silu', 'conv', 'tm')
PHASES = ('proj', 'attn', 'dn')


def _scratch(nc):
    scr = {}
    for nm, shape, dt in (('QT', [2, 128, S], BF16), ('KT', [1, 128, S], BF16), ('V', [S, 128], BF16),
                          ('SAZ', [2, 128, S], BF16), ('SDZ', [2, 128, S], BF16), ('DQ', [2, 128, S], BF16),
                          ('DK', [2, 128, S], BF16), ('DV', [2, 128, S], BF16), ('GB', [S, 8], F32)):
        if DEBUG_SCR:
            scr[nm] = nc.dram_tensor("scr_" + nm, shape, dt, kind="ExternalOutput")
        else:
            scr[nm] = nc.dram_tensor("scr_" + nm, shape, dt)
    return scr


def build(mode):
    nc = bass.Bass("TRN2", target_bir_lowering=False)
    io = {}
    names = []
    if mode in ('p1', 'fused'):
        names += P1_INPUTS
    if mode in ('p2', 'fused'):
        names += [n for n in P2_INPUTS if n[0] not in [m[0] for m in names]]
    for nm, shape, dt in names:
        io[nm] = nc.dram_tensor(nm, shape, dt, kind="ExternalInput")
    with contextlib.ExitStack() as stack:
        P = Prog(nc, stack)
        if mode == 'p1':
            scr = _scratch(nc)
            scr['GT'] = nc.dram_tensor("GT", [512, S], BF16, kind="ExternalOutput")
            if 'proj' in PHASES:
                phase_proj(nc, P, io, scr)
            if 'attn' in PHASES:
                phase_attn(nc, P, io, scr)
            if 'dn' in PHASES:
                phase_dn(nc, P, io, scr)
        elif mode == 'fused':
            scr = _scratch(nc)
            scr['GT'] = nc.dram_tensor("GT_int", [512, S], BF16)
            gall = nc.dram_tensor("GALL_int", [8 * 512, S], BF16)
            io['selm'] = nc.dram_tensor("selm", [128, 8, 128], BF16, kind="ExternalInput")
            io['out'] = nc.dram_tensor("out", [TOK2, D], F32, kind="ExternalOutput")
            phase_proj(nc, P, io, scr)
            phase_attn(nc, P, io, scr)
            phase_dn(nc, P, io, scr)
            P.op('pool', lambda e: e.collective_compute("AllGather", ALU.bypass, replica_groups=[list(range(8))],
                                                        ins=[scr['GT'].ap().opt()], outs=[gall.ap().opt()]),
                 reads=['GT'], writes=['GALL'], dma=True, inc=1)
            phase_tail(nc, P, io, None, None, gall=gall)
        elif mode == 'p2':
            io['G2'] = nc.dram_tensor("G2", [16, 128, TOK2], BF16, kind="ExternalInput")
            io['out'] = nc.dram_tensor("out", [TOK2, D], F32, kind="ExternalOutput")
            phase_tail(nc, P, io, lambda kind, kt, t0: io['G2'][kind * 8 + kt, :, t0:t0 + 512], [])
        P.emit()
    return nc


def _consts():
    bf = ml_dtypes.bfloat16
    idx = np.arange(128)
    same = (idx[:, None] // 64) == (idx[None, :] // 64)
    k = idx[:, None]
    i = idx[None, :]
    tri_f = (same & (k <= i)).astype(np.float32)
    tri_b = (same & (k >= i)).astype(np.float32)
    smt_f = (same & (i > k)).astype(np.float32)
    smt_b = (same & (i < k)).astype(np.float32)
    imt_f = (same & (i >= k)).astype(np.float32)
    imt_b = (same & (i <= k)).astype(np.float32)
    eye = np.eye(128, dtype=np.float32)
    st4 = lambda a, b: np.ascontiguousarray(np.stack([a, a, b, b], axis=1))
    c = {
        'ident': np.eye(128, dtype=np.float32).astype(bf),
        'ones': np.ones((128, 128), np.float32).astype(bf),
        'TRI4': st4(tri_f, tri_b), 'SM4': st4(smt_f, smt_b), 'IM4': st4(imt_f, imt_b), 'I4': st4(eye, eye),
        'ONESF': np.ones((128, 128), np.float32), 'BLK': same.astype(np.float32),
    }
    t = np.arange(S)
    row = (t // 64).astype(np.float32)
    col = (t % 64).astype(np.float32)
    inv = (np.float32(10000.0) ** (-np.arange(32, dtype=np.float32) / np.float32(32))).astype(np.float32)
    ang = np.concatenate([row[:, None] * inv[None, :], col[:, None] * inv[None, :]], axis=1).astype(np.float32)
    cosd = np.repeat(np.cos(ang.astype(np.float64)), 2, axis=1).T.astype(np.float32)
    sind = np.repeat(np.sin(ang.astype(np.float64)), 2, axis=1).T.astype(np.float32)
    sign = np.where(np.arange(128) % 2 == 0, -1.0, 1.0).astype(np.float32)[:, None]
    c['cos'] = np.ascontiguousarray(cosd)
    c['sin'] = np.ascontiguousarray(sind * sign)
    return c


def _p1_inputs(c, inputs, consts):
    b, j = c // 4, c % 4
    w_in = inputs['w_in'][0]
    o = np.cumsum([0, 1024, 256, 256, 1024, 1024, 1024, 1024, 16, 16, 1024, 1024, 1024])
    aq, ak, av, az, dq, dk, dv, db, da, dz = [w_in[:, o[i]:o[i + 1]] for i in range(10)]
    kv = j // 2
    sw = np.arange(128) ^ 1
    hs = [2 * j, 2 * j + 1]
    col = lambda m, h: m[:, h * 128:(h + 1) * 128]
    groups = [col(aq, hs[0]), col(aq, hs[1]), col(aq, hs[0])[:, sw], col(aq, hs[1])[:, sw],
              col(ak, kv), col(ak, kv)[:, sw], col(az, hs[0]), col(az, hs[1]),
              col(dq, hs[0]), col(dq, hs[1]), col(dk, hs[0]), col(dk, hs[1]), col(dv, hs[0]), col(dv, hs[1]),
              col(dz, hs[0]), col(dz, hs[1]), col(av, kv)]
    sel = [0 * 8 + hs[0], 0 * 8 + hs[1], 1 * 8 + hs[0], 1 * 8 + hs[1]]
    groups += [db[:, sel], da[:, sel]]
    wsl = np.ascontiguousarray(np.concatenate(groups, axis=1))
    qn, kn = inputs['q_norm'][0], inputs['k_norm'][0]
    nw = np.stack([qn, qn[sw], kn, kn[sw]], axis=1).astype(np.float32)
    conv = inputs['conv_w'][0]
    cg = []
    for base in (0, 1024, 2048):
        for h in hs:
            cg.append(conv[:, base + h * 128: base + (h + 1) * 128].T)
    cw = np.ascontiguousarray(np.stack(cg, axis=1)).astype(np.float32)
    dtb = inputs['dt_bias'][0].reshape(16)[sel]
    alog = inputs['a_log'][0].reshape(16)[sel]
    gcst = np.concatenate([dtb, alog])[None, :].astype(np.float32)
    d = {'x': np.ascontiguousarray(inputs['x'][b]), 'w_in': wsl, 'norm_pre': inputs['norm_pre'], 'nw': np.ascontiguousarray(nw),
         'cw': cw, 'gcst': gcst, 'dn_norm': inputs['dn_norm']}
    for k_ in ('cos', 'sin', 'ident', 'ones', 'TRI4', 'SM4', 'IM4', 'I4', 'ONESF', 'BLK'):
        d[k_] = consts[k_]
    return d


def _p2_inputs(c, inputs, consts):
    b, q = c // 4, c % 4
    w_in = inputs['w_in'][0]
    tok = slice(q * TOK2, (q + 1) * TOK2)
    return {'x2': np.ascontiguousarray(inputs['x'][b, tok]), 'p2': np.ascontiguousarray(inputs['p'][0, b, tok]),
            'norm_pre': inputs['norm_pre'], 'norm_post': inputs['norm_post'], 'ple_norm': inputs['ple_norm'],
            'ident': consts['ident'], 'w_ba': inputs['w_br_att'][0], 'w_bd': inputs['w_br_dn'][0],
            'w_ga': np.ascontiguousarray(w_in[:, 6688:7712]), 'w_gd': np.ascontiguousarray(w_in[:, 7712:8736]),
            'w_out': inputs['w_out'][0], 'w_pg': inputs['w_ple_gate'][0], 'w_pp': inputs['w_ple_proj'][0]}


FUSED = True


def kernel(**inputs):
    inputs = {k: np.asarray(v) for k, v in inputs.items()}
    consts = _consts()
    if FUSED:
        maps = []
        eye = np.eye(128, dtype=np.float32)
        for c in range(8):
            d = _p1_inputs(c, inputs, consts)
            d.update(_p2_inputs(c, inputs, consts))
            selm = np.zeros((128, 8, 128), np.float32)
            selm[:, c, :] = eye
            d['selm'] = selm.astype(ml_dtypes.bfloat16)
            maps.append(d)
        nc = build('fused')
        r = run_bass_kernel_spmd(nc, maps, core_ids=list(range(8)))
        out = np.zeros((2, S, D), np.float32)
        for c in range(8):
            b, q = c // 4, c % 4
            out[b, q * TOK2:(q + 1) * TOK2] = np.asarray(r.results[c]['out'])
        return out
    nc1 = build('p1')
    r1 = run_bass_kernel_spmd(nc1, [_p1_inputs(c, inputs, consts) for c in range(8)], core_ids=list(range(8)))
    GT = [np.asarray(r1.results[c]['GT']) for c in range(8)]
    in2 = []
    for c in range(8):
        b, q = c // 4, c % 4
        d = _p2_inputs(c, inputs, consts)
        tok = slice(q * TOK2, (q + 1) * TOK2)
        tiles = []
        for kind in range(2):
            for kt in range(8):
                src = GT[b * 4 + kt // 2]
                r0 = kind * 256 + (kt % 2) * 128
                tiles.append(src[r0:r0 + 128, tok])
        d['G2'] = np.ascontiguousarray(np.stack(tiles, axis=0))
        in2.append(d)
    nc2 = build('p2')
    r2 = run_bass_kernel_spmd(nc2, in2, core_ids=list(range(8)))
    out = np.zeros((2, S, D), np.float32)
    for c in range(8):
        b, q = c // 4, c % 4
        out[b, q * TOK2:(q + 1) * TOK2] = np.asarray(r2.results[c]['out'])
    return out
```

```python
import contextlib
import numpy as np
import ml_dtypes
import concourse.bass as bass
import concourse.mybir as mybir
from concourse.bass_utils import run_bass_kernel_spmd

F32 = mybir.dt.float32
BF16 = mybir.dt.bfloat16
I32 = mybir.dt.int32
AF = mybir.ActivationFunctionType
ALU = mybir.AluOpType

S = 8192
D = 1024
NT = S // 512
NPAIR = S // 128
EPS = 1e-6
NFM = 16
NTM = 136
WCOLS = NFM * 128 + NTM
TOK2 = 2048
NDMA_SEMS = 40


class Prog:
    NGEN = 6

    def __init__(self, nc, stack):
        self.nc = nc
        self.eng = {'pe': nc.tensor, 'act': nc.scalar, 'dve': nc.vector, 'pool': nc.gpsimd, 'sp': nc.sync}
        self.lists = {e: [] for e in self.eng}
        self.sems = {}
        for g in range(self.NGEN):
            for e in ('pe', 'act', 'dve', 'pool'):
                self.sems[('c', e, g)] = stack.enter_context(nc.semaphore("c_%s_%d" % (e, g)))
        self.dma = []
        for i in range(NDMA_SEMS):
            self.sems[('d', i)] = stack.enter_context(nc.semaphore("d_%d" % i))
            self.dma.append(0)
        self.pools = {'sp': list(range(0, 28)), 'pool': list(range(28, NDMA_SEMS)), 'act': list(range(28, NDMA_SEMS))}
        self.rr = {'sp': 0, 'pool': 0, 'act': 0}
        self.gen = 0
        self.cnt = {e: 0 for e in self.eng}
        self.seen = {e: {} for e in self.eng}
        self.lastw = {}
        self.readers = {}
        self.excl = set()

    def psum_keys(self, *keys):
        self.excl.update(keys)

    @staticmethod
    def _owner(tok):
        s = tok[0]
        return s[1] if s[0] == 'c' else None

    def _need(self, engine, tok, waits):
        s, v = tok
        if s[0] == 'c' and s[2] < self.gen:
            return
        if self.seen[engine].get(s, 0) < v:
            self.seen[engine][s] = v
            waits.append((s, v))

    def op(self, engine, fn, reads=(), writes=(), dma=False, inc=16):
        waits = []
        same_sync = engine in ('act', 'dve', 'pool')
        for k in list(reads) + list(writes):
            t = self.lastw.get(k)
            if t is not None and (self._owner(t) != engine or same_sync):
                self._need(engine, t, waits)
        for k in list(writes) + [k for k in reads if k in self.excl]:
            for t in self.readers.get(k, ()):
                if self._owner(t) != engine:
                    self._need(engine, t, waits)
        if dma:
            pl = self.pools[engine]
            i = pl[self.rr[engine] % len(pl)]
            self.rr[engine] += 1
            s = ('d', i)
            if self.dma[i] > 0:
                self._need(engine, (s, self.dma[i]), waits)
            self.dma[i] += inc
            tok = (s, self.dma[i])
            incv = inc
        else:
            self.cnt[engine] += 1
            s = ('c', engine, self.gen)
            tok = (s, self.cnt[engine])
            incv = 1
        self.lists[engine].append((waits, fn, s, incv))
        for k in writes:
            self.lastw[k] = tok
            self.readers[k] = []
        for k in reads:
            self.readers.setdefault(k, []).append(tok)
        return tok

    def barrier(self):
        toks = [(('c', e, self.gen), self.cnt[e]) for e in ('pe', 'act', 'dve', 'pool') if self.cnt[e] > 0]
        toks += [(('d', i), v) for i, v in enumerate(self.dma) if v > 0]
        for e in self.eng:
            waits = []
            for t in toks:
                if self._owner(t) != e:
                    self._need(e, t, waits)
            if waits:
                self.lists[e].append((waits, None, None, 0))
        self.gen += 1
        assert self.gen < self.NGEN
        for e in self.cnt:
            self.cnt[e] = 0

    def emit(self):
        nc = self.nc
        with nc.Block() as block:
            def run(ename):
                def body(eng):
                    for waits, fn, s, incv in self.lists[ename]:
                        for (ws, wv) in waits:
                            eng.wait_ge(self.sems[ws], wv)
                        if fn is not None:
                            fn(eng).then_inc(self.sems[s], incv)
                return body
            block.tensor(run('pe'))
            block.scalar(run('act'))
            block.vector(run('dve'))
            block.gpsimd(run('pool'))
            block.sync(run('sp'))


def emit_rsqrt(P, out_ap, in_ap, scale, reads, writes):
    P.op('act', lambda e: e.activation(out=out_ap, in_=in_ap, func=AF.Ln, bias=EPS, scale=scale), reads=reads, writes=writes)
    P.op('act', lambda e: e.activation(out=out_ap, in_=out_ap, func=AF.Exp, scale=-0.5), reads=writes, writes=writes)


def emit_rms_hT(P, x_src, wb, ident, xt, xn, hT, ps_tp, stat, tb, key):
    xs = xt[tb % len(xt)]
    xk = ('xt', tb % len(xt))
    P.op('sp', lambda e: e.dma_start(out=xs[:, :], in_=x_src), writes=[xk], dma=True)
    sq = xn[tb % 2]
    nk = ('xn', tb % 2)
    st = stat[tb % 2]
    sk = ('stat', tb % 2)
    P.op('act', lambda e: e.activation(out=sq[:, :], in_=xs[:, :], func=AF.Square, accum_out=st[:, 0:1]),
         reads=[xk], writes=[nk, sk])
    emit_rsqrt(P, st[:, 2:3], st[:, 0:1], 1.0 / D, [sk], [sk])
    P.op('dve', lambda e: e.scalar_tensor_tensor(out=sq[:, :], in0=xs[:, :], scalar=st[:, 2:3], in1=wb[:, :],
                                                 op0=ALU.mult, op1=ALU.mult), reads=[xk, sk, 'wb'], writes=[nk])
    pk = ('ps_tp', tb % 2)
    pt = ps_tp[tb % 2]
    for k in range(8):
        P.op('pe', lambda e, k=k: e.transpose(out=pt[:, k * 128:(k + 1) * 128], in_=sq[:, k * 128:(k + 1) * 128],
                                              identity=ident[:, :]), reads=[nk, 'ident'], writes=[pk])
    P.op('act', lambda e: e.copy(out=hT[:, :, tb * 128:(tb + 1) * 128],
                                 in_=pt[:, :].rearrange("p (k t) -> p k t", k=8)), reads=[pk], writes=[key])


def load_cast(P, dst_ap_fn, src_ap_fn, stg, nchunks, dkey, width):
    for c in range(nchunks):
        sl = c % 2
        P.op('sp', lambda e, c=c, sl=sl: e.dma_start(out=stg[sl][:, 0:width], in_=src_ap_fn(c)),
             writes=[('stg', sl)], dma=True)
        eng = 'dve' if c % 2 == 0 else 'pool'
        P.op(eng, lambda e, c=c, sl=sl: e.tensor_copy(out=dst_ap_fn(c), in_=stg[sl][:, 0:width]),
             reads=[('stg', sl)], writes=[dkey])


def phase_proj(nc, P, io, scr):
    with contextlib.ExitStack() as st:
        sb = lambda name, shape, dt: st.enter_context(nc.sbuf_tensor(name, shape, dt))
        ps = lambda name, shape, dt: st.enter_context(nc.psum_tensor(name, shape, dt))
        W = sb("a_W", [128, 8, WCOLS], BF16)
        stg = [sb("a_stg%d" % i, [128, WCOLS], F32) for i in range(2)]
        wb = sb("a_wb", [128, D], F32)
        ident = sb("a_ident", [128, 128], BF16)
        ones = sb("a_ones", [128, 128], BF16)
        xt = [sb("a_xt%d" % i, [128, D], F32) for i in range(4)]
        xn = [sb("a_xn%d" % i, [128, D], BF16) for i in range(2)]
        stat = [sb("a_stat%d" % i, [128, 4], F32) for i in range(2)]
        hT = [sb("a_hT%d" % i, [128, 8, 512], BF16) for i in range(2)]
        cs = [sb("a_cs%d" % i, [128, 2, 512], F32) for i in range(2)]
        nw = sb("a_nw", [128, 4], F32)
        cw = sb("a_cw", [128, 6, 5], F32)
        cstg = [sb("a_cstg%d" % g, [128, 520], F32) for g in range(6)]
        cacc = [sb("a_cacc%d" % i, [128, 512], F32) for i in range(6)]
        csil = [sb("a_csil%d" % i, [128, 512], F32) for i in range(6)]
        sqb = [sb("a_sqb%d" % i, [128, 512], BF16) for i in range(7)]
        rstd = [sb("a_rstd%d" % i, [128, 512], F32) for i in range(7)]
        t1 = [sb("a_t1%d" % i, [128, 512], F32) for i in range(3)]
        t2 = [sb("a_t2%d" % i, [128, 512], F32) for i in range(3)]
        ob = [sb("a_ob%d" % i, [128, 512], BF16) for i in range(8)]
        vtm = [sb("a_vtm%d" % i, [128, 128], BF16) for i in range(2)]
        gsm = [sb("a_gsm%d" % i, [128, 32], F32) for i in range(2)]
        cst = sb("a_cst", [128, 8], F32)
        ps_tp = [ps("a_ptp%d" % i, [128, D], BF16) for i in range(2)]
        ps_fm = [ps("a_pfm%d" % i, [128, 512], F32) for i in range(5)]
        ps_tm = ps("a_ptm", [128, 512], F32)
        P.psum_keys(('pfm', 0), ('pfm', 1), ('pfm', 2), ('pfm', 3), ('pfm', 4), 'ptm', ('ps_tp', 0), ('ps_tp', 1))

        P.op('sp', lambda e: e.dma_start(out=wb[:, :], in_=io['norm_pre'][0:1, :].partition_broadcast(128)),
             writes=['wb'], dma=True)
        P.op('sp', lambda e: e.dma_start(out=ident[:, :], in_=io['ident'][:, :]), writes=['ident'], dma=True)
        P.op('sp', lambda e: e.dma_start(out=ones[:, :], in_=io['ones'][:, :]), writes=['ones'], dma=True)
        P.op('sp', lambda e: e.dma_start(out=nw[:, :], in_=io['nw'][:, :]), writes=['nw'], dma=True)
        P.op('sp', lambda e: e.dma_start(out=cw[:, :, :], in_=io['cw'][:, :, :]), writes=['cw'], dma=True)
        P.op('sp', lambda e: e.dma_start(out=cst[:, :], in_=io['gcst'][0:1, :].partition_broadcast(128)),
             writes=['cst'], dma=True)
        P.op('act', lambda e: e.activation(out=cst[:, 4:8], in_=cst[:, 4:8], func=AF.Exp), reads=['cst'], writes=['cst'])
        P.op('dve', lambda e: e.tensor_scalar(out=cst[:, 4:8], in0=cst[:, 4:8], scalar1=-1.0, scalar2=None,
                                              op0=ALU.mult), reads=['cst'], writes=['cst'])
        for g in range(6):
            P.op('pool', lambda e, g=g: e.memset(cstg[g][:, :], 0.0), writes=[('cstg', g)])
        load_cast(P, lambda c: W[:, c, :], lambda c: io['w_in'][c * 128:(c + 1) * 128, :], stg, 8, 'W', WCOLS)

        rope_pairs = [(0, 2, 0, ('QT', 0)), (1, 3, 0, ('QT', 1)), (4, 5, 2, ('KT', 0))]
        silu_groups = [(6, ('SAZ', 0)), (7, ('SAZ', 1)), (14, ('SDZ', 0)), (15, ('SDZ', 1))]
        conv_groups = [(8, 'DQ', 0), (9, 'DQ', 1), (10, 'DK', 0), (11, 'DK', 1), (12, 'DV', 0), (13, 'DV', 1)]
        cnt = {'fm': 0, 'ob': 0}
        NFMS = len(ps_fm)
        NOB = len(ob)

        def fm_slot():
            slot = cnt['fm'] % NFMS
            cnt['fm'] += 1
            return slot

        def fm_matmul(g, h):
            slot = fm_slot()
            for k in range(8):
                P.op('pe', lambda e, k=k, slot=slot: e.matmul(ps_fm[slot][:, :], lhsT=W[:, k, g * 128:(g + 1) * 128],
                                                              rhs=h[0][:, k, :], start=(k == 0), stop=(k == 7)),
                     reads=['W', h[1]], writes=[('pfm', slot)])
            return slot

        def next_ob():
            i = cnt['ob'] % NOB
            cnt['ob'] += 1
            return i

        def conv_stages(ncols, ocol0, tok0):
            n = ncols - ocol0
            for gi in range(6):
                P.op('pool', lambda e, gi=gi: e.tensor_scalar(out=cacc[gi][:, 0:ncols], in0=cstg[gi][:, 0:ncols],
                                                              scalar1=cw[:, gi, 0:1], scalar2=None, op0=ALU.mult),
                     reads=[('cstg', gi), 'cw'], writes=[('cacc', gi)])
            for k in range(1, 5):
                for gi in range(6):
                    P.op('dve', lambda e, gi=gi, k=k: e.scalar_tensor_tensor(out=cacc[gi][:, 0:ncols], in0=cstg[gi][:, k:k + ncols],
                                                                             scalar=cw[:, gi, k:k + 1], in1=cacc[gi][:, 0:ncols],
                                                                             op0=ALU.mult, op1=ALU.add),
                         reads=[('cstg', gi), 'cw', ('cacc', gi)], writes=[('cacc', gi)])
            for gi in range(6):
                P.op('act', lambda e, gi=gi: e.activation(out=csil[gi][:, 0:ncols], in_=cacc[gi][:, 0:ncols], func=AF.Silu),
                     reads=[('cacc', gi)], writes=[('csil', gi)])
            for gi in range(4):
                P.op('act', lambda e, gi=gi: e.activation(out=sqb[3 + gi][:, 0:ncols], in_=csil[gi][:, 0:ncols], func=AF.Square),
                     reads=[('csil', gi)], writes=[('sqb', 3 + gi)])
            slots = []
            for gi in range(4):
                slot = fm_slot()
                slots.append(slot)
                P.op('pe', lambda e, gi=gi, slot=slot: e.matmul(ps_fm[slot][:, 0:ncols], lhsT=ones[:, :], rhs=sqb[3 + gi][:, 0:ncols],
                                                                start=True, stop=True),
                     reads=['ones', ('sqb', 3 + gi)], writes=[('pfm', slot)])
            for gi in range(4):
                P.op('act', lambda e, gi=gi, slot=slots[gi]: e.activation(out=rstd[3 + gi][:, 0:ncols], in_=ps_fm[slot][:, 0:ncols],
                                                                          func=AF.Ln, bias=EPS, scale=1.0),
                     reads=[('pfm', slots[gi])], writes=[('rstd', 3 + gi)])
            for gi in range(4):
                P.op('act', lambda e, gi=gi: e.activation(out=rstd[3 + gi][:, 0:ncols], in_=rstd[3 + gi][:, 0:ncols],
                                                          func=AF.Exp, scale=-0.5),
                     reads=[('rstd', 3 + gi)], writes=[('rstd', 3 + gi)])
            for gi, (g, name, hh) in enumerate(conv_groups):
                oi = next_ob()
                if name == 'DV':
                    P.op('dve', lambda e, gi=gi, oi=oi: e.tensor_copy(out=ob[oi][:, 0:n], in_=csil[gi][:, ocol0:ncols]),
                         reads=[('csil', gi)], writes=[('ob', oi)])
                else:
                    sc = (128.0 ** -0.5) if name == 'DQ' else 1.0
                    P.op('dve', lambda e, gi=gi, oi=oi, sc=sc: e.scalar_tensor_tensor(
                        out=ob[oi][:, 0:n], in0=csil[gi][:, ocol0:ncols], scalar=sc, in1=rstd[3 + gi][:, ocol0:ncols],
                        op0=ALU.mult, op1=ALU.mult), reads=[('csil', gi), ('rstd', 3 + gi)], writes=[('ob', oi)])
                P.op('pool', lambda e, oi=oi, name=name, hh=hh: e.dma_start(out=scr[name][hh, :, tok0:tok0 + n], in_=ob[oi][:, 0:n]),
                     reads=[('ob', oi)], writes=[(name, hh)], dma=True)

        for T in range(NT if NT_LIM is None else NT_LIM):
            t0 = T * 512
            h = (hT[T % 2], ('hT', T % 2))
            for tb in range(4):
                emit_rms_hT(P, io['x'][t0 + tb * 128:t0 + (tb + 1) * 128, :], wb, ident, xt, xn, h[0], ps_tp, stat,
                            tb, h[1])
            c2 = cs[T % 2]
            ck2 = ('cs', T % 2)
            P.op('sp', lambda e, c2=c2, t0=t0: e.dma_start(out=c2[:, 0, :], in_=io['cos'][:, t0:t0 + 512]),
                 writes=[ck2], dma=True)
            P.op('sp', lambda e, c2=c2, t0=t0: e.dma_start(out=c2[:, 1, :], in_=io['sin'][:, t0:t0 + 512]),
                 writes=[ck2], dma=True)
            for gi, (g, name, hh) in enumerate(conv_groups):
                sa = fm_matmul(g, h)
                P.op('act', lambda e, sa=sa, gi=gi: e.copy(out=cstg[gi][:, 4:516], in_=ps_fm[sa][:, :]),
                     reads=[('pfm', sa)], writes=[('cstg', gi)])
            for (g, (dn, hh)) in silu_groups:
                sa = fm_matmul(g, h)
                oi = next_ob()
                P.op('act', lambda e, sa=sa, oi=oi: e.activation(out=ob[oi][:, :], in_=ps_fm[sa][:, :], func=AF.Silu),
                     reads=[('pfm', sa)], writes=[('ob', oi)])
                P.op('pool', lambda e, oi=oi, dn=dn, hh=hh, t0=t0: e.dma_start(out=scr[dn][hh, :, t0:t0 + 512],
                                                                               in_=ob[oi][:, :]),
                     reads=[('ob', oi)], writes=[(dn, hh)], dma=True)
            for pi, (g, gs, wc, (dn, hh)) in enumerate(rope_pairs):
                sa = fm_matmul(g, h)
                sbk = fm_matmul(gs, h)
                P.op('act', lambda e, sa=sa, pi=pi: e.activation(out=sqb[pi][:, :], in_=ps_fm[sa][:, :], func=AF.Square),
                     reads=[('pfm', sa)], writes=[('sqb', pi)])
                P.op('dve', lambda e, sa=sa, pi=pi, wc=wc, c2=c2: e.scalar_tensor_tensor(
                    out=t1[pi][:, :], in0=ps_fm[sa][:, :], scalar=nw[:, wc:wc + 1], in1=c2[:, 0, :],
                    op0=ALU.mult, op1=ALU.mult), reads=[('pfm', sa), 'nw', ck2], writes=[('t1', pi)])
                P.op('dve', lambda e, sbk=sbk, pi=pi, wc=wc, c2=c2: e.scalar_tensor_tensor(
                    out=t2[pi][:, :], in0=ps_fm[sbk][:, :], scalar=nw[:, wc + 1:wc + 2], in1=c2[:, 1, :],
                    op0=ALU.mult, op1=ALU.mult), reads=[('pfm', sbk), 'nw', ck2], writes=[('t2', pi)])
            conv_stages(512, 2 if T == 0 else 0, 0 if T == 0 else t0 - 2)
            rslots = []
            for pi in range(3):
                slot = fm_slot()
                rslots.append(slot)
                P.op('pe', lambda e, pi=pi, slot=slot: e.matmul(ps_fm[slot][:, :], lhsT=ones[:, :], rhs=sqb[pi][:, :],
                                                                start=True, stop=True),
                     reads=['ones', ('sqb', pi)], writes=[('pfm', slot)])
            for pi in range(3):
                P.op('pool', lambda e, pi=pi: e.tensor_tensor(out=t1[pi][:, :], in0=t1[pi][:, :], in1=t2[pi][:, :], op=ALU.add),
                     reads=[('t1', pi), ('t2', pi)], writes=[('t1', pi)])
            for pi in range(3):
                P.op('act', lambda e, pi=pi, slot=rslots[pi]: e.activation(out=rstd[pi][:, :], in_=ps_fm[slot][:, :], func=AF.Ln,
                                                                           bias=EPS, scale=1.0 / 128),
                     reads=[('pfm', rslots[pi])], writes=[('rstd', pi)])
            for pi in range(3):
                P.op('act', lambda e, pi=pi: e.activation(out=rstd[pi][:, :], in_=rstd[pi][:, :], func=AF.Exp, scale=-0.5),
                     reads=[('rstd', pi)], writes=[('rstd', pi)])
            for pi, (g, gs, wc, (dn, hh)) in enumerate(rope_pairs):
                oi = next_ob()
                P.op('dve', lambda e, pi=pi, oi=oi: e.tensor_tensor(out=ob[oi][:, :], in0=t1[pi][:, :], in1=rstd[pi][:, :], op=ALU.mult),
                     reads=[('t1', pi), ('rstd', pi)], writes=[('ob', oi)])
                P.op('pool', lambda e, oi=oi, dn=dn, hh=hh, t0=t0: e.dma_start(out=scr[dn][hh, :, t0:t0 + 512], in_=ob[oi][:, :]),
                     reads=[('ob', oi)], writes=[(dn, hh)], dma=True)
            for gi in range(6):
                P.op('pool', lambda e, gi=gi: e.tensor_copy(out=cstg[gi][:, 0:4], in_=cstg[gi][:, 512:516]),
                     reads=[('cstg', gi)], writes=[('cstg', gi)])
            if T == NT - 1:
                for gi in range(6):
                    P.op('pool', lambda e, gi=gi: e.tensor_copy(out=cstg[gi][:, 0:66], in_=cstg[gi][:, 450:516]),
                         reads=[('cstg', gi)], writes=[('cstg', gi)])
                    P.op('pool', lambda e, gi=gi: e.memset(cstg[gi][:, 66:72], 0.0), writes=[('cstg', gi)])
                conv_stages(64, 0, S - 64)

            for tb in (range(4) if 'tm' in PARTS else []):
                for k in range(8):
                    P.op('pe', lambda e, k=k, tb=tb, h=h: e.matmul(ps_tm[:, 0:NTM], lhsT=h[0][:, k, tb * 128:(tb + 1) * 128],
                                                                   rhs=W[:, k, NFM * 128:WCOLS], start=(k == 0), stop=(k == 7)),
                         reads=['W', h[1]], writes=['ptm'])
                i2 = tb % 2
                vk = ('vtm', i2)
                P.op('act', lambda e, i2=i2: e.copy(out=vtm[i2][:, :], in_=ps_tm[:, 0:128]), reads=['ptm'], writes=[vk])
                r0 = t0 + tb * 128
                P.op('pool', lambda e, i2=i2, r0=r0: e.dma_start(out=scr['V'][r0:r0 + 128, :], in_=vtm[i2][:, :]),
                     reads=[vk], writes=['V'], dma=True)
                g2 = gsm[i2]
                gk = ('gsm', i2)
                P.op('act', lambda e, g2=g2: e.activation(out=g2[:, 4:8], in_=ps_tm[:, 128:132], func=AF.Sigmoid),
                     reads=['ptm'], writes=[gk])
                P.op('dve', lambda e, g2=g2: e.tensor_tensor(out=g2[:, 8:12], in0=ps_tm[:, 132:136], in1=cst[:, 0:4],
                                                             op=ALU.add), reads=['ptm', 'cst'], writes=[gk])
                P.op('act', lambda e, g2=g2: e.activation(out=g2[:, 12:16], in_=g2[:, 8:12], func=AF.Abs), reads=[gk], writes=[gk])
                P.op('act', lambda e, g2=g2: e.activation(out=g2[:, 16:20], in_=g2[:, 12:16], func=AF.Exp, scale=-1.0),
                     reads=[gk], writes=[gk])
                P.op('act', lambda e, g2=g2: e.activation(out=g2[:, 20:24], in_=g2[:, 16:20], func=AF.Ln, bias=1.0),
                     reads=[gk], writes=[gk])
                P.op('act', lambda e, g2=g2: e.activation(out=g2[:, 24:28], in_=g2[:, 8:12], func=AF.Relu), reads=[gk], writes=[gk])
                P.op('dve', lambda e, g2=g2: e.tensor_tensor(out=g2[:, 24:28], in0=g2[:, 24:28], in1=g2[:, 20:24],
                                                             op=ALU.add), reads=[gk], writes=[gk])
                P.op('dve', lambda e, g2=g2: e.tensor_tensor(out=g2[:, 0:4], in0=g2[:, 24:28], in1=cst[:, 4:8],
                                                             op=ALU.mult), reads=[gk, 'cst'], writes=[gk])
                P.op('pool', lambda e, g2=g2, r0=r0: e.dma_start(out=scr['GB'][r0:r0 + 128, :], in_=g2[:, 0:8]),
                     reads=[gk], writes=['GB'], dma=True)
        P.barrier()


def phase_attn(nc, P, io, scr):
    NKB = S // 128
    NQC = S // 512
    with contextlib.ExitStack() as st:
        sb = lambda name, shape, dt: st.enter_context(nc.sbuf_tensor(name, shape, dt))
        ps = lambda name, shape, dt: st.enter_context(nc.psum_tensor(name, shape, dt))
        QT = [sb("b_QT%d" % i, [128, S], BF16) for i in range(2)]
        KT = sb("b_KT", [128, S], BF16)
        V = sb("b_V", [128, NKB, 128], BF16)
        ones = sb("b_ones", [128, 128], BF16)
        pT = [sb("b_pT%d" % i, [128, 512], BF16) for i in range(3)]
        rinv = [sb("b_rinv%d" % i, [128, 512], F32) for i in range(2)]
        of = [sb("b_of%d" % i, [128, 512], F32) for i in range(2)]
        saz = [sb("b_saz%d" % i, [128, 512], BF16) for i in range(2)]
        gb = [sb("b_g%d" % i, [128, 512], BF16) for i in range(2)]
        ps_s = [ps("b_ps%d" % i, [128, 512], F32) for i in range(3)]
        ps_o = [ps("b_po%d" % i, [128, 512], F32) for i in range(2)]
        ps_r = [ps("b_pr%d" % i, [128, 512], F32) for i in range(2)]
        P.psum_keys(('b_ps', 0), ('b_ps', 1), ('b_ps', 2), ('b_po', 0), ('b_po', 1), ('b_pr', 0), ('b_pr', 1))

        P.op('sp', lambda e: e.dma_start(out=ones[:, :], in_=io['ones'][:, :]), writes=['b_ones'], dma=True)
        for c in range(4):
            sl = slice(c * 2048, (c + 1) * 2048)
            for hh in range(2):
                P.op('sp', lambda e, hh=hh, sl=sl: e.dma_start(out=QT[hh][:, sl], in_=scr['QT'][hh, :, sl]),
                     reads=[('QT', hh)], writes=[('b_QT', hh)], dma=True)
            P.op('sp', lambda e, sl=sl: e.dma_start(out=KT[:, sl], in_=scr['KT'][0, :, sl]),
                 reads=[('KT', 0)], writes=['b_KT'], dma=True)
        vsrc = scr['V'].ap().rearrange("(b p) d -> p b d", p=128)
        for c in range(4):
            P.op('sp', lambda e, c=c: e.dma_start(out=V[:, c * 16:(c + 1) * 16, :], in_=vsrc[:, c * 16:(c + 1) * 16, :]),
                 reads=['V'], writes=['b_V'], dma=True)

        scale = 128.0 ** -0.5
        it = 0
        for hh in range(2):
            for qc in range(NQC):
                q0 = qc * 512
                po = ps_o[it % 2]
                pr = ps_r[it % 2]
                pok = ('b_po', it % 2)
                prk = ('b_pr', it % 2)

                def smm(kb, hh=hh, q0=q0):
                    s = kb % 3
                    P.op('pe', lambda e: e.matmul(ps_s[s][:, :], lhsT=KT[:, kb * 128:(kb + 1) * 128],
                                                  rhs=QT[hh][:, q0:q0 + 512], start=True, stop=True),
                         reads=['b_KT', ('b_QT', hh)], writes=[('b_ps', s)])
                    P.op('act', lambda e: e.activation(out=pT[s][:, :], in_=ps_s[s][:, :], func=AF.Exp, scale=scale),
                         reads=[('b_ps', s)], writes=[('b_pT', s)])

                smm(0)
                smm(1)
                for kb in range(NKB):
                    if kb + 2 < NKB:
                        smm(kb + 2)
                    s = kb % 3
                    P.op('pe', lambda e, kb=kb, s=s, po=po: e.matmul(po[:, :], lhsT=V[:, kb, :], rhs=pT[s][:, :],
                                                              start=(kb == 0), stop=(kb == NKB - 1)),
                         reads=['b_V', ('b_pT', s)], writes=[pok])
                    P.op('pe', lambda e, kb=kb, s=s, pr=pr: e.matmul(pr[:, :], lhsT=ones[:, :], rhs=pT[s][:, :],
                                                              start=(kb == 0), stop=(kb == NKB - 1)),
                         reads=['b_ones', ('b_pT', s)], writes=[prk])
                i2 = it % 2
                P.op('sp', lambda e, i2=i2, hh=hh, q0=q0: e.dma_start(out=saz[i2][:, :], in_=scr['SAZ'][hh, :, q0:q0 + 512]),
                     reads=[('SAZ', hh)], writes=[('b_saz', i2)], dma=True)
                P.op('dve', lambda e, i2=i2, pr=pr: e.reciprocal(out=rinv[i2][:, :], in_=pr[:, :]),
                     reads=[prk], writes=[('b_rinv', i2)])
                P.op('dve', lambda e, i2=i2, po=po: e.tensor_tensor(out=of[i2][:, :], in0=po[:, :], in1=rinv[i2][:, :],
                                                                    op=ALU.mult),
                     reads=[pok, ('b_rinv', i2)], writes=[('b_of', i2)])
                P.op('pool', lambda e, i2=i2: e.tensor_tensor(out=gb[i2][:, :], in0=of[i2][:, :], in1=saz[i2][:, :],
                                                              op=ALU.mult),
                     reads=[('b_of', i2), ('b_saz', i2)], writes=[('b_g', i2)])
                P.op('pool', lambda e, i2=i2, hh=hh, q0=q0: e.dma_start(out=scr['GTA'][hh * 128:(hh + 1) * 128, q0:q0 + 512],
                                                                      in_=gb[i2][:, :]),
                     reads=[('b_g', i2)], writes=['GTA'], dma=True)
                it += 1
        P.barrier()


def phase_dn(nc, P, io, scr):
    with contextlib.ExitStack() as st:
        sb = lambda name, shape, dt: st.enter_context(nc.sbuf_tensor(name, shape, dt))
        ps = lambda name, shape, dt: st.enter_context(nc.psum_tensor(name, shape, dt))
        TRI4 = sb("c_TRI4", [128, 4, 128], F32)
        SM4 = sb("c_SM4", [128, 4, 128], F32)
        IM4 = sb("c_IM4", [128, 4, 128], F32)
        I4 = sb("c_I4", [128, 4, 128], F32)
        ONESF = sb("c_ONESF", [128, 128], F32)
        BLK = sb("c_BLK", [128, 128], F32)
        ident = sb("c_ident", [128, 128], BF16)
        dnw = sb("c_dnw", [128, 128], F32)
        Oacc = sb("c_Oacc", [128, NPAIR, 2, 128], F32)
        Sf = sb("c_Sf", [128, 4, 128], F32)
        Sbf = sb("c_Sbf", [128, 4, 128], BF16)
        qT4_ = [sb("c_qT4%d" % i, [128, 4, 128], BF16) for i in range(2)]
        kT4_ = [sb("c_kT4%d" % i, [128, 4, 128], BF16) for i in range(2)]
        vT4_ = [sb("c_vT4%d" % i, [128, 4, 128], BF16) for i in range(2)]
        gbt_ = [sb("c_gb%d" % i, [128, 8], F32) for i in range(2)]
        sm_ = [sb("c_sm%d" % i, [128, 24], F32) for i in range(2)]
        gtri = sb("c_gtri", [128, 4, 128], F32)
        absz = sb("c_absz", [128, 4, 128], F32)
        W4 = sb("c_W4", [128, 4, 128], F32)
        EROW_ = [sb("c_EROW%d" % i, [128, 4, 128], F32) for i in range(2)]
        Wm = sb("c_Wm", [128, 4, 128], F32)
        Wi = sb("c_Wi", [128, 4, 128], F32)
        Pb = [sb("c_P%d" % i, [128, 4, 128], BF16) for i in range(2)]
        Ptb = [sb("c_Pt%d" % i, [128, 4, 128], BF16) for i in range(2)]
        Xb = [sb("c_X%d" % i, [128, 4, 128], BF16) for i in range(2)]
        aT4_ = [sb("c_aT4%d" % i, [128, 4, 128], BF16) for i in range(2)]
        qg4_ = [sb("c_qg4%d" % i, [128, 4, 128], BF16) for i in range(2)]
        kg4 = sb("c_kg4", [128, 4, 128], BF16)
        kdec4_ = [sb("c_kdec4%d" % i, [128, 4, 128], BF16) for i in range(2)]
        vtok4 = sb("c_vtok4", [128, 4, 128], BF16)
        ub4_ = [sb("c_ub4%d" % i, [128, 4, 128], F32) for i in range(2)]
        wT4_ = [sb("c_wT4%d" % i, [128, 4, 128], BF16) for i in range(2)]
        vnew = sb("c_vnew", [128, 4, 128], BF16)
        fstat = sb("c_fstat", [128, 4], F32)
        fsq = sb("c_fsq", [128, 128], F32)
        fon = sb("c_fon", [128, 128], BF16)
        sdz = sb("c_sdz", [128, 512], BF16)
        gout = sb("c_gout", [128, 512], BF16)
        B0 = ps("c_B0", [128, 8, 128], BF16)
        B1 = ps("c_B1", [128, 4, 128], F32)
        B2 = ps("c_B2", [128, 4, 128], F32)
        B3 = ps("c_B3", [128, 4, 128], F32)
        U = [ps("c_U%d" % i, [128, 4, 128], F32) for i in range(4)]
        P.psum_keys('B0', 'B1', 'B2', 'B3', ('U', 0), ('U', 1), ('U', 2), ('U', 3))

        def ld(t, src, key):
            P.op('sp', lambda e: e.dma_start(out=t, in_=src), writes=[key], dma=True)
        ld(TRI4[:, :, :], io['TRI4'][:, :, :], 'TRI4')
        ld(SM4[:, :, :], io['SM4'][:, :, :], 'SM4')
        ld(IM4[:, :, :], io['IM4'][:, :, :], 'IM4')
        ld(I4[:, :, :], io['I4'][:, :, :], 'I4')
        ld(ONESF[:, :], io['ONESF'][:, :], 'ONESF')
        ld(BLK[:, :], io['BLK'][:, :], 'BLK')
        ld(ident[:, :], io['ident'][:, :], 'c_ident')
        ld(dnw[:, :], io['dn_norm'][0:1, :].partition_broadcast(128), 'dnw')
        P.op('pool', lambda e: e.memset(Sf[:, :, :], 0.0), writes=['Sf'])
        P.op('pool', lambda e: e.memset(Sbf[:, :, :], 0.0), writes=['Sbf'])

        flat = lambda t: t[:, :, :].rearrange("p u t -> p (u t)")

        def pre(p):
            par = p % 2
            pair = [p, p, NPAIR - 1 - p, NPAIR - 1 - p]
            for u in range(4):
                c0 = pair[u] * 128
                hh = u % 2
                P.op('sp', lambda e, u=u, hh=hh, c0=c0: e.dma_start(out=qT4_[par][:, u, :], in_=scr['DQ'][hh, :, c0:c0 + 128]),
                     reads=[('DQ', hh)], writes=[('qT4', par)], dma=True)
                P.op('sp', lambda e, u=u, hh=hh, c0=c0: e.dma_start(out=kT4_[par][:, u, :], in_=scr['DK'][hh, :, c0:c0 + 128]),
                     reads=[('DK', hh)], writes=[('kT4', par)], dma=True)
                P.op('sp', lambda e, u=u, hh=hh, c0=c0: e.dma_start(out=vT4_[par][:, u, :], in_=scr['DV'][hh, :, c0:c0 + 128]),
                     reads=[('DV', hh)], writes=[('vT4', par)], dma=True)
            for d in range(2):
                r0 = pair[2 * d] * 128
                for off in (0, 4):
                    a = off + 2 * d
                    P.op('sp', lambda e, r0=r0, a=a: e.dma_start(out=gbt_[par][:, a:a + 2], in_=scr['GB'][r0:r0 + 128, a:a + 2]),
                         reads=['GB'], writes=[('gbt', par)], dma=True)
            yield
            for u in range(4):
                P.op('dve', lambda e, u=u: e.tensor_scalar(out=gtri[:, u, :], in0=TRI4[:, u, :], scalar1=gbt_[par][:, u:u + 1],
                                                           scalar2=None, op0=ALU.mult), reads=['TRI4', ('gbt', par)], writes=['gtri'])
            P.op('dve', lambda e: e.tensor_scalar(out=sm_[par][:, 20:24], in0=gbt_[par][:, 4:8], scalar1=-1.0, scalar2=None,
                                                  op0=ALU.mult), reads=[('gbt', par)], writes=[('negb', par)])
            P.op('pe', lambda e: e.matmul(B1[:, 0, 0:2], lhsT=TRI4[:, 0, :], rhs=gbt_[par][:, 0:2], start=True, stop=True),
                 reads=['TRI4', ('gbt', par)], writes=['B1'])
            P.op('pe', lambda e: e.matmul(B1[:, 0, 2:4], lhsT=TRI4[:, 2, :], rhs=gbt_[par][:, 2:4], start=True, stop=True),
                 reads=['TRI4', ('gbt', par)], writes=['B1'])
            P.op('pe', lambda e: e.matmul(B1[:, 0, 4:8], lhsT=BLK[:, :], rhs=gbt_[par][:, 0:4], start=True, stop=True),
                 reads=['BLK', ('gbt', par)], writes=['B1'])
            P.op('act', lambda e: e.copy(out=sm_[par][:, 0:8], in_=B1[:, 0, 0:8]), reads=['B1'], writes=[('sm', par)])
            for u in range(4):
                P.op('pe', lambda e, u=u: e.matmul(B1[:, u, :], lhsT=ONESF[:, :], rhs=gtri[:, u, :], start=True, stop=True),
                     reads=['ONESF', 'gtri'], writes=['B1'])
            P.op('dve', lambda e: e.tensor_tensor(out=sm_[par][:, 8:12], in0=sm_[par][:, 4:8], in1=sm_[par][:, 0:4], op=ALU.subtract),
                 reads=[('sm', par)], writes=[('sm2', par)])
            P.op('act', lambda e: e.activation(out=sm_[par][:, 12:16], in_=sm_[par][:, 8:12], func=AF.Exp), reads=[('sm2', par)], writes=[('sm3', par)])
            P.op('act', lambda e: e.activation(out=sm_[par][:, 16:20], in_=sm_[par][:, 0:4], func=AF.Exp), reads=[('sm', par)], writes=[('sm4', par)])
            for u in range(4):
                P.op('dve', lambda e, u=u: e.tensor_scalar(out=absz[:, u, :], in0=B1[:, u, :], scalar1=sm_[par][:, u:u + 1],
                                                           scalar2=None, op0=ALU.subtract),
                     reads=['B1', ('sm', par)], writes=['absz'])
            P.op('act', lambda e: e.activation(out=flat(absz), in_=flat(absz), func=AF.Abs), reads=['absz'], writes=['absz'])
            P.op('act', lambda e: e.activation(out=flat(W4), in_=flat(absz), func=AF.Exp, scale=-1.0),
                 reads=['absz'], writes=['W4'])
            P.op('act', lambda e: e.activation(out=flat(EROW_[par]), in_=B1[:, :, :].rearrange("p u t -> p (u t)"), func=AF.Exp),
                 reads=['B1'], writes=[('EROW', par)])
            yield
            for u in range(4):
                P.op('pe', lambda e, u=u: e.matmul(B2[:, u, :], lhsT=kT4_[par][:, u, :], rhs=kT4_[par][:, u, :], start=True, stop=True),
                     reads=[('kT4', par)], writes=['B2'])
            for u in range(4):
                P.op('pe', lambda e, u=u: e.matmul(B3[:, u, :], lhsT=kT4_[par][:, u, :], rhs=qT4_[par][:, u, :], start=True, stop=True),
                     reads=[('kT4', par), ('qT4', par)], writes=['B3'])
            for u in range(4):
                P.op('pe', lambda e, u=u: e.transpose(out=B0[:, u, :], in_=kT4_[par][:, u, :], identity=ident[:, :]),
                     reads=[('kT4', par), 'c_ident'], writes=['B0'])
            for u in range(4):
                P.op('pe', lambda e, u=u: e.transpose(out=B0[:, 4 + u, :], in_=vT4_[par][:, u, :], identity=ident[:, :]),
                     reads=[('vT4', par), 'c_ident'], writes=['B0'])
            P.op('dve', lambda e: e.tensor_tensor(out=flat(Wm), in0=flat(W4), in1=flat(SM4), op=ALU.mult),
                 reads=['W4', 'SM4'], writes=['Wm'])
            P.op('pool', lambda e: e.tensor_tensor(out=flat(Wi), in0=flat(W4), in1=flat(IM4), op=ALU.mult),
                 reads=['W4', 'IM4'], writes=['Wi'])
            for u in range(4):
                P.op('dve', lambda e, u=u: e.scalar_tensor_tensor(out=Pb[0][:, u, :], in0=B2[:, u, :], scalar=sm_[par][:, 20 + u:21 + u],
                                                                  in1=Wm[:, u, :], op0=ALU.mult, op1=ALU.mult),
                     reads=['B2', ('negb', par), 'Wm'], writes=[('P', 0)])
            P.op('dve', lambda e: e.tensor_tensor(out=flat(aT4_[par]), in0=B3[:, :, :].rearrange("p u t -> p (u t)"), in1=flat(Wi),
                                                  op=ALU.mult), reads=['B3', 'Wi'], writes=[('aT4', par)])
            P.op('dve', lambda e: e.tensor_tensor(out=flat(qg4_[par]), in0=flat(qT4_[par]), in1=flat(EROW_[par]), op=ALU.mult),
                 reads=[('qT4', par), ('EROW', par)], writes=[('qg4', par)])
            for u in range(4):
                P.op('act', lambda e, u=u: e.activation(out=kg4[:, u, :], in_=B0[:, u, :], func=AF.Copy, scale=sm_[par][:, 16 + u:17 + u]),
                     reads=['B0', ('sm4', par)], writes=['kg4'])
                P.op('act', lambda e, u=u: e.activation(out=kdec4_[par][:, u, :], in_=B0[:, u, :], func=AF.Copy, scale=sm_[par][:, 12 + u:13 + u]),
                     reads=['B0', ('sm3', par)], writes=[('kdec4', par)])
            P.op('act', lambda e: e.copy(out=flat(vtok4), in_=B0[:, 4:8, :].rearrange("p u t -> p (u t)")),
                 reads=['B0'], writes=['vtok4'])
            yield
            for u in range(4):
                P.op('pe', lambda e, u=u: e.transpose(out=B0[:, u, :], in_=Pb[0][:, u, :], identity=ident[:, :]),
                     reads=[('P', 0), 'c_ident'], writes=['B0'])
            P.op('act', lambda e: e.copy(out=flat(Ptb[0]), in_=B0[:, 0:4, :].rearrange("p u t -> p (u t)")),
                 reads=['B0'], writes=[('Pt', 0)])
            P.op('pool', lambda e: e.tensor_tensor(out=flat(Xb[0]), in0=flat(Pb[0]), in1=flat(I4), op=ALU.add),
                 reads=[('P', 0), 'I4'], writes=[('X', 0)])
            pc, xc = 0, 0
            for b in range(1, 7):
                if b <= 4:
                    for u in range(4):
                        P.op('pe', lambda e, u=u, pc=pc: e.matmul(B2[:, u, :], lhsT=Ptb[pc][:, u, :], rhs=Pb[pc][:, u, :],
                                                                  start=True, stop=True),
                             reads=[('P', pc), ('Pt', pc)], writes=['B2'])
                if b <= 5:
                    for u in range(4):
                        P.op('pe', lambda e, u=u, pc=pc: e.matmul(B3[:, u, :], lhsT=Pb[pc][:, u, :], rhs=Ptb[pc][:, u, :],
                                                                  start=True, stop=True),
                             reads=[('P', pc), ('Pt', pc)], writes=['B3'])
                if b >= 2:
                    for u in range(4):
                        P.op('pe', lambda e, u=u, pc=pc, xc=xc: e.matmul(B1[:, u, :], lhsT=Ptb[pc][:, u, :], rhs=Xb[xc][:, u, :],
                                                                         start=True, stop=True),
                             reads=[('Pt', pc), ('X', xc)], writes=['B1'])
                if b <= 4:
                    P.op('act', lambda e, pc=pc: e.copy(out=flat(Pb[1 - pc]), in_=B2[:, :, :].rearrange("p u t -> p (u t)")),
                         reads=['B2'], writes=[('P', 1 - pc)])
                if b <= 5:
                    P.op('dve', lambda e, pc=pc: e.tensor_copy(out=flat(Ptb[1 - pc]), in_=B3[:, :, :].rearrange("p u t -> p (u t)")),
                         reads=['B3'], writes=[('Pt', 1 - pc)])
                if b >= 2:
                    P.op('dve', lambda e, xc=xc: e.tensor_tensor(out=flat(Xb[1 - xc]), in0=B1[:, :, :].rearrange("p u t -> p (u t)"),
                                                                 in1=flat(Xb[xc]), op=ALU.add),
                         reads=['B1', ('X', xc)], writes=[('X', 1 - xc)])
                    xc = 1 - xc
                if b <= 5:
                    pc = 1 - pc
                yield
            yield
            cur = xc
            X = Xb[cur]
            Xk = ('X', cur)
            for u in range(4):
                P.op('pe', lambda e, u=u, X=X: e.matmul(B2[:, u, :], lhsT=X[:, u, :], rhs=vtok4[:, u, :], start=True, stop=True),
                     reads=[Xk, 'vtok4'], writes=['B2'])
            for u in range(4):
                P.op('pe', lambda e, u=u, X=X: e.matmul(B3[:, u, :], lhsT=kg4[:, u, :], rhs=X[:, u, :], start=True, stop=True),
                     reads=[Xk, 'kg4'], writes=['B3'])
            for u in range(4):
                P.op('act', lambda e, u=u: e.activation(out=ub4_[par][:, u, :], in_=B2[:, u, :], func=AF.Copy, scale=gbt_[par][:, 4 + u:5 + u]),
                     reads=['B2', ('gbt', par)], writes=[('ub4', par)])
            P.op('dve', lambda e: e.tensor_copy(out=flat(wT4_[par]), in_=B3[:, :, :].rearrange("p u t -> p (u t)")),
                 reads=['B3'], writes=[('wT4', par)])

            yield
        def rec(p):
            par = p % 2
            pair = [p, p, NPAIR - 1 - p, NPAIR - 1 - p]
            for ci in range(2):
                for u in range(4):
                    fwd = u < 2
                    c = ci if fwd else 1 - ci
                    r = slice(64 * c, 64 * c + 64)
                    col = 64 * c + 63 if fwd else 64 * c
                    sk = ('Sbf', u)
                    uk = ('U', u)
                    P.op('pe', lambda e, u=u: e.matmul(U[u][:, 0, :], lhsT=wT4_[par][:, u, :], rhs=Sbf[:, u, :], start=True, stop=True),
                         reads=[('wT4', par), sk, 'Sbf'], writes=[uk])
                    P.op('dve', lambda e, u=u, r=r: e.scalar_tensor_tensor(out=vnew[r, u, :], in0=U[u][r, 0, :],
                                                                           scalar=sm_[par][r, 20 + u:21 + u], in1=ub4_[par][r, u, :],
                                                                           op0=ALU.mult, op1=ALU.add),
                         reads=[uk, ('negb', par), ('ub4', par)], writes=[('vnew', u)])
                    P.op('pe', lambda e, u=u: e.matmul(U[u][:, 1, :], lhsT=qg4_[par][:, u, :], rhs=Sbf[:, u, :], start=True, stop=False),
                         reads=[('qg4', par), sk, 'Sbf'], writes=[uk])
                    P.op('pe', lambda e, u=u, r=r: e.matmul(U[u][:, 1, :], lhsT=aT4_[par][r, u, :], rhs=vnew[r, u, :], start=False, stop=True),
                         reads=[('aT4', par), ('vnew', u)], writes=[uk])
                    P.op('pe', lambda e, u=u, r=r: e.matmul(U[u][:, 2, :], lhsT=kdec4_[par][r, u, :], rhs=vnew[r, u, :], start=True, stop=True),
                         reads=[('kdec4', par), ('vnew', u)], writes=[uk])
                    hh = u % 2
                    ok = ('Oacc', pair[u], hh)
                    if p < NPAIR // 2:
                        P.op('act', lambda e, u=u, r=r, hh=hh, pu=pair[u]: e.copy(out=Oacc[r, pu, hh, :], in_=U[u][r, 1, :]),
                             reads=[uk], writes=[ok])
                    else:
                        P.op('dve', lambda e, u=u, r=r, hh=hh, pu=pair[u]: e.tensor_tensor(out=Oacc[r, pu, hh, :], in0=U[u][r, 1, :],
                                                                                           in1=Oacc[r, pu, hh, :], op=ALU.add),
                             reads=[uk, ok], writes=[ok])
                    P.op('dve', lambda e, u=u, col=col: e.scalar_tensor_tensor(out=Sf[:, u, :], in0=Sf[:, u, :],
                                                                               scalar=EROW_[par][:, u, col:col + 1], in1=U[u][:, 2, :],
                                                                               op0=ALU.mult, op1=ALU.add),
                         reads=[uk, ('EROW', par), ('Sf', u), 'Sf'], writes=[('Sf', u)])
                    P.op('act', lambda e, u=u: e.copy(out=Sbf[:, u, :], in_=Sf[:, u, :]), reads=[('Sf', u), 'Sf'], writes=[sk])
                    yield

        nsteps = NPAIR if DN_STEPS is None else DN_STEPS
        for _ in pre(0):
            pass
        for p in range(nsteps):
            r = rec(p)
            q = pre(p + 1) if p + 1 < nsteps else iter(())
            ra, qa = True, True
            while ra or qa:
                if ra:
                    ra = next(r, 'END') != 'END'
                if qa:
                    qa = next(q, 'END') != 'END'
                if qa:
                    qa = next(q, 'END') != 'END'

        for hh in (range(2) if DN_LVL >= 6 else []):
            for T in range(NT):
                P.op('sp', lambda e, hh=hh, T=T: e.dma_start(out=sdz[:, :], in_=scr['SDZ'][hh, :, T * 512:(T + 1) * 512]),
                     reads=[('SDZ', hh)], writes=['sdz'], dma=True)
                for j in range(4):
                    pp = T * 4 + j
                    ok = ('Oacc', pp, hh)
                    P.op('act', lambda e, pp=pp, hh=hh: e.activation(out=fsq[:, :], in_=Oacc[:, pp, hh, :], func=AF.Square,
                                                                     accum_out=fstat[:, 0:1]), reads=[ok], writes=['fsq', 'fstat'])
                    emit_rsqrt(P, fstat[:, 2:3], fstat[:, 0:1], 1.0 / 128, ['fstat'], ['fstat'])
                    P.op('dve', lambda e, pp=pp, hh=hh: e.scalar_tensor_tensor(out=fon[:, :], in0=Oacc[:, pp, hh, :],
                                                                               scalar=fstat[:, 2:3], in1=dnw[:, :],
                                                                               op0=ALU.mult, op1=ALU.mult),
                         reads=[ok, 'fstat', 'dnw'], writes=['fon'])
                    P.op('pe', lambda e: e.transpose(out=B0[:, 0, :], in_=fon[:, :], identity=ident[:, :]),
                         reads=['fon', 'c_ident'], writes=['B0'])
                    P.op('dve', lambda e, j=j: e.tensor_tensor(out=gout[:, j * 128:(j + 1) * 128], in0=B0[:, 0, :],
                                                               in1=sdz[:, j * 128:(j + 1) * 128], op=ALU.mult),
                         reads=['B0', 'sdz'], writes=['gout'])
                P.op('pool', lambda e, hh=hh, T=T: e.dma_start(out=scr['GTD'][hh * 128:(hh + 1) * 128, T * 512:(T + 1) * 512],
                                                             in_=gout[:, :]), reads=['gout'], writes=['GTD'], dma=True)
        P.barrier()


def phase_tail(nc, P, io, gsrc, g_reads, gall=None):
    with contextlib.ExitStack() as st:
        sb = lambda name, shape, dt: st.enter_context(nc.sbuf_tensor(name, shape, dt))
        ps = lambda name, shape, dt: st.enter_context(nc.psum_tensor(name, shape, dt))
        Wn = {}
        for nm in ('w_ba', 'w_bd', 'w_ga', 'w_gd', 'w_out', 'w_pg'):
            Wn[nm] = sb("e_" + nm, [128, 8, D], BF16)
        Wpp = sb("e_wpp", [128, 2, D], BF16)
        stg = [sb("e_stg%d" % i, [128, D], F32) for i in range(2)]
        wb = sb("e_wb", [128, D], F32)
        wpost = sb("e_wpost", [128, D], F32)
        wple = sb("e_wple", [128, D], F32)
        ident = sb("e_ident", [128, 128], BF16)
        xt = [sb("e_xt%d" % i, [128, D], F32) for i in range(2)]
        xn = [sb("e_xn%d" % i, [128, D], BF16) for i in range(2)]
        stat = [sb("e_stat%d" % i, [128, 4], F32) for i in range(2)]
        st2 = sb("e_st2", [128, 8], F32)
        hT = sb("e_hT", [128, 8, 512], BF16)
        GaT = sb("e_GaT", [128, 8, 512], BF16)
        GdT = sb("e_GdT", [128, 8, 512], BF16)
        mixT = sb("e_mixT", [128, 8, 512], BF16)
        sa = sb("e_sa", [128, 512], F32)
        sd = sb("e_sd", [128, 512], F32)
        m1 = sb("e_m1", [128, 512], F32)
        m2 = sb("e_m2", [128, 512], F32)
        tmp = sb("e_tmp", [128, D], F32)
        x1 = sb("e_x1", [128, D], F32)
        x1b = sb("e_x1b", [128, D], BF16)
        x1T = sb("e_x1T", [128, 8, 128], BF16)
        pt = sb("e_pt", [128, 256], F32)
        ptb = sb("e_ptb", [128, 256], BF16)
        pT = sb("e_pT", [128, 2, 128], BF16)
        s2 = sb("e_s2", [128, D], F32)
        sq = sb("e_sq", [128, D], BF16)
        E = [ps("e_E%d" % i, [128, 512], F32) for i in range(4)]
        E45 = ps("e_E45", [128, 2, 512], F32)
        ps_tp = [ps("e_ptp%d" % i, [128, D], BF16) for i in range(2)]
        P.psum_keys(('E', 0), ('E', 1), ('E', 2), ('E', 3), 'E45', ('ps_tp', 0), ('ps_tp', 1))
        if gall is not None:
            selm = sb("e_selm", [128, 8, 128], BF16)
            cbuf = [sb("e_cb%d" % i, [128, 512], BF16) for i in range(6)]
            P.op('sp', lambda e: e.dma_start(out=selm[:, :, :], in_=io['selm'][:, :, :]), writes=['selm'], dma=True)
        ncb = [0]

        def ldb(t, src, key):
            P.op('sp', lambda e: e.dma_start(out=t, in_=src), writes=[key], dma=True)
        ldb(wb[:, :], io['norm_pre'][0:1, :].partition_broadcast(128), 'wb')
        ldb(wpost[:, :], io['norm_post'][0:1, :].partition_broadcast(128), 'wpost')
        ldb(wple[:, :], io['ple_norm'][0:1, :].partition_broadcast(128), 'wple')
        ldb(ident[:, :], io['ident'][:, :], 'ident')
        for nm in Wn:
            load_cast(P, lambda c, nm=nm: Wn[nm][:, c, :], lambda c, nm=nm: io[nm][c * 128:(c + 1) * 128, :], stg, 8, nm, D)
        load_cast(P, lambda c: Wpp[:, c, :], lambda c: io['w_pp'][c * 128:(c + 1) * 128, :], stg, 2, 'w_pp', D)

        for T in range(TOK2 // 512):
            t0 = T * 512
            for tb in range(4):
                emit_rms_hT(P, io['x2'][t0 + tb * 128:t0 + (tb + 1) * 128, :], wb, ident, xt, xn, hT, ps_tp, stat, tb, 'hT')
            for kt in range(8):
                if gall is None:
                    P.op('sp', lambda e, kt=kt, t0=t0: e.dma_start(out=GaT[:, kt, :], in_=gsrc(0, kt, t0)),
                         reads=g_reads, writes=['GaT'], dma=True)
                    P.op('sp', lambda e, kt=kt, t0=t0: e.dma_start(out=GdT[:, kt, :], in_=gsrc(1, kt, t0)),
                         reads=g_reads, writes=['GdT'], dma=True)
                    continue
                for kind, dst, dkey in ((0, GaT, 'GaT'), (1, GdT, 'GdT')):
                    bank = (kt * 2 + kind) % 4
                    for cand in range(8):
                        bb, qq = cand // 4, cand % 4
                        row0 = (bb * 4 + kt // 2) * 512 + kind * 256 + (kt % 2) * 128
                        col0 = qq * TOK2 + t0
                        ci = ncb[0] % 6
                        ncb[0] += 1
                        P.op('sp', lambda e, ci=ci, row0=row0, col0=col0, kind=kind: e.dma_start(
                            out=cbuf[ci][:, :], in_=gall[row0:row0 + 128, col0:col0 + 512]),
                            reads=['GALL'], writes=[('cb', ci)], dma=True)
                        P.op('pe', lambda e, ci=ci, cand=cand, bank=bank: e.matmul(
                            E[bank][:, :], lhsT=selm[:, cand, :], rhs=cbuf[ci][:, :], start=(cand == 0), stop=(cand == 7)),
                            reads=['selm', ('cb', ci)], writes=[('E', bank)])
                    if kind == 0:
                        P.op('act', lambda e, kt=kt, bank=bank, dst=dst: e.copy(out=dst[:, kt, :], in_=E[bank][:, :]),
                             reads=[('E', bank)], writes=[dkey])
                    else:
                        P.op('dve', lambda e, kt=kt, bank=bank, dst=dst: e.tensor_copy(out=dst[:, kt, :], in_=E[bank][:, :]),
                             reads=[('E', bank)], writes=[dkey])
            for fo in range(8):
                fs = slice(fo * 128, (fo + 1) * 128)
                for (bank, wn, src, skey) in ((0, 'w_ba', GaT, 'GaT'), (1, 'w_ga', hT, 'hT'), (2, 'w_bd', GdT, 'GdT'), (3, 'w_gd', hT, 'hT')):
                    for k in range(8):
                        P.op('pe', lambda e, bank=bank, wn=wn, src=src, k=k, fs=fs: e.matmul(
                            E[bank][:, :], lhsT=Wn[wn][:, k, fs], rhs=src[:, k, :], start=(k == 0), stop=(k == 7)),
                            reads=[wn, skey], writes=[('E', bank)])
                P.op('act', lambda e: e.activation(out=sa[:, :], in_=E[1][:, :], func=AF.Sigmoid), reads=[('E', 1)], writes=['sa'])
                P.op('act', lambda e: e.activation(out=sd[:, :], in_=E[3][:, :], func=AF.Sigmoid), reads=[('E', 3)], writes=['sd'])
                P.op('dve', lambda e: e.tensor_tensor(out=m1[:, :], in0=E[0][:, :], in1=sa[:, :], op=ALU.mult),
                     reads=[('E', 0), 'sa'], writes=['m1'])
                P.op('dve', lambda e: e.tensor_tensor(out=m2[:, :], in0=E[2][:, :], in1=sd[:, :], op=ALU.mult),
                     reads=[('E', 2), 'sd'], writes=['m2'])
                P.op('pool', lambda e, fo=fo: e.tensor_tensor(out=mixT[:, fo, :], in0=m1[:, :], in1=m2[:, :], op=ALU.add),
                     reads=['m1', 'm2'], writes=['mixT'])
            for tb in range(4):
                ts_ = slice(tb * 128, (tb + 1) * 128)
                r0 = t0 + tb * 128
                for half in range(2):
                    for k in range(8):
                        P.op('pe', lambda e, half=half, k=k, ts_=ts_: e.matmul(
                            E45[:, half, :], lhsT=mixT[:, k, ts_], rhs=Wn['w_out'][:, k, half * 512:(half + 1) * 512],
                            start=(k == 0), stop=(k == 7)), reads=['mixT', 'w_out'], writes=['E45'])
                for half in range(2):
                    P.op('act', lambda e, half=half: e.activation(out=sq[:, half * 512:(half + 1) * 512], in_=E45[:, half, :],
                                                                  func=AF.Square, accum_out=st2[:, half:half + 1]),
                         reads=['E45'], writes=['sq', 'st2'])
                P.op('dve', lambda e: e.tensor_tensor(out=st2[:, 2:3], in0=st2[:, 0:1], in1=st2[:, 1:2], op=ALU.add),
                     reads=['st2'], writes=['st2'])
                emit_rsqrt(P, st2[:, 4:5], st2[:, 2:3], 1.0 / D, ['st2'], ['st2'])
                P.op('sp', lambda e, r0=r0: e.dma_start(out=x1[:, :], in_=io['x2'][r0:r0 + 128, :]), writes=['x1'], dma=True)
                for half in range(2):
                    hs = slice(half * 512, (half + 1) * 512)
                    P.op('dve', lambda e, half=half, hs=hs: e.scalar_tensor_tensor(out=tmp[:, hs], in0=E45[:, half, :],
                                                                                   scalar=st2[:, 4:5], in1=wpost[:, hs],
                                                                                   op0=ALU.mult, op1=ALU.mult),
                         reads=['E45', 'st2', 'wpost'], writes=['tmp'])
                P.op('pool', lambda e: e.tensor_tensor(out=x1[:, :], in0=x1[:, :], in1=tmp[:, :], op=ALU.add),
                     reads=['x1', 'tmp'], writes=['x1'])
                P.op('pool', lambda e: e.tensor_copy(out=x1b[:, :], in_=x1[:, :]), reads=['x1'], writes=['x1b'])
                ptp = ps_tp[tb % 2]
                pk = ('ps_tp', tb % 2)
                for k in range(8):
                    P.op('pe', lambda e, k=k, ptp=ptp: e.transpose(out=ptp[:, k * 128:(k + 1) * 128],
                                                                   in_=x1b[:, k * 128:(k + 1) * 128], identity=ident[:, :]),
                         reads=['x1b', 'ident'], writes=[pk])
                P.op('act', lambda e, ptp=ptp: e.copy(out=x1T[:, :, :], in_=ptp[:, :].rearrange("p (k t) -> p k t", k=8)),
                     reads=[pk], writes=['x1T'])
                P.op('sp', lambda e, r0=r0: e.dma_start(out=pt[:, :], in_=io['p2'][r0:r0 + 128, :]), writes=['pt'], dma=True)
                P.op('dve', lambda e: e.tensor_copy(out=ptb[:, :], in_=pt[:, :]), reads=['pt'], writes=['ptb'])
                ptp2 = ps_tp[(tb + 1) % 2]
                pk2 = ('ps_tp', (tb + 1) % 2)
                for k in range(2):
                    P.op('pe', lambda e, k=k, ptp2=ptp2: e.transpose(out=ptp2[:, k * 128:(k + 1) * 128],
                                                                     in_=ptb[:, k * 128:(k + 1) * 128], identity=ident[:, :]),
                         reads=['ptb', 'ident'], writes=[pk2])
                P.op('act', lambda e, ptp2=ptp2: e.copy(out=pT[:, :, :], in_=ptp2[:, 0:256].rearrange("p (k t) -> p k t", k=2)),
                     reads=[pk2], writes=['pT'])
                for half in range(2):
                    hs = slice(half * 512, (half + 1) * 512)
                    for k in range(8):
                        P.op('pe', lambda e, half=half, k=k, hs=hs: e.matmul(E45[:, half, :], lhsT=x1T[:, k, :], rhs=Wn['w_pg'][:, k, hs],
                                                                             start=(k == 0), stop=(k == 7)),
                             reads=['x1T', 'w_pg'], writes=['E45'])
                    for k in range(2):
                        P.op('pe', lambda e, half=half, k=k, hs=hs: e.matmul(E[half][:, :], lhsT=pT[:, k, :], rhs=Wpp[:, k, hs],
                                                                             start=(k == 0), stop=(k == 1)),
                             reads=['pT', 'w_pp'], writes=[('E', half)])
                for half in range(2):
                    hs = slice(half * 512, (half + 1) * 512)
                    P.op('act', lambda e, half=half, hs=hs: e.activation(out=s2[:, hs], in_=E45[:, half, :], func=AF.Sigmoid),
                         reads=['E45'], writes=['s2'])
                    P.op('dve', lambda e, half=half, hs=hs: e.tensor_tensor(out=s2[:, hs], in0=E[half][:, :], in1=s2[:, hs], op=ALU.mult),
                         reads=[('E', half), 's2'], writes=['s2'])
                P.op('act', lambda e: e.activation(out=sq[:, :], in_=s2[:, :], func=AF.Square, accum_out=st2[:, 5:6]),
                     reads=['s2'], writes=['sq', 'st2b'])
                emit_rsqrt(P, st2[:, 7:8], st2[:, 5:6], 1.0 / D, ['st2b'], ['st2b'])
                P.op('dve', lambda e: e.scalar_tensor_tensor(out=tmp[:, :], in0=s2[:, :], scalar=st2[:, 7:8], in1=wple[:, :],
                                                             op0=ALU.mult, op1=ALU.mult), reads=['s2', 'st2b', 'wple'], writes=['tmp'])
                P.op('pool', lambda e: e.tensor_tensor(out=tmp[:, :], in0=tmp[:, :], in1=x1[:, :], op=ALU.add),
                     reads=['tmp', 'x1'], writes=['tmp'])
                P.op('pool', lambda e, r0=r0: e.dma_start(out=io['out'][r0:r0 + 128, :], in_=tmp[:, :]), reads=['tmp'], writes=['out'], dma=True)
        P.barrier()


P1_INPUTS = [('x', [S, D], F32), ('w_in', [D, WCOLS], F32), ('norm_pre', [1, D], F32), ('nw', [128, 4], F32),
             ('cw', [128, 6, 5], F32), ('gcst', [1, 8], F32), ('cos', [128, S], F32), ('sin', [128, S], F32),
             ('ident', [128, 128], BF16), ('ones', [128, 128], BF16), ('TRI4', [128, 4, 128], F32),
             ('SM4', [128, 4, 128], F32), ('IM4', [128, 4, 128], F32), ('I4', [128, 4, 128], F32),
             ('ONESF', [128, 128], F32), ('BLK', [128, 128], F32), ('dn_norm', [1, 128], F32)]
P2_INPUTS = [('x2', [TOK2, D], F32), ('p2', [TOK2, 256], F32), ('norm_pre', [1, D], F32), ('norm_post', [1, D], F32),
             ('ple_norm', [1, D], F32), ('ident', [128, 128], BF16), ('w_ba', [D, D], F32), ('w_bd', [D, D], F32),
             ('w_ga', [D, D], F32), ('w_gd', [D, D], F32), ('w_out', [D, D], F32), ('w_pg', [D, D], F32),
             ('w_pp', [256, D], F32)]


DEBUG_SCR = False
NT_LIM = None
ROPE_LVL = 9
DN_STEPS = None
DN_LVL = 9
ROPE_VAR = 0
PARTS = ('rope', 'silu', 'conv', 'tm')
PHASES = ('proj', 'attn', 'dn')


def _scratch(nc):
    scr = {}
    for nm, shape, dt in (('QT', [2, 128, S], BF16), ('KT', [1, 128, S], BF16), ('V', [S, 128], BF16),
                          ('SAZ', [2, 128, S], BF16), ('SDZ', [2, 128, S], BF16), ('DQ', [2, 128, S], BF16),
                          ('DK', [2, 128, S], BF16), ('DV', [2, 128, S], BF16), ('GB', [S, 8], F32)):
        if DEBUG_SCR:
            scr[nm] = nc.dram_tensor("scr_" + nm, shape, dt, kind="ExternalOutput")
        else:
            scr[nm] = nc.dram_tensor("scr_" + nm, shape, dt)
    return scr


def build(mode):
    nc = bass.Bass("TRN2", target_bir_lowering=False)
    io = {}
    names = []
    if mode in ('p1', 'fused'):
        names += P1_INPUTS
    if mode in ('p2', 'fused'):
        names += [n for n in P2_INPUTS if n[0] not in [m[0] for m in names]]
    for nm, shape, dt in names:
        io[nm] = nc.dram_tensor(nm, shape, dt, kind="ExternalInput")
    with contextlib.ExitStack() as stack:
        P = Prog(nc, stack)
        if mode == 'p1':
            scr = _scratch(nc)
            scr['GT'] = nc.dram_tensor("GT", [512, S], BF16, kind="ExternalOutput")
            scr['GTA'] = scr['GT'].ap()[0:256, :]
            scr['GTD'] = scr['GT'].ap()[256:512, :]
            if 'proj' in PHASES:
                phase_proj(nc, P, io, scr)
            if 'attn' in PHASES:
                phase_attn(nc, P, io, scr)
            if 'dn' in PHASES:
                phase_dn(nc, P, io, scr)
        elif mode == 'fused':
            scr = _scratch(nc)
            scr['GT'] = nc.dram_tensor("GT_int", [512, S], BF16)
            scr['GTA'] = scr['GT'].ap()[0:256, :]
            scr['GTD'] = scr['GT'].ap()[256:512, :]
            gall = nc.dram_tensor("GALL_int", [8 * 512, S], BF16)
            io['selm'] = nc.dram_tensor("selm", [128, 8, 128], BF16, kind="ExternalInput")
            io['out'] = nc.dram_tensor("out", [TOK2, D], F32, kind="ExternalOutput")
            phase_proj(nc, P, io, scr)
            phase_attn(nc, P, io, scr)
            phase_dn(nc, P, io, scr)
            P.op('pool', lambda e: e.collective_compute("AllGather", ALU.bypass, replica_groups=[list(range(8))],
                                                        ins=[scr['GT'].ap().opt()], outs=[gall.ap().opt()]),
                 reads=['GTA', 'GTD'], writes=['GALL'], dma=True, inc=1)
            phase_tail(nc, P, io, None, None, gall=gall)
        elif mode == 'p2':
            io['G2'] = nc.dram_tensor("G2", [16, 128, TOK2], BF16, kind="ExternalInput")
            io['out'] = nc.dram_tensor("out", [TOK2, D], F32, kind="ExternalOutput")
            phase_tail(nc, P, io, lambda kind, kt, t0: io['G2'][kind * 8 + kt, :, t0:t0 + 512], [])
        P.emit()
    return nc


def _consts():
    bf = ml_dtypes.bfloat16
    idx = np.arange(128)
    same = (idx[:, None] // 64) == (idx[None, :] // 64)
    k = idx[:, None]
    i = idx[None, :]
    tri_f = (same & (k <= i)).astype(np.float32)
    tri_b = (same & (k >= i)).astype(np.float32)
    smt_f = (same & (i > k)).astype(np.float32)
    smt_b = (same & (i < k)).astype(np.float32)
    imt_f = (same & (i >= k)).astype(np.float32)
    imt_b = (same & (i <= k)).astype(np.float32)
    eye = np.eye(128, dtype=np.float32)
    st4 = lambda a, b: np.ascontiguousarray(np.stack([a, a, b, b], axis=1))
    c = {
        'ident': np.eye(128, dtype=np.float32).astype(bf),
        'ones': np.ones((128, 128), np.float32).astype(bf),
        'TRI4': st4(tri_f, tri_b), 'SM4': st4(smt_f, smt_b), 'IM4': st4(imt_f, imt_b), 'I4': st4(eye, eye),
        'ONESF': np.ones((128, 128), np.float32), 'BLK': same.astype(np.float32),
    }
    t = np.arange(S)
    row = (t // 64).astype(np.float32)
    col = (t % 64).astype(np.float32)
    inv = (np.float32(10000.0) ** (-np.arange(32, dtype=np.float32) / np.float32(32))).astype(np.float32)
    ang = np.concatenate([row[:, None] * inv[None, :], col[:, None] * inv[None, :]], axis=1).astype(np.float32)
    cosd = np.repeat(np.cos(ang.astype(np.float64)), 2, axis=1).T.astype(np.float32)
    sind = np.repeat(np.sin(ang.astype(np.float64)), 2, axis=1).T.astype(np.float32)
    sign = np.where(np.arange(128) % 2 == 0, -1.0, 1.0).astype(np.float32)[:, None]
    c['cos'] = np.ascontiguousarray(cosd)
    c['sin'] = np.ascontiguousarray(sind * sign)
    return c


def _p1_inputs(c, inputs, consts):
    b, j = c // 4, c % 4
    w_in = inputs['w_in'][0]
    o = np.cumsum([0, 1024, 256, 256, 1024, 1024, 1024, 1024, 16, 16, 1024, 1024, 1024])
    aq, ak, av, az, dq, dk, dv, db, da, dz = [w_in[:, o[i]:o[i + 1]] for i in range(10)]
    kv = j // 2
    sw = np.arange(128) ^ 1
    hs = [2 * j, 2 * j + 1]
    col = lambda m, h: m[:, h * 128:(h + 1) * 128]
    groups = [col(aq, hs[0]), col(aq, hs[1]), col(aq, hs[0])[:, sw], col(aq, hs[1])[:, sw],
              col(ak, kv), col(ak, kv)[:, sw], col(az, hs[0]), col(az, hs[1]),
              col(dq, hs[0]), col(dq, hs[1]), col(dk, hs[0]), col(dk, hs[1]), col(dv, hs[0]), col(dv, hs[1]),
              col(dz, hs[0]), col(dz, hs[1]), col(av, kv)]
    sel = [0 * 8 + hs[0], 0 * 8 + hs[1], 1 * 8 + hs[0], 1 * 8 + hs[1]]
    groups += [db[:, sel], da[:, sel]]
    wsl = np.ascontiguousarray(np.concatenate(groups, axis=1))
    qn, kn = inputs['q_norm'][0], inputs['k_norm'][0]
    nw = np.stack([qn, qn[sw], kn, kn[sw]], axis=1).astype(np.float32)
    conv = inputs['conv_w'][0]
    cg = []
    for base in (0, 1024, 2048):
        for h in hs:
            cg.append(conv[:, base + h * 128: base + (h + 1) * 128].T)
    cw = np.ascontiguousarray(np.stack(cg, axis=1)).astype(np.float32)
    dtb = inputs['dt_bias'][0].reshape(16)[sel]
    alog = inputs['a_log'][0].reshape(16)[sel]
    gcst = np.concatenate([dtb, alog])[None, :].astype(np.float32)
    d = {'x': np.ascontiguousarray(inputs['x'][b]), 'w_in': wsl, 'norm_pre': inputs['norm_pre'], 'nw': np.ascontiguousarray(nw),
         'cw': cw, 'gcst': gcst, 'dn_norm': inputs['dn_norm']}
    for k_ in ('cos', 'sin', 'ident', 'ones', 'TRI4', 'SM4', 'IM4', 'I4', 'ONESF', 'BLK'):
        d[k_] = consts[k_]
    return d


def _p2_inputs(c, inputs, consts):
    b, q = c // 4, c % 4
    w_in = inputs['w_in'][0]
    tok = slice(q * TOK2, (q + 1) * TOK2)
    return {'x2': np.ascontiguousarray(inputs['x'][b, tok]), 'p2': np.ascontiguousarray(inputs['p'][0, b, tok]),
            'norm_pre': inputs['norm_pre'], 'norm_post': inputs['norm_post'], 'ple_norm': inputs['ple_norm'],
            'ident': consts['ident'], 'w_ba': inputs['w_br_att'][0], 'w_bd': inputs['w_br_dn'][0],
            'w_ga': np.ascontiguousarray(w_in[:, 6688:7712]), 'w_gd': np.ascontiguousarray(w_in[:, 7712:8736]),
            'w_out': inputs['w_out'][0], 'w_pg': inputs['w_ple_gate'][0], 'w_pp': inputs['w_ple_proj'][0]}


FUSED = True


def kernel(**inputs):
    inputs = {k: np.asarray(v) for k, v in inputs.items()}
    consts = _consts()
    if FUSED:
        maps = []
        eye = np.eye(128, dtype=np.float32)
        for c in range(8):
            d = _p1_inputs(c, inputs, consts)
            d.update(_p2_inputs(c, inputs, consts))
            selm = np.zeros((128, 8, 128), np.float32)
            selm[:, c, :] = eye
            d['selm'] = selm.astype(ml_dtypes.bfloat16)
            maps.append(d)
        nc = build('fused')
        r = run_bass_kernel_spmd(nc, maps, core_ids=list(range(8)))
        out = np.zeros((2, S, D), np.float32)
        for c in range(8):
            b, q = c // 4, c % 4
            out[b, q * TOK2:(q + 1) * TOK2] = np.asarray(r.results[c]['out'])
        return out
    nc1 = build('p1')
    r1 = run_bass_kernel_spmd(nc1, [_p1_inputs(c, inputs, consts) for c in range(8)], core_ids=list(range(8)))
    GT = [np.asarray(r1.results[c]['GT']) for c in range(8)]
    in2 = []
    for c in range(8):
        b, q = c // 4, c % 4
        d = _p2_inputs(c, inputs, consts)
        tok = slice(q * TOK2, (q + 1) * TOK2)
        tiles = []
        for kind in range(2):
            for kt in range(8):
                src = GT[b * 4 + kt // 2]
                r0 = kind * 256 + (kt % 2) * 128
                tiles.append(src[r0:r0 + 128, tok])
        d['G2'] = np.ascontiguousarray(np.stack(tiles, axis=0))
        in2.append(d)
    nc2 = build('p2')
    r2 = run_bass_kernel_spmd(nc2, in2, core_ids=list(range(8)))
    out = np.zeros((2, S, D), np.float32)
    for c in range(8):
        b, q = c // 4, c % 4
        out[b, q * TOK2:(q + 1) * TOK2] = np.asarray(r2.results[c]['out'])
    return out
```
